# Optimizing a Trainium2 kernel written in Bass

```python
import jax
import jax.numpy as jnp
from jax import lax
import numpy as np

D_MODEL = 1024
BATCH = 16
SEQ = 2048
DEPTH = 2

GRID_W = 64
CTX_LEN = 256
HEAD_DIM = 64
GLA_HEADS = 4
GLA_DK = 32
GLA_DV = 64
GLA_QK = GLA_HEADS * GLA_DK
GLA_V = GLA_HEADS * GLA_DV
GLA_GATE_RANK = 16
GLA_GATE_TAU = 16.0
GLA_CHUNK = 64
GMLP_GROUPS = 4
GMLP_GDIM = 64
GMLP_WIDTH = GMLP_GROUPS * GMLP_GDIM
GMLP_CHUNK = 128
SWA_HEADS = 8
SWA_KV_HEADS = 2
SWA_REP = SWA_HEADS // SWA_KV_HEADS
SWA_Q = SWA_HEADS * HEAD_DIM
SWA_KV = SWA_KV_HEADS * HEAD_DIM
SWA_WINDOW = 128
ROPE_AXIS_DIM = HEAD_DIM // 2
ROPE_THETA = 10000.0
MIX_WIDTH = GLA_V + GMLP_WIDTH + SWA_Q
IN_SPLITS = (GLA_QK, GLA_QK, GLA_V, GLA_V, 2 * GLA_GATE_RANK, 2 * GMLP_WIDTH, SWA_Q, SWA_KV, SWA_KV)
IN_COLS = 2 * GLA_QK + 2 * GLA_V + 2 * GLA_GATE_RANK + 2 * GMLP_WIDTH + SWA_Q + 2 * SWA_KV
D_FF = -(-8 * D_MODEL // (3 * 256)) * 256

kernel_name = 'hybrid_dit_gla_gmlp_swa'


def rmsnorm(x, g, eps=1e-6):
    xf = x.astype(jnp.float32)
    y = xf * lax.rsqrt(jnp.mean(xf * xf, axis=-1, keepdims=True) + eps)
    return (y * g.astype(jnp.float32)).astype(x.dtype)


def layernorm(x, g, b, eps=1e-5):
    xf = x.astype(jnp.float32)
    mu = jnp.mean(xf, axis=-1, keepdims=True)
    xc = xf - mu
    y = xc * lax.rsqrt(jnp.mean(xc * xc, axis=-1, keepdims=True) + eps)
    return y * g.astype(jnp.float32) + b.astype(jnp.float32)


def modulate(h, shift, scale):
    return h * (1.0 + scale) + shift


def split_cols(z):
    offsets = np.cumsum(np.array(IN_SPLITS))[:-1].tolist()
    return jnp.split(z, offsets, axis=-1)


def rope_axis(x, ang):
    half = x.shape[-1] // 2
    cos = jnp.cos(ang)[:, None, :]
    sin = jnp.sin(ang)[:, None, :]
    x1, x2 = x[..., :half], x[..., half:]
    return jnp.concatenate([x1 * cos - x2 * sin, x2 * cos + x1 * sin], axis=-1)


def rope_2d(x, ang_row, ang_col):
    return jnp.concatenate([rope_axis(x[..., :ROPE_AXIS_DIM], ang_row),
                            rope_axis(x[..., ROPE_AXIS_DIM:], ang_col)], axis=-1)


def gla_heads(a, d):
    b, t = a.shape[:2]
    return a.astype(jnp.float32).reshape(b, t, GLA_HEADS, d).transpose(0, 2, 1, 3)


def gla_log_gates(code, wa2, ba):
    z = code @ wa2.astype(jnp.float32) + ba.astype(jnp.float32)
    return gla_heads(jax.nn.log_sigmoid(z) / GLA_GATE_TAU, GLA_DK)


def gla_chunk_scan(q, k, v, logg, s0):
    b, h, t, _ = q.shape
    n = t // GLA_CHUNK

    def chunks(a):
        return a.reshape(b, h, n, GLA_CHUNK, a.shape[-1]).transpose(2, 0, 1, 3, 4)

    causal = jnp.tril(jnp.ones((GLA_CHUNK, GLA_CHUNK), dtype=bool))[:, :, None]

    def step(s, inp):
        qc, kc, vc, gc = inp
        cum = jnp.cumsum(gc, axis=2)
        diff = cum[:, :, :, None, :] - cum[:, :, None, :, :]
        decay = jnp.exp(jnp.where(causal, diff, -jnp.inf))
        att = jnp.einsum('bhtd,bhsd,bhtsd->bhts', qc, kc, decay)
        o = att @ vc + jnp.einsum('bhtd,bhde->bhte', qc * jnp.exp(cum), s)
        cum_end = cum[:, :, -1:, :]
        s = jnp.exp(cum_end[:, :, 0, :])[..., None] * s + jnp.einsum(
            'bhsd,bhse->bhde', kc * jnp.exp(cum_end - cum), vc)
        return s, o

    s, o = lax.scan(step, s0, (chunks(q), chunks(k), chunks(v), chunks(logg)))
    return o.transpose(1, 2, 0, 3, 4).reshape(b, h, t, v.shape[-1]), s


def gla_mixer(zl, zc, wa2, ba, norm_g, need_ctx):
    def prep(q, k, v, code):
        cf, cb = jnp.split(code.astype(jnp.float32), 2, axis=-1)
        return (gla_heads(q, GLA_DK) * GLA_DK ** -0.5, gla_heads(k, GLA_DK), gla_heads(v, GLA_DV),
                gla_log_gates(cf, wa2[0], ba[0]), gla_log_gates(cb, wa2[1], ba[1]))

    lq, lk, lv, lgf, lgb = prep(zl[0], zl[1], zl[2], zl[4])
    cq, ck, cv, cgf, cgb = prep(zc[0], zc[1], zc[2], zc[4])
    s0 = jnp.zeros((lq.shape[0], GLA_HEADS, GLA_DK, GLA_DV), jnp.float32)
    flip = lambda a: a[:, :, ::-1]
    oc_f, s_f = gla_chunk_scan(cq, ck, cv, cgf, s0)
    oc_b, s_b = gla_chunk_scan(flip(cq), flip(ck), flip(cv), flip(cgb), s0)
    ol_f, _ = gla_chunk_scan(lq, lk, lv, lgf, s_f)
    ol_b, _ = gla_chunk_scan(flip(lq), flip(lk), flip(lv), flip(lgb), s_b)

    def finish(o, g):
        bb, _, t, _ = o.shape
        o = rmsnorm(o.transpose(0, 2, 1, 3), norm_g)
        g = g.astype(jnp.float32).reshape(bb, t, GLA_HEADS, GLA_DV)
        return (o * jax.nn.silu(g)).reshape(bb, t, GLA_V)

    y_lat = finish(ol_f + flip(ol_b), zl[3])
    y_ctx = finish(oc_f + flip(oc_b), zc[3]) if need_ctx else None
    return y_lat, y_ctx


def gmlp_mixer(z, ln_g, ln_b, ws, bs):
    zf = jax.nn.gelu(z.astype(jnp.float32), approximate=False)
    u, v = jnp.split(zf, 2, axis=-1)
    v = layernorm(v, ln_g, ln_b)
    b, t, _ = v.shape
    vb = v.reshape(b, t // GMLP_CHUNK, GMLP_CHUNK, GMLP_GROUPS, GMLP_GDIM)
    mixed = jnp.einsum('gpq,bnqgc->bnpgc', ws.astype(jnp.float32), vb) + bs.astype(jnp.float32).T[:, :, None]
    return u * mixed.reshape(b, t, GMLP_WIDTH)


def band_blocks(a):
    b, t = a.shape[:2]
    w = SWA_WINDOW
    ap = jnp.pad(a, ((0, 0), (w, w), (0, 0), (0, 0)))
    blk = ap.reshape(b, t // w + 2, w, *a.shape[2:])
    return jnp.concatenate([blk[:, :-2], blk[:, 1:-1], blk[:, 2:]], axis=2)


def swa_mixer(zl, zc, ang_row, ang_col, sink, need_ctx):
    f32 = jnp.float32
    w = SWA_WINDOW
    scale = HEAD_DIM ** -0.5
    q_l, k_l, v_l = [a.astype(f32) for a in zl]
    b, t = q_l.shape[:2]
    nb = t // w
    q_l = rope_2d(q_l.reshape(b, t, SWA_HEADS, HEAD_DIM), ang_row, ang_col) * scale
    k_l = rope_2d(k_l.reshape(b, t, SWA_KV_HEADS, HEAD_DIM), ang_row, ang_col)
    v_l = v_l.reshape(b, t, SWA_KV_HEADS, HEAD_DIM)
    n_ctx = zc[0].shape[1]
    k_c = zc[1].astype(f32).reshape(b, n_ctx, SWA_KV_HEADS, HEAD_DIM)
    v_c = zc[2].astype(f32).reshape(b, n_ctx, SWA_KV_HEADS, HEAD_DIM)
    sink = sink.astype(f32).reshape(SWA_KV_HEADS, SWA_REP)

    qb = q_l.reshape(b, nb, w, SWA_KV_HEADS, SWA_REP, HEAD_DIM)
    kb, vb = band_blocks(k_l), band_blocks(v_l)
    blk = jnp.arange(nb)[:, None, None]
    t_pos = blk * w + jnp.arange(w)[None, :, None]
    s_pos = blk * w + jnp.arange(3 * w)[None, None, :] - w
    valid = (s_pos >= 0) & (s_pos < t) & (jnp.abs(t_pos - s_pos) <= w)
    s_loc = jnp.where(valid[None, :, None, None], jnp.einsum('bnqgrd,bnkgd->bngrqk', qb, kb), -jnp.inf)
    s_ctx = jnp.einsum('bnqgrd,bkgd->bngrqk', qb, k_c)
    sink_l = sink[None, None, :, :, None, None]
    m = jnp.maximum(jnp.maximum(s_loc.max(-1, keepdims=True), s_ctx.max(-1, keepdims=True)), sink_l)
    p_loc = jnp.exp(s_loc - m)
    p_ctx = jnp.exp(s_ctx - m)
    inv = 1.0 / (p_loc.sum(-1, keepdims=True) + p_ctx.sum(-1, keepdims=True) + jnp.exp(sink_l - m))
    o = (jnp.einsum('bngrqk,bnkgd->bnqgrd', p_loc * inv, vb)
         + jnp.einsum('bngrqk,bkgd->bnqgrd', p_ctx * inv, v_c))
    y_lat = o.reshape(b, t, SWA_Q)
    if not need_ctx:
        return y_lat, None
    q_c = zc[0].astype(f32).reshape(b, n_ctx, SWA_KV_HEADS, SWA_REP, HEAD_DIM) * scale
    s = jnp.einsum('bqgrd,bkgd->bgrqk', q_c, k_c)
    sink_c = sink[None, :, :, None, None]
    mc = jnp.maximum(s.max(-1, keepdims=True), sink_c)
    p = jnp.exp(s - mc)
    p = p / (p.sum(-1, keepdims=True) + jnp.exp(sink_c - mc))
    y_ctx = jnp.einsum('bgrqk,bkgd->bqgrd', p, v_c).reshape(b, n_ctx, SWA_Q)
    return y_lat, y_ctx


def hybrid_mixer(h_lat, h_ctx, ang_row, ang_col, w_in, w_out, gla_wa2, gla_ba, gla_norm,
                 gmlp_ln_g, gmlp_ln_b, gmlp_ws, gmlp_bs, gmlp_out_g, swa_sink, swa_out_g, need_ctx):
    zl = split_cols(h_lat @ w_in)
    zc = split_cols(h_ctx @ w_in)
    a_lat, a_ctx = gla_mixer(zl[0:5], zc[0:5], gla_wa2, gla_ba, gla_norm, need_ctx)
    c_lat, c_ctx = swa_mixer(zl[6:9], zc[6:9], ang_row, ang_col, swa_sink, need_ctx)

    def merge(a_out, z_gm, c_out, dtype):
        b_out = rmsnorm(gmlp_mixer(z_gm, gmlp_ln_g, gmlp_ln_b, gmlp_ws, gmlp_bs), gmlp_out_g)
        cat = jnp.concatenate([a_out, b_out, rmsnorm(c_out, swa_out_g)], axis=-1)
        return cat.astype(dtype) @ w_out

    y_lat = merge(a_lat, zl[5], c_lat, h_lat.dtype)
    y_ctx = merge(a_ctx, zc[5], c_ctx, h_ctx.dtype) if need_ctx else None
    return y_lat, y_ctx


def swiglu(h, w_gu, w_down):
    g, u = jnp.split(h @ w_gu, 2, axis=-1)
    return (jax.nn.silu(g) * u) @ w_down


def setup_inputs(seed: int = 0) -> dict:
    key = jax.random.key(seed)
    ks = jax.random.split(key, 32)
    f32 = jnp.float32

    def nrm(k, shape, scale):
        return jax.random.normal(k, shape, f32) * scale

    def gain(k, shape):
        return 1.0 + 0.05 * jax.random.normal(k, shape, f32)

    d = D_MODEL
    return {
        'x': nrm(ks[0], (BATCH, SEQ, d), 1.0),
        'c': nrm(ks[1], (BATCH, d), 1.0),
        'ctx': nrm(ks[2], (BATCH, CTX_LEN, d), 1.0),
        'c_ctx': nrm(ks[3], (d,), 1.0),
        'mod_w': nrm(ks[4], (DEPTH, d, 6 * d), 0.5 * d ** -0.5),
        'mod_b': nrm(ks[5], (DEPTH, 6 * d), 0.01),
        'n1_pre': gain(ks[6], (DEPTH, d)),
        'n1_post': gain(ks[7], (DEPTH, d)),
        'n2_pre': gain(ks[8], (DEPTH, d)),
        'n2_post': gain(ks[9], (DEPTH, d)),
        'w_in': nrm(ks[10], (DEPTH, d, IN_COLS), d ** -0.5),
        'w_out': nrm(ks[11], (DEPTH, MIX_WIDTH, d), MIX_WIDTH ** -0.5),
        'gla_wa2': nrm(ks[12], (DEPTH, 2, GLA_GATE_RANK, GLA_QK), GLA_GATE_RANK ** -0.5),
        'gla_ba': nrm(ks[13], (DEPTH, 2, GLA_QK), 0.1),
        'gla_norm': gain(ks[14], (DEPTH, GLA_DV)),
        'gmlp_ln_g': gain(ks[15], (DEPTH, GMLP_WIDTH)),
        'gmlp_ln_b': nrm(ks[16], (DEPTH, GMLP_WIDTH), 0.02),
        'gmlp_ws': nrm(ks[17], (DEPTH, GMLP_GROUPS, GMLP_CHUNK, GMLP_CHUNK), GMLP_CHUNK ** -0.5),
        'gmlp_bs': 1.0 + nrm(ks[18], (DEPTH, GMLP_GROUPS, GMLP_CHUNK), 0.02),
        'gmlp_out_g': gain(ks[19], (DEPTH, GMLP_WIDTH)),
        'swa_sink': nrm(ks[20], (DEPTH, SWA_HEADS), 0.5),
        'swa_out_g': gain(ks[21], (DEPTH, SWA_Q)),
        'ffn_w_gu': nrm(ks[22], (DEPTH, d, 2 * D_FF), d ** -0.5),
        'ffn_w_down': nrm(ks[23], (DEPTH, D_FF, d), D_FF ** -0.5),
    }


def reference(x, c, ctx, c_ctx, mod_w, mod_b, n1_pre, n1_post, n2_pre, n2_post, w_in, w_out,
              gla_wa2, gla_ba, gla_norm, gmlp_ln_g, gmlp_ln_b, gmlp_ws, gmlp_bs, gmlp_out_g,
              swa_sink, swa_out_g, ffn_w_gu, ffn_w_down):
    n_lat = x.shape[1]
    rows = n_lat // GRID_W
    row = jnp.repeat(jnp.arange(rows), GRID_W).astype(jnp.float32)
    col = jnp.tile(jnp.arange(GRID_W), rows).astype(jnp.float32)
    inv_freq = jnp.power(ROPE_THETA, -jnp.arange(0, ROPE_AXIS_DIM, 2, dtype=jnp.float32) / ROPE_AXIS_DIM)
    ang_row = row[:, None] * inv_freq[None, :]
    ang_col = col[:, None] * inv_freq[None, :]

    x_lat, x_ctx = x, ctx
    for l in range(DEPTH):
        need_ctx = l < DEPTH - 1
        mod_l = [m[:, None, :] for m in jnp.split(jax.nn.silu(c) @ mod_w[l] + mod_b[l], 6, axis=-1)]
        mod_c = jnp.split(jax.nn.silu(c_ctx) @ mod_w[l] + mod_b[l], 6, axis=-1)

        h_lat = modulate(rmsnorm(x_lat, n1_pre[l]), mod_l[0], mod_l[1])
        h_ctx = modulate(rmsnorm(x_ctx, n1_pre[l]), mod_c[0], mod_c[1])
        y_lat, y_ctx = hybrid_mixer(h_lat, h_ctx, ang_row, ang_col, w_in[l], w_out[l],
                                    gla_wa2[l], gla_ba[l], gla_norm[l], gmlp_ln_g[l], gmlp_ln_b[l],
                                    gmlp_ws[l], gmlp_bs[l], gmlp_out_g[l], swa_sink[l], swa_out_g[l],
                                    need_ctx)
        x_lat = x_lat + mod_l[2] * rmsnorm(y_lat, n1_post[l])

        f_lat = swiglu(modulate(rmsnorm(x_lat, n2_pre[l]), mod_l[3], mod_l[4]), ffn_w_gu[l], ffn_w_down[l])
        x_lat = x_lat + mod_l[5] * rmsnorm(f_lat, n2_post[l])

        if need_ctx:
            x_ctx = x_ctx + mod_c[2] * rmsnorm(y_ctx, n1_post[l])
            f_ctx = swiglu(modulate(rmsnorm(x_ctx, n2_pre[l]), mod_c[3], mod_c[4]), ffn_w_gu[l], ffn_w_down[l])
            x_ctx = x_ctx + mod_c[5] * rmsnorm(f_ctx, n2_post[l])
    return x_lat
```

```python
import numpy as np
import ml_dtypes
import concourse.bass as bass
import concourse.mybir as mybir
from concourse.bass_utils import run_bass_kernel_spmd

F32 = mybir.dt.float32
BF16 = mybir.dt.bfloat16
ALU = mybir.AluOpType
AF = mybir.ActivationFunctionType
AX = mybir.AxisListType

D = 1024
NT = 18
DFF = 2816
NCORES = 8


class Res:
    __slots__ = ("name", "w", "r")

    def __init__(self, name=""):
        self.name = name
        self.w = None
        self.r = {}


class Op:
    __slots__ = ("eng", "fn", "deps", "is_dma", "sem", "val", "needs_inc", "ring_wait")

    def __init__(self, eng, fn, is_dma):
        self.eng = eng
        self.fn = fn
        self.deps = []
        self.is_dma = is_dma
        self.sem = None
        self.val = 0
        self.needs_inc = False
        self.ring_wait = None


class Sched:
    ENGS = ("pe", "dve", "act", "pool", "sp")
    RING = 12

    def __init__(self, nc):
        self.nc = nc
        self.ops = {e: [] for e in self.ENGS}
        self.dma_count = {e: 0 for e in self.ENGS}
        self.pending_barrier = {e: [] for e in self.ENGS}
        self.all_dma_since_barrier = []
        self.n_ops = 0

    def _dep(self, op, p, kind):
        if p is None or p is op:
            return
        if (not p.is_dma) and p.eng == op.eng and not op.is_dma and p.eng == "pe":
            return
        op.deps.append(p)
        if not p.is_dma:
            p.needs_inc = True

    def op(self, eng, fn, reads=(), writes=(), dma=False):
        o = Op(eng, fn, dma)
        for b in self.pending_barrier[eng]:
            self._dep(o, b, "raw")
        self.pending_barrier[eng] = []
        for r in reads:
            self._dep(o, r.w, "raw")
        for w in writes:
            self._dep(o, w.w, "waw")
            for rd in w.r.values():
                self._dep(o, rd, "war")
        for r in reads:
            if dma:
                r.r[("dma", id(o))] = o
            else:
                r.r[eng] = o
        for w in writes:
            w.w = o
            w.r = {}
        if dma:
            i = self.dma_count[eng]
            self.dma_count[eng] = i + 1
            o.sem = (eng, i % self.RING)
            o.val = 16 * (i // self.RING + 1)
            if i >= self.RING:
                o.ring_wait = (o.sem, o.val - 16)
            self.all_dma_since_barrier.append(o)
        self.ops[eng].append(o)
        self.n_ops += 1
        return o

    def barrier(self):
        lasts = []
        for e in self.ENGS:
            for o in reversed(self.ops[e]):
                if not o.is_dma:
                    lasts.append(o)
                    break
        lasts += self.all_dma_since_barrier
        self.all_dma_since_barrier = []
        for e in self.ENGS:
            self.pending_barrier[e] = list(lasts)

    def mm(self, out, lhsT, rhs, start, stop, reads, writes, **kw):
        return self.op("pe", lambda e: e.matmul(out, lhsT, rhs, start=start, stop=stop, **kw), reads, writes)

    def tr(self, out, in_, ident, reads, writes):
        return self.op("pe", lambda e: e.transpose(out, in_, ident), reads, writes)

    def dma(self, eng, out, in_, reads, writes):
        return self.op(eng, lambda e: e.dma_start(out=out, in_=in_), reads, writes, dma=True)

    def act(self, out, in_, func, reads, writes, **kw):
        return self.op("act", lambda e: e.activation(out, in_, func, **kw), reads, writes)

    def tt(self, eng, out, in0, in1, op, reads, writes):
        return self.op(eng, lambda e: e.tensor_tensor(out, in0, in1, op), reads, writes)

    def ts(self, eng, out, in0, s1, s2, op0, op1, reads, writes):
        return self.op(eng, lambda e: e.tensor_scalar(out, in0, s1, s2, op0, op1), reads, writes)

    def stt(self, eng, out, in0, scalar, in1, op0, op1, reads, writes):
        return self.op(eng, lambda e: e.scalar_tensor_tensor(out, in0, scalar, in1, op0, op1), reads, writes)

    def cp(self, eng, out, in_, reads, writes):
        if eng == "act":
            return self.op(eng, lambda e: e.copy(out, in_), reads, writes)
        return self.op(eng, lambda e: e.tensor_copy(out, in_), reads, writes)

    def memset(self, eng, ap, val, writes):
        return self.op(eng, lambda e: e.memset(ap, val), [], writes)

    def recip(self, out, in_, reads, writes):
        return self.op("dve", lambda e: e.reciprocal(out, in_), reads, writes)

    def reduce(self, out, in_, op, reads, writes):
        return self.op("dve", lambda e: e.tensor_reduce(out, in_, AX.X, op), reads, writes)

    def emit(self):
        nc = self.nc
        from contextlib import ExitStack

        with ExitStack() as st:
            esem = {e: st.enter_context(nc.semaphore("s_" + e)) for e in self.ENGS if e != "sp"}
            rsem = {}
            for e in self.ENGS:
                for k in range(min(self.RING, self.dma_count[e])):
                    rsem[(e, k)] = st.enter_context(nc.semaphore("d_%s_%d" % (e, k)))
            for e in self.ENGS:
                c = 0
                for o in self.ops[e]:
                    if o.is_dma:
                        continue
                    if o.needs_inc:
                        c += 1
                        o.val = c
            block = st.enter_context(nc.Block())

            def run(ename, eng):
                seen = {}

                def wait(semkey, v):
                    if seen.get(semkey, 0) >= v:
                        return
                    seen[semkey] = v
                    h = rsem[semkey] if isinstance(semkey, tuple) else esem[semkey]
                    eng.wait_ge(h, v)

                for o in self.ops[ename]:
                    for p in o.deps:
                        if p.is_dma:
                            wait(p.sem, p.val)
                        else:
                            wait(p.eng, p.val)
                    if o.ring_wait is not None:
                        wait(o.ring_wait[0], o.ring_wait[1])
                    ins = o.fn(eng)
                    if o.is_dma:
                        ins.then_inc(rsem[o.sem], 16)
                    elif o.needs_inc:
                        ins.then_inc(esem[ename], 1)
                n = self.dma_count[ename]
                for k in range(min(self.RING, n)):
                    uses = (n - 1 - k) // self.RING + 1
                    wait((ename, k), 16 * uses)

            @block.tensor
            def _(eng):
                run("pe", eng)

            @block.vector
            def _(eng):
                run("dve", eng)

            @block.scalar
            def _(eng):
                run("act", eng)

            @block.gpsimd
            def _(eng):
                run("pool", eng)

            @block.sync
            def _(eng):
                run("sp", eng)


class Arena:
    def __init__(self, nc, base=16384, limit=229376):
        self.nc = nc
        self.top = base
        self.limit = limit
        self.n = 0
        self.peak = base

    def mark(self):
        return self.top

    def release(self, m):
        self.top = m

    def alloc(self, shape, dtype, name="t"):
        esz = 4 if dtype == F32 else 2
        per_part = esz * int(np.prod(shape[1:]))
        per_part = (per_part + 63) // 64 * 64
        off = self.top
        self.top += per_part
        self.peak = max(self.peak, self.top)
        assert self.top <= self.limit, "SBUF arena overflow %d (%s)" % (self.top, name)
        self.n += 1
        return self.nc.alloc_sbuf_tensor_at("%s_%d" % (name, self.n), list(shape), dtype, offset=off)


class T:
    def __init__(self, A, shape, dtype, name="t"):
        self.t = A.alloc(shape, dtype, name)
        self.r = Res(name)


class Ring:
    def __init__(self, A, n, shape, dtype, name="r"):
        self.items = [T(A, shape, dtype, name) for _ in range(n)]
        self.i = 0

    def next(self):
        x = self.items[self.i % len(self.items)]
        self.i += 1
        return x


def _consts():
    c = {}
    c["ident"] = np.eye(128, dtype=np.float32)
    s = np.arange(128)
    same = (s[:, None] // 64) == (s[None, :] // 64)
    le = s[:, None] <= s[None, :]
    ge = s[:, None] >= s[None, :]
    mF = (same & le).astype(np.float32)
    mB = (same & ge).astype(np.float32)
    c["mF"] = mF
    c["mB"] = mB
    c["triF"] = mF / 16.0
    c["triB"] = mB / 16.0
    c["uF"] = (same & (s[:, None] > s[None, :])).astype(np.float32) / 16.0
    c["uB"] = (same & (s[:, None] < s[None, :])).astype(np.float32) / 16.0
    c["mP"] = ge.astype(np.float32)
    c["mN"] = le.astype(np.float32)
    p = np.arange(128)
    c["bm"] = (p[:, None] // 32 == np.arange(4)[None, :]).astype(np.float32)
    c["smask"] = np.repeat(c["bm"], 64, axis=1).astype(np.float32)
    rows = 2048 // 64
    row = np.repeat(np.arange(rows), 64).astype(np.float32)
    col = np.tile(np.arange(64), rows).astype(np.float32)
    inv_freq = np.power(np.float32(10000.0), -np.arange(0, 32, 2, dtype=np.float32) / np.float32(32)).astype(np.float32)
    ang_row = (row[:, None] * inv_freq[None, :]).astype(np.float32)
    ang_col = (col[:, None] * inv_freq[None, :]).astype(np.float32)
    cosT = np.zeros((64, 2048), np.float32)
    sinT = np.zeros((64, 2048), np.float32)
    for d in range(64):
        j = d % 16
        ang = ang_row if d < 32 else ang_col
        half = (d % 32) // 16
        cosT[d] = np.cos(ang[:, j])
        sn = np.sin(ang[:, j])
        sinT[d] = -sn if half == 0 else sn
    c["cosT"] = np.concatenate([cosT, cosT], 0)
    c["sinT"] = np.concatenate([sinT, sinT], 0)
    return c


def _partner(d):
    return d + 16 if (d % 32) < 16 else d - 16


def _prep_weights(inp):
    w = {}
    w_in = np.asarray(inp["w_in"], np.float32)
    w["w_in_a"] = np.ascontiguousarray(w_in[:, :, 0:1312])
    qoff, koff, voff = 1312, 1824, 1952
    qcols, qpcols = [], []
    for j in range(4):
        for hd in (j, 4 + j):
            for d in range(64):
                qcols.append(qoff + hd * 64 + d)
                qpcols.append(qoff + hd * 64 + _partner(d))
    kcols = [koff + g * 64 + d for g in range(2) for d in range(64)]
    kpcols = [koff + g * 64 + _partner(d) for g in range(2) for d in range(64)]
    vcols = list(range(voff, voff + 128))
    cols = qcols + kcols + vcols + qpcols + kpcols
    w["w_in_b"] = np.ascontiguousarray(w_in[:, :, cols])
    w["wsT"] = np.ascontiguousarray(np.transpose(np.asarray(inp["gmlp_ws"], np.float32), (0, 1, 3, 2)))
    w["bsT"] = np.ascontiguousarray(np.transpose(np.asarray(inp["gmlp_bs"], np.float32), (0, 2, 1)))
    w["gnorm4"] = np.ascontiguousarray(np.tile(np.asarray(inp["gla_norm"], np.float32), (1, 4)))
    for k in ("mod_w", "mod_b", "n1_pre", "n1_post", "n2_pre", "n2_post", "w_out", "gla_wa2", "gla_ba",
              "gmlp_ln_g", "gmlp_ln_b", "gmlp_out_g", "swa_sink", "swa_out_g", "ffn_w_gu", "ffn_w_down"):
        w[k] = np.ascontiguousarray(np.asarray(inp[k], np.float32))
    return w


def build(nb=2, nlayers=2, debug=False):
    nc = bass.Bass("TRN2", target_bir_lowering=False)
    S = Sched(nc)
    WBASE = 229376 - 24576 - 512
    A = Arena(nc, limit=WBASE)
    C = _consts()

    def din(name, shape, dt=F32):
        return nc.dram_tensor(name, list(shape), dt, kind="ExternalInput").ap()

    kind_dbg = "ExternalOutput" if debug else "Internal"
    xin = din("xin", [nb, NT * 128, D])
    ccT = din("ccT", [128, 8, 3])
    Wd = {}
    shapes = {
        "mod_w": [2, D, 6 * D], "mod_b": [2, 6 * D], "n1_pre": [2, D], "n1_post": [2, D], "n2_pre": [2, D],
        "n2_post": [2, D], "w_in_a": [2, D, 1312], "w_in_b": [2, D, 1408], "w_out": [2, D, D],
        "gla_wa2": [2, 2, 16, 128], "gla_ba": [2, 2, 128], "gnorm4": [2, 256], "gmlp_ln_g": [2, 256],
        "gmlp_ln_b": [2, 256], "wsT": [2, 4, 128, 128], "bsT": [2, 128, 4], "gmlp_out_g": [2, 256],
        "swa_sink": [2, 8], "swa_out_g": [2, 512], "ffn_w_gu": [2, D, 2 * DFF], "ffn_w_down": [2, DFF, D],
    }
    for k, shp in shapes.items():
        Wd[k] = din(k, shp)
    Cd = {k: din("c_" + k, list(v.shape)) for k, v in C.items()}
    out = nc.dram_tensor("out", [nb, 2048, D], F32, kind="ExternalOutput").ap()
    xs1 = nc.dram_tensor("xs1", [nb, NT * 128, D], F32, kind=kind_dbg).ap()
    xs2 = nc.dram_tensor("xs2", [nb, NT * 128, D], F32, kind=kind_dbg).ap()
    modv = nc.dram_tensor("modv", [2, 6, 3, D], F32, kind=kind_dbg).ap()
    dbg_cat = nc.dram_tensor("dbg_cat", [nb, NT * 128, D], BF16, kind="ExternalOutput").ap() if debug else None
    r_xs1 = [[Res() for _ in range(NT)] for _ in range(nb)]
    r_xs2 = [[Res() for _ in range(NT)] for _ in range(nb)]
    r_modv = Res()
    hts = nc.dram_tensor("hts", [nb, 5, 128, 8, 512], BF16, kind="Internal").ap()
    r_hts = [[Res() for _ in range(5)] for _ in range(nb)]

    PS = [nc.alloc_psum_tensor("ps%d" % i, [128, 512], F32) for i in range(8)]
    PR = [Res("ps%d" % i) for i in range(8)]

    ident = T(A, [128, 128], BF16, "ident")
    S.dma("pool", ident.t[:, :], Cd["ident"][:, :], [], [ident.r])
    cf = {}
    for k in ("mF", "mB", "triF", "triB", "uF", "uB"):
        cf[k] = T(A, [128, 128], F32, k)
        S.dma("sp", cf[k].t[:, :], Cd[k][:, :], [], [cf[k].r])
    for k in ("mP", "mN"):
        cf[k] = T(A, [128, 128], BF16, k)
        S.dma("pool", cf[k].t[:, :], Cd[k][:, :], [], [cf[k].r])
    bm = T(A, [128, 4], BF16, "bm")
    S.dma("pool", bm.t[:, :], Cd["bm"][:, :], [], [bm.r])
    smask = T(A, [128, 256], F32, "smask")
    S.dma("sp", smask.t[:, :], Cd["smask"][:, :], [], [smask.r])
    junk = T(A, [128, 1024], F32, "junk")
    eps6 = 1e-6
    epsT = {}
    for ev in (1e-6, 1e-5):
        epsT[ev] = T(A, [128, 1], F32, "eps")
        S.memset("dve", epsT[ev].t[:, :], ev, [epsT[ev].r])

    stat = Ring(A, 24, [128, 8], F32, "stat")

    def rstd_from_ss(ss, n, eps, reads):
        k = ss.shape[1]
        a = stat.next()
        S.act(a.t[:, 0:k], ss, AF.Ln, list(reads) + [epsT[eps].r], [a.r], scale=1.0 / n, bias=epsT[eps].t[:, 0:1])
        c_ = stat.next()
        S.act(c_.t[:, 0:k], a.t[:, 0:k], AF.Exp, [a.r], [c_.r], scale=-0.5)
        return c_

    def sumsq(in_ap, reads, n_free):
        a = stat.next()
        S.act(junk.t[:, 0:n_free], in_ap, AF.Square, reads, [junk.r, a.r], accum_out=a.t[:, 0:1])
        return a

    def load_bcast(dst, src_row):
        S.dma("sp", dst.t[:, :], src_row.partition_broadcast(128), [r_modv], [dst.r])

    m0 = A.mark()
    cc32 = T(A, [128, 8, 3], F32, "cc32")
    S.dma("sp", cc32.t[:, :, :], ccT[:, :, :], [], [cc32.r])
    scT = T(A, [128, 8, 3], BF16, "scT")
    S.act(scT.t[:, :, :], cc32.t[:, :, :], AF.Silu, [cc32.r], [scT.r])
    modraw = T(A, [3, 6 * D], F32, "modraw")
    biasT = T(A, [3, 6 * D], F32, "biasT")
    nrm3 = {k: T(A, [3, D], F32, k) for k in ("n1_pre", "n1_post", "n2_pre", "n2_post")}
    mwb = Ring(A, 2, [128, 8, 512], BF16, "mwb")
    mtmp = Ring(A, 2, [3, D], F32, "mtmp")
    for l in range(nlayers):
        S.dma("sp", biasT.t[:, :], Wd["mod_b"][l:l + 1, :].partition_broadcast(3), [], [biasT.r])
        for k in nrm3:
            S.dma("sp", nrm3[k].t[:, :], Wd[k][l:l + 1, :].partition_broadcast(3), [], [nrm3[k].r])
        mw = Wd["mod_w"][l].rearrange("(k p) n -> p k n", p=128)
        for blk in range(12):
            wb = mwb.next()
            S.dma("pool", wb.t[:, :, :], mw[:, :, blk * 512:(blk + 1) * 512], [], [wb.r])
            pb = blk % 2
            for k in range(8):
                S.mm(PS[pb][0:3, :], scT.t[:, k, :], wb.t[:, k, :], k == 0, k == 7, [scT.r, wb.r], [PR[pb]])
            S.tt("dve", modraw.t[:, blk * 512:(blk + 1) * 512], PS[pb][0:3, :], biasT.t[:, blk * 512:(blk + 1) * 512],
                 ALU.add, [PR[pb], biasT.r], [modraw.r])
        combos = [(1, "n1_pre", "a"), (0, None, "b"), (2, "n1_post", "c"), (4, "n2_pre", "a"), (3, None, "b"), (5, "n2_post", "c")]
        for w_, (mi, nk, kind) in enumerate(combos):
            src = modraw.t[:, mi * D:(mi + 1) * D]
            if kind == "b":
                S.dma("sp", modv[l, w_, :, :], src, [modraw.r], [r_modv])
                continue
            tmp = mtmp.next()
            if kind == "a":
                S.stt("dve", tmp.t[:, :], src, 1.0, nrm3[nk].t[:, :], ALU.add, ALU.mult, [modraw.r, nrm3[nk].r], [tmp.r])
            else:
                S.tt("dve", tmp.t[:, :], src, nrm3[nk].t[:, :], ALU.mult, [modraw.r, nrm3[nk].r], [tmp.r])
            S.dma("sp", modv[l, w_, :, :], tmp.t[:, :], [tmp.r], [r_modv])
    S.barrier()
    A.release(m0)

    def norm_elem(xt, Am, Bm, hring, add_eng="pool"):
        ss = sumsq(xt.t[:, :], [xt.r], 1024)
        rs = rstd_from_ss(ss.t[:, 0:1], 1024.0, eps6, [ss.r])
        tmp = hring["tmp"].next()
        S.stt("dve", tmp.t[:, :], xt.t[:, :], rs.t[:, 0:1], Am.t[:, :], ALU.mult, ALU.mult, [xt.r, rs.r, Am.r], [tmp.r])
        h = hring["h"].next()
        S.tt(add_eng, h.t[:, :], tmp.t[:, :], Bm.t[:, :], ALU.add, [tmp.r, Bm.r], [h.r])
        return h

    def norm_tr(h, hT, hT_res, col0, trbank):
        pbf = PS[trbank][:, :].bitcast(BF16)
        for k in range(8):
            S.tr(pbf[:, k * 128:(k + 1) * 128], h.t[:, k * 128:(k + 1) * 128], ident.t[:, :], [h.r, ident.r], [PR[trbank]])
        S.cp("dve", hT[:, :, col0:col0 + 128], pbf[:, :].rearrange("p (k t) -> p k t", k=8), [PR[trbank]], [hT_res])

    def norm_mod_T(xt, Am, Bm, hT, col0, hring, trbank):
        h = norm_elem(xt, Am, Bm, hring)
        norm_tr(h, hT.t, hT.r, col0, trbank)

    def post_residual(y_banks, xt, Cm, dst_ap, dst_res, oring):
        s0 = sumsq(PS[y_banks[0]][:, :], [PR[y_banks[0]]], 512)
        s1 = sumsq(PS[y_banks[1]][:, :], [PR[y_banks[1]]], 512)
        st = stat.next()
        S.tt("dve", st.t[:, 0:1], s0.t[:, 0:1], s1.t[:, 0:1], ALU.add, [s0.r, s1.r], [st.r])
        rs = rstd_from_ss(st.t[:, 0:1], 1024.0, eps6, [st.r])
        o = oring.next()
        for hf in range(2):
            S.stt("dve", o.t[:, hf * 512:(hf + 1) * 512], PS[y_banks[hf]][:, :], rs.t[:, 0:1], Cm.t[:, hf * 512:(hf + 1) * 512],
                  ALU.mult, ALU.mult, [PR[y_banks[hf]], rs.r, Cm.r], [o.r])
        S.tt("dve", o.t[:, :], o.t[:, :], xt.t[:, :], ALU.add, [o.r, xt.r], [o.r])
        S.dma("sp", dst_ap, o.t[:, :], [o.r], [dst_res])
        return o

    class TX:
        def __init__(self, t, name="w"):
            self.t = t
            self.r = Res(name)

    wcur = {"res": []}
    wmark = T(A, [128, 1], F32, "wmark")
    wcount = [0]
    pre = {"wa": None}

    def w_marker():
        if wcur["res"]:
            S.memset("pool", wmark.t[:, :], 0.0, [wmark.r] + list(wcur["res"]))

    def w_load(kind, l_):
        w_marker()
        wcount[0] += 1
        ncols = {"wa": 1312, "wb": 1408, "wo": D}[kind]
        src = {"wa": Wd["w_in_a"], "wb": Wd["w_in_b"], "wo": Wd["w_out"]}[kind][l_].rearrange("(k p) n -> p k n", p=128)
        t = nc.alloc_sbuf_tensor_at("W%s_%d" % (kind, wcount[0]), [128, 8, ncols], BF16, offset=WBASE)
        rs = [Res() for _ in range(8)]
        for k in range(8):
            S.dma("pool", t[:, k, :], src[:, k, :], [], [rs[k]])
        wcur["res"] = rs
        return TX(t, kind), rs

    def w_rings():
        w_marker()
        wcount[0] += 1
        items = [TX(nc.alloc_sbuf_tensor_at("Wr%d_%d" % (i, wcount[0]), [128, 8, 256], BF16, offset=WBASE + i * 4096)) for i in range(6)]
        wcur["res"] = [x.r for x in items]
        rg, ru = Ring.__new__(Ring), Ring.__new__(Ring)
        rg.items, rg.i = items[0:3], 0
        ru.items, ru.i = items[3:6], 0
        return rg, ru

    for l in range(nlayers):
        need_ctx = l < 1
        last = l == nlayers - 1
        xsrc, r_xsrc = (xin, None) if l == 0 else (xs2, r_xs2)
        tiles_all = list(range(NT)) if need_ctx else list(range(2, NT))
        mL = A.mark()
        W2 = T(A, [33, 256], F32, "W2")
        S.memset("pool", W2.t[:, :], 0.0, [W2.r])
        S.dma("sp", W2.t[0:16, 0:128], Wd["gla_wa2"][l, 0, :, :], [], [W2.r])
        S.dma("sp", W2.t[16:32, 128:256], Wd["gla_wa2"][l, 1, :, :], [W2.r], [W2.r])
        S.dma("sp", W2.t[32:33, 0:128], Wd["gla_ba"][l, 0:1, :], [W2.r], [W2.r])
        S.dma("sp", W2.t[32:33, 128:256], Wd["gla_ba"][l, 1:2, :], [W2.r], [W2.r])
        wsT = T(A, [128, 4, 128], BF16, "wsT")
        S.dma("pool", wsT.t[:, :, :], Wd["wsT"][l].rearrange("g q p -> q g p"), [], [wsT.r])
        bsT = T(A, [128, 4], F32, "bsT")
        S.dma("sp", bsT.t[:, :], Wd["bsT"][l, :, :], [], [bsT.r])
        fv = {}
        for k, n in (("gnorm4", 256), ("gmlp_ln_g", 256), ("gmlp_ln_b", 256), ("gmlp_out_g", 256), ("swa_out_g", 512)):
            fv[k] = T(A, [128, n], F32, k)
            S.dma("sp", fv[k].t[:, :], Wd[k][l:l + 1, :].partition_broadcast(128), [], [fv[k].r])
        esink = T(A, [128, 8], F32, "esink")
        S.dma("sp", esink.t[:, :], Wd["swa_sink"][l:l + 1, :].partition_broadcast(128), [], [esink.r])
        S.act(esink.t[:, :], esink.t[:, :], AF.Exp, [esink.r], [esink.r])

        for b in range(nb):
            mB_ = A.mark()
            cat = T(A, [128, NT, D], BF16, "cat")
            catr = [Res() for _ in range(NT)]
            A1 = [T(A, [128, D], F32, "A1") for _ in range(2)]
            B1 = [T(A, [128, D], F32, "B1") for _ in range(2)]
            for si, j in ((0, 2), (1, b)):
                load_bcast(A1[si], modv[l, 0, j:j + 1, :])
                load_bcast(B1[si], modv[l, 1, j:j + 1, :])
            groups = [[0, 1], [2, 3, 4, 5], [6, 7, 8, 9], [10, 11, 12, 13], [14, 15, 16, 17]]

            def load_x(tt, xring):
                xt = xring.next()
                rd = [] if r_xsrc is None else [r_xsrc[b][tt]]
                S.dma("sp", xt.t[:, :], xsrc[b, tt * 128:(tt + 1) * 128, :], rd, [xt.r])
                return xt

            mG = A.mark()
            qst = T(A, [128, 2, NT * 128], BF16, "qst")
            kst = T(A, [128, 2, NT * 128], BF16, "kst")
            kpst = T(A, [128, NT, 2, 128], BF16, "kpst")
            vst = T(A, [128, NT, 256], BF16, "vst")
            sog = T(A, [128, NT, 256], BF16, "sog")
            dst_ = T(A, [128, 2, 2 * NT], F32, "dst")
            r_q = [Res() for _ in range(NT)]
            r_k = [Res() for _ in range(NT)]
            r_kp = [Res() for _ in range(NT)]
            r_v = [Res() for _ in range(NT)]
            r_sog = [Res() for _ in range(NT)]
            r_d = [Res() for _ in range(NT)]
            mP1 = A.mark()
            if pre["wa"] is not None:
                wa, wa_r = pre["wa"]
                pre["wa"] = None
            else:
                wa, wa_r = w_load("wa", l)
            hTs = [T(A, [128, 8, 512], BF16, "hT") for _ in range(2)]
            xring = Ring(A, 2, [128, D], F32, "xt")
            hring = {"tmp": Ring(A, 1, [128, D], F32, "ntmp"), "h": Ring(A, 5, [128, D], BF16, "h")}
            codes = T(A, [33, 512], F32, "codes")
            S.memset("pool", codes.t[32:33, :], 1.0, [codes.r])
            R2 = lambda shp, dt, nm: Ring(A, 2, shp, dt, nm)
            e_sbR, spR, e1R, e2R, erR = (R2([128, 256], F32, n_) for n_ in ("e_sb", "sp", "e1", "e2", "er"))
            zfR = R2([128, 512], F32, "zf")
            vnR = R2([128, 256], F32, "vn")
            vgR = R2([128, 256], BF16, "vg")
            goutR = R2([128, 256], F32, "gout")
            bnstR = R2([128, 6], F32, "bnst")
            bnagR = R2([128, 2], F32, "bnag")
            ktokR = R2([128, 128], F32, "ktok")
            sgR = R2([128, 256], F32, "sgate")
            cq = 32.0 ** -0.5

            def p1_elem(grp_):
                hs_ = []
                for tt_ in grp_:
                    xt_ = load_x(tt_, xring)
                    si_ = 0 if tt_ < 2 else 1
                    hs_.append(norm_elem(xt_, A1[si_], B1[si_], hring))
                return hs_

            def p1_tr(hs_, hT_, gi_):
                for i_, h_ in enumerate(hs_):
                    norm_tr(h_, hT_.t, hT_.r, i_ * 128, 0)
                n_ = len(hs_) * 128
                S.dma("sp", hts[b, gi_, :, :, 0:n_], hT_.t[:, :, 0:n_], [hT_.r], [r_hts[b][gi_]])

            def stage_a(hT, i, tt):
                cs = slice(i * 128, (i + 1) * 128)
                for (bank, o0, c0, m) in ((4, 0, 128, 384), (5, 0, 512, 256), (6, 0, 800, 512)):
                    for k in range(8):
                        S.mm(PS[bank][:, o0:o0 + m], hT.t[:, k, cs], wa.t[:, k, c0:c0 + m], k == 0, k == 7,
                             [hT.r, wa_r[k]], [PR[bank]])
                S.mm(PS[5][:, 256:512], codes.t[0:33, cs], W2.t[0:33, :], True, True, [codes.r, W2.r], [PR[5]])
                st_ = {}
                e_sb, sp_, zf, ktok = e_sbR.next(), spR.next(), zfR.next(), ktokR.next()
                S.act(e_sb.t[:, :], PS[5][:, 256:512], AF.Exp, [PR[5]], [e_sb.r], scale=-1.0)
                S.act(sp_.t[:, :], e_sb.t[:, :], AF.Ln, [e_sb.r], [sp_.r], bias=1.0)
                S.cp("dve", ktok.t[:, :], PS[4][:, 0:128], [PR[4]], [ktok.r])
                S.cp("dve", vst.t[:, tt, :], PS[4][:, 128:384], [PR[4]], [r_v[tt]])
                sg_ = sgR.next()
                S.act(sg_.t[:, :], PS[5][:, 0:256], AF.Exp, [PR[5]], [sg_.r], scale=-1.0)
                S.act(sg_.t[:, :], sg_.t[:, :], AF.Ln, [sg_.r], [sg_.r], bias=1.0)
                S.act(sg_.t[:, :], sg_.t[:, :], AF.Exp, [sg_.r], [sg_.r], scale=-1.0)
                S.tt("dve", sog.t[:, tt, :], PS[5][:, 0:256], sg_.t[:, :], ALU.mult, [PR[5], sg_.r], [r_sog[tt]])
                S.act(zf.t[:, :], PS[6][:, :], AF.Gelu, [PR[6]], [zf.r])
                bnst, bnag, vn, vg = bnstR.next(), bnagR.next(), vnR.next(), vgR.next()
                S.op("dve", (lambda a, b_: (lambda e: e.bn_stats(a, b_)))(bnst.t[:, :], zf.t[:, 256:512]), [zf.r], [bnst.r])
                S.op("dve", (lambda a, b_: (lambda e: e.bn_aggr(a, b_)))(bnag.t[:, :], bnst.t[:, :]), [bnst.r], [bnag.r])
                rs = rstd_from_ss(bnag.t[:, 1:2], 1.0, 1e-5, [bnag.r])
                S.ts("dve", vn.t[:, :], zf.t[:, 256:512], bnag.t[:, 0:1], rs.t[:, 0:1], ALU.subtract, ALU.mult,
                     [zf.r, bnag.r, rs.r], [vn.r])
                S.tt("pool", vn.t[:, :], vn.t[:, :], fv["gmlp_ln_g"].t[:, :], ALU.mult, [vn.r, fv["gmlp_ln_g"].r], [vn.r])
                S.tt("pool", vg.t[:, :], vn.t[:, :], fv["gmlp_ln_b"].t[:, :], ALU.add, [vn.r, fv["gmlp_ln_b"].r], [vg.r])
                return dict(sp=sp_, zf=zf, ktok=ktok, vg=vg, cs=cs, tt=tt)

            def stage_b(st_):
                sp_, zf, ktok, vg, cs, tt = st_["sp"], st_["zf"], st_["ktok"], st_["vg"], st_["cs"], st_["tt"]
                tk = slice(tt * 128, (tt + 1) * 128)
                S.mm(PS[7][:, 0:128], sp_.t[:, 0:128], cf["triF"].t[:, :], True, True, [sp_.r, cf["triF"].r], [PR[7]])
                S.mm(PS[7][:, 128:256], sp_.t[:, 128:256], cf["triB"].t[:, :], True, True, [sp_.r, cf["triB"].r], [PR[7]])
                S.mm(PS[7][:, 256:384], cf["uF"].t[:, :], sp_.t[:, 0:128], True, True, [sp_.r, cf["uF"].r], [PR[7]])
                S.mm(PS[7][:, 384:512], cf["uB"].t[:, :], sp_.t[:, 128:256], True, True, [sp_.r, cf["uB"].r], [PR[7]])
                for g in range(4):
                    S.mm(PS[3][:, g * 64:(g + 1) * 64], wsT.t[:, g, :], vg.t[:, g * 64:(g + 1) * 64], True, True,
                         [wsT.r, vg.r], [PR[3]])
                e1, e2, er = e1R.next(), e2R.next(), erR.next()
                S.act(e1.t[:, :], PS[7][:, 0:256], AF.Exp, [PR[7]], [e1.r], scale=-1.0)
                S.act(e2.t[:, :], PS[7][:, 0:256], AF.Exp, [PR[7]], [e2.r])
                S.act(er.t[:, :], PS[7][:, 256:512], AF.Exp, [PR[7]], [er.r], scale=-1.0)
                e1v = e1.t[:, :].rearrange("p (d t) -> p d t", d=2)
                e2v = e2.t[:, :].rearrange("p (d t) -> p d t", d=2)
                erv = er.t[:, :].rearrange("p (d t) -> p d t", d=2)
                S.stt("dve", qst.t[:, :, tk], e1v, cq, PS[1][:, cs].unsqueeze(1).to_broadcast([128, 2, 128]),
                      ALU.mult, ALU.mult, [e1.r, PR[1]], [r_q[tt]])
                S.tt("dve", kst.t[:, :, tk], e2v, PS[2][:, cs].unsqueeze(1).to_broadcast([128, 2, 128]), ALU.mult,
                     [e2.r, PR[2]], [r_k[tt]])
                S.tt("dve", kpst.t[:, tt, :, :], erv, ktok.t[:, :].unsqueeze(1).to_broadcast([128, 2, 128]), ALU.mult,
                     [er.r, ktok.r], [r_kp[tt]])
                S.cp("dve", dst_.t[:, 0, 2 * tt:2 * tt + 2], e1.t[:, 63:128:64], [e1.r], [r_d[tt]])
                S.cp("dve", dst_.t[:, 1, 2 * tt:2 * tt + 2], e1.t[:, 128:256:64], [e1.r], [r_d[tt]])
                gout = goutR.next()
                S.tt("dve", gout.t[:, :].rearrange("p (g c) -> p g c", g=4), PS[3][:, 0:256].rearrange("p (g c) -> p g c", g=4),
                     bsT.t[:, :].unsqueeze(2).to_broadcast([128, 4, 64]), ALU.add, [PR[3], bsT.r], [gout.r])
                S.tt("dve", gout.t[:, :], gout.t[:, :], zf.t[:, 0:256], ALU.mult, [gout.r, zf.r], [gout.r])
                ss = sumsq(gout.t[:, :], [gout.r], 256)
                rs2 = rstd_from_ss(ss.t[:, 0:1], 256.0, eps6, [ss.r])
                S.stt("dve", cat.t[:, tt, 256:512], gout.t[:, :], rs2.t[:, 0:1], fv["gmlp_out_g"].t[:, :], ALU.mult, ALU.mult,
                      [gout.r, rs2.r, fv["gmlp_out_g"].r], [catr[tt]])

            p1_tr(p1_elem(groups[0]), hTs[0], 0)
            for gi, grp in enumerate(groups):
                n = len(grp) * 128
                hT = hTs[gi % 2]
                hs_next = p1_elem(groups[gi + 1]) if gi + 1 < len(groups) else None
                for (bank, c0, m) in ((3, 768, 32), (1, 0, 128), (2, 128, 128)):
                    for k in range(8):
                        S.mm(PS[bank][0:m, 0:n], wa.t[:, k, c0:c0 + m], hT.t[:, k, 0:n], k == 0, k == 7,
                             [wa_r[k], hT.r], [PR[bank]])
                    if bank == 3:
                        S.cp("act", codes.t[0:32, 0:n], PS[3][0:32, 0:n], [PR[3]], [codes.r])
                pend = None
                for i, tt in enumerate(grp):
                    st_ = stage_a(hT, i, tt)
                    if i == 0 and hs_next is not None:
                        p1_tr(hs_next, hTs[(gi + 1) % 2], gi + 1)
                    if pend is not None:
                        stage_b(pend)
                    pend = st_
                stage_b(pend)
            S.barrier()
            A.release(mP1)

            wb_, wb_r = w_load("wb", l)
            ofs = T(A, [128, NT, 256], F32, "ofs")
            ofs_r = [Res() for _ in range(NT)]
            osbR = Ring(A, 2, [128, 256], F32, "osb")
            osqR = Ring(A, 2, [128, 256], F32, "osq")
            arrived = [False] * NT
            dirs = []
            for dr in range(2):
                dd = dict(dr=dr, Sst=Ring(A, 2, [128, 256], F32, "Sst"), Sbd=Ring(A, 3, [128, 256], BF16, "Sbd"),
                          Qbd=Ring(A, 2, [128, 4, 128], BF16, "Qbd"), att=Ring(A, 2, [128, 4, 128], BF16, "att"),
                          order=list(range(NT)) if dr == 0 else [1, 0] + list(range(NT - 1, 1, -1)),
                          mk=cf["mF"] if dr == 0 else cf["mB"], kvb=(0, 1) if dr == 0 else (4, 5), ab=2 + dr, ob=6 + dr)
                dd["Scur"] = dd["Sst"].next()
                S.memset("dve", dd["Scur"].t[:, :], 0.0, [dd["Scur"].r])
                dd["Sb0"] = dd["Sbd"].next()
                S.memset("dve", dd["Sb0"].t[:, :], 0.0, [dd["Sb0"].r])
                dirs.append(dd)

            def gla_step(dd, tt):
                dr = dd["dr"]
                tk0 = tt * 128
                chunks = (0, 1) if dr == 0 else (1, 0)
                need_out = need_ctx or tt >= 2
                obank, abank = dd["ob"], dd["ab"]
                Sbs = [dd["Sb0"]]
                for ci, c in enumerate(chunks):
                    rows = slice(c * 64, (c + 1) * 64)
                    kvbank = dd["kvb"][ci]
                    S.mm(PS[kvbank][:, 0:256], kpst.t[rows, tt, dr, :], vst.t[rows, tt, :], True, True,
                         [r_kp[tt], r_v[tt]], [PR[kvbank]])
                    Sn = dd["Sst"].next()
                    ch = 2 * tt + c
                    S.stt("dve", Sn.t[:, :], dd["Scur"].t[:, :], dst_.t[:, dr, ch:ch + 1], PS[kvbank][:, 0:256], ALU.mult, ALU.add,
                          [dd["Scur"].r, r_d[tt], PR[kvbank]], [Sn.r])
                    dd["Scur"] = Sn
                    Sbn = dd["Sbd"].next()
                    S.tt("pool", Sbn.t[:, :], Sn.t[:, :], smask.t[:, :], ALU.mult, [Sn.r, smask.r], [Sbn.r])
                    Sbs.append(Sbn)
                    if ci == 0 and need_out:
                        qb = dd["Qbd"].next()
                        S.tt("dve", qb.t[:, :, :], qst.t[:, dr, tk0:tk0 + 128].unsqueeze(1).to_broadcast([128, 4, 128]),
                             bm.t[:, :].unsqueeze(2).to_broadcast([128, 4, 128]), ALU.mult, [r_q[tt], bm.r], [qb.r])
                        S.mm(PS[abank][:, :], kst.t[:, dr, tk0:tk0 + 128], qb.t[:, :, :].rearrange("p h t -> p (h t)"), True, True,
                             [r_k[tt], qb.r], [PR[abank]])
                        at = dd["att"].next()
                        S.tt("dve", at.t[:, :, :], PS[abank][:, :].rearrange("p (h t) -> p h t", h=4),
                             dd["mk"].t[:, :].unsqueeze(1).to_broadcast([128, 4, 128]), ALU.mult, [PR[abank], dd["mk"].r], [at.r])
                        for cj, c2 in enumerate(chunks):
                            r2 = slice(c2 * 64, (c2 + 1) * 64)
                            kw = {"tile_position": (0, 64)} if c2 == 1 else {}
                            S.mm(PS[obank][r2, 0:256], qst.t[:, dr, tk0 + c2 * 64:tk0 + (c2 + 1) * 64], Sbs[cj].t[:, :],
                                 True, False, [r_q[tt], Sbs[cj].r], [PR[obank]], skip_group_check=True, **kw)
                        for h in range(4):
                            S.mm(PS[obank][:, h * 64:(h + 1) * 64], at.t[:, h, :], vst.t[:, tt, h * 64:(h + 1) * 64], False, h == 3,
                                 [at.r, r_v[tt]], [PR[obank]], skip_group_check=True)
                        if not arrived[tt]:
                            arrived[tt] = True
                            S.cp("act", ofs.t[:, tt, :], PS[obank][:, 0:256], [PR[obank]], [ofs_r[tt]])
                        else:
                            osb, osq = osbR.next(), osqR.next()
                            S.tt("dve", osb.t[:, :], PS[obank][:, 0:256], ofs.t[:, tt, :], ALU.add, [PR[obank], ofs_r[tt]], [osb.r])
                            S.tt("pool", osq.t[:, :], osb.t[:, :], osb.t[:, :], ALU.mult, [osb.r], [osq.r])
                            s4 = stat.next()
                            S.reduce(s4.t[:, 0:4], osq.t[:, :].rearrange("p (h d) -> p h d", h=4), ALU.add, [osq.r], [s4.r])
                            r4 = rstd_from_ss(s4.t[:, 0:4], 64.0, eps6, [s4.r])
                            S.tt("dve", osb.t[:, :].rearrange("p (h d) -> p h d", h=4), osb.t[:, :].rearrange("p (h d) -> p h d", h=4),
                                 r4.t[:, 0:4].unsqueeze(2).to_broadcast([128, 4, 64]), ALU.mult, [osb.r, r4.r], [osb.r])
                            S.tt("pool", osb.t[:, :], osb.t[:, :], fv["gnorm4"].t[:, :], ALU.mult, [osb.r, fv["gnorm4"].r], [osb.r])
                            S.tt("dve", cat.t[:, tt, 0:256], osb.t[:, :], sog.t[:, tt, :], ALU.mult, [osb.r, r_sog[tt]], [catr[tt]])
                dd["Sb0"] = Sbs[2]

            for stp in range(NT):
                for dd in dirs:
                    gla_step(dd, dd["order"][stp])
            S.barrier()
            A.release(mG)

            mS = A.mark()
            cosT = T(A, [128, 2048], F32, "cosT")
            sinT = T(A, [128, 2048], F32, "sinT")
            S.dma("sp", cosT.t[:, :], Cd["cosT"][:, :], [], [cosT.r])
            S.dma("sp", sinT.t[:, :], Cd["sinT"][:, :], [], [sinT.r])
            qs = T(A, [128, 4, 2048], BF16, "qs")
            qc = T(A, [128, 4, 256], BF16, "qc")
            ks = T(A, [128, NT * 128], BF16, "ks")
            va = T(A, [128, NT, 2, 65], BF16, "va")
            S.memset("pool", va.t[:, :, :, :].rearrange("p a b c -> p (a b c)"), 1.0, [va.r])
            grp_r = [Res() for _ in range(5)]
            mP2 = A.mark()
            hTs = [T(A, [128, 8, 512], BF16, "hT2") for _ in range(2)]
            rt = Ring(A, 2, [128, 512], F32, "ropetmp")

            def p2_load(gi_):
                n_ = len(groups[gi_]) * 128
                S.dma("sp", hTs[gi_ % 2].t[:, :, 0:n_], hts[b, gi_, :, :, 0:n_], [r_hts[b][gi_]], [hTs[gi_ % 2].r])

            p2_load(0)
            for gi, grp in enumerate(groups):
                n = len(grp) * 128
                is_ctx = gi == 0
                hT = hTs[gi % 2]
                if gi + 1 < len(groups):
                    p2_load(gi + 1)
                p0 = (grp[0] - 2) * 128
                units = []
                if not (is_ctx and not need_ctx):
                    units += [("q", j) for j in range(4)]
                units.append(("k", 0))
                for ui, (kind, j) in enumerate(units):
                    c0 = j * 128 if kind == "q" else 512
                    cp0 = 768 + j * 128 if kind == "q" else 1280
                    b1, b2 = 1 + 2 * (ui % 2), 2 + 2 * (ui % 2)
                    for k in range(8):
                        S.mm(PS[b1][:, 0:n], wb_.t[:, k, c0:c0 + 128], hT.t[:, k, 0:n], k == 0, k == 7, [wb_r[k], hT.r], [PR[b1]])
                    if is_ctx:
                        dst = qc.t[:, j, :] if kind == "q" else ks.t[:, 0:256]
                        S.cp("act", dst, PS[b1][:, 0:n], [PR[b1]], [grp_r[gi]])
                        continue
                    for k in range(8):
                        S.mm(PS[b2][:, 0:n], wb_.t[:, k, cp0:cp0 + 128], hT.t[:, k, 0:n], k == 0, k == 7, [wb_r[k], hT.r], [PR[b2]])
                    t1, t2 = rt.next(), rt.next()
                    S.tt("dve", t1.t[:, :], PS[b1][:, :], cosT.t[:, p0:p0 + 512], ALU.mult, [PR[b1], cosT.r], [t1.r])
                    S.tt("dve", t2.t[:, :], PS[b2][:, :], sinT.t[:, p0:p0 + 512], ALU.mult, [PR[b2], sinT.r], [t2.r])
                    dst = qs.t[:, j, p0:p0 + 512] if kind == "q" else ks.t[:, 256 + p0:256 + p0 + 512]
                    S.tt("pool", dst, t1.t[:, :], t2.t[:, :], ALU.add, [t1.r, t2.r], [grp_r[gi]])
                for i, tt in enumerate(grp):
                    cs = slice(i * 128, (i + 1) * 128)
                    vb = 5 + (i % 2)
                    for k in range(8):
                        S.mm(PS[vb][:, 0:128], hT.t[:, k, cs], wb_.t[:, k, 640:768], k == 0, k == 7, [hT.r, wb_r[k]], [PR[vb]])
                    S.cp("act", va.t[:, tt, :, 0:64], PS[vb][:, 0:128].rearrange("p (g d) -> p g d", g=2), [PR[vb], va.r], [grp_r[gi]])
            S.barrier()
            A.release(mP2)

            wo, wo_r = w_load("wo", l)
            pT = Ring(A, 4, [128, 4, 128], BF16, "pT")
            csos = Ring(A, 2, [128, 512], F32, "cso")
            den = Ring(A, 2, [128, 4], F32, "den")
            all_r = grp_r + [va.r]
            qblocks = ([("c", 0), ("c", 1)] if need_ctx else []) + [("l", n_) for n_ in range(16)]
            units = []
            for (qk, n_) in qblocks:
                tt = n_ if qk == "c" else 2 + n_
                cso = csos.next()
                for g in range(2):
                    pr = slice(64 * g, 64 * g + 64)
                    if qk == "c":
                        qap = qc.t[pr, :, n_ * 128:(n_ + 1) * 128]
                        keys = [(0, None), (1, None)]
                    else:
                        qap = qs.t[pr, :, n_ * 128:(n_ + 1) * 128]
                        keys = [(0, None), (1, None)]
                        if n_ > 0:
                            keys.append((2 + n_ - 1, "mP"))
                        keys.append((2 + n_, None))
                        if n_ < 15:
                            keys.append((2 + n_ + 1, "mN"))
                    units.append((tt, g, pr, qap, keys, cso))
            steps = [(ui, ki) for ui, u in enumerate(units) for ki in range(len(u[4]))]
            LOOK = 2

            def swa_score(si):
                ui, ki = steps[si]
                tt, g, pr, qap, keys, cso = units[ui]
                kt = keys[ki][0]
                sb = 1 + si % 4
                S.mm(PS[sb][:, :].rearrange("p (r q) -> p r q", r=4), ks.t[pr, kt * 128:(kt + 1) * 128], qap, True, True, all_r, [PR[sb]])

            def swa_rest(si):
                ui, ki = steps[si]
                tt, g, pr, qap, keys, cso = units[ui]
                kt, mname = keys[ki]
                sb = 1 + si % 4
                ob = 6 + (ui % 2)
                p = pT.next()
                S.act(p.t[:, :, :], PS[sb][:, :].rearrange("p (r q) -> p r q", r=4), AF.Exp, [PR[sb]], [p.r], scale=0.125)
                if mname is not None:
                    S.tt("dve", p.t[:, :, :], p.t[:, :, :], cf[mname].t[:, :].unsqueeze(1).to_broadcast([128, 4, 128]), ALU.mult,
                         [p.r, cf[mname].r], [p.r])
                for r_ in range(4):
                    S.mm(PS[ob][:, r_ * 65:(r_ + 1) * 65], p.t[:, r_, :], va.t[:, kt, g, :], ki == 0 and r_ == 0,
                         ki == len(keys) - 1 and r_ == 3, [p.r] + all_r, [PR[ob]], skip_group_check=True)
                if ki < len(keys) - 1:
                    return
                ov = PS[ob][:, 0:260].rearrange("p (r d) -> p r d", r=4)
                dn = den.next()
                S.tt("dve", dn.t[:, :], ov[:, :, 64], esink.t[:, 4 * g:4 * g + 4], ALU.add, [PR[ob], esink.r], [dn.r])
                S.recip(dn.t[:, :], dn.t[:, :], [dn.r], [dn.r])
                S.tt("dve", cso.t[:, 256 * g:256 * g + 256].rearrange("p (r d) -> p r d", r=4), ov[:, :, 0:64],
                     dn.t[:, :].unsqueeze(2).to_broadcast([128, 4, 64]), ALU.mult, [PR[ob], dn.r], [cso.r])
                if g == 1:
                    ss = sumsq(cso.t[:, :], [cso.r], 512)
                    rs = rstd_from_ss(ss.t[:, 0:1], 512.0, eps6, [ss.r])
                    S.stt("dve", cat.t[:, tt, 512:1024], cso.t[:, :], rs.t[:, 0:1], fv["swa_out_g"].t[:, :], ALU.mult, ALU.mult,
                          [cso.r, rs.r, fv["swa_out_g"].r], [catr[tt]])

            for si in range(len(steps) + LOOK):
                if si < len(steps):
                    swa_score(si)
                if si >= LOOK:
                    swa_rest(si - LOOK)
            S.barrier()
            A.release(mS)

            mO = A.mark()
            A.top = mB_
            sgs = [tiles_all[i:i + 6] for i in range(0, len(tiles_all), 6)]
            TM = max(len(s_) for s_ in sgs) * 128
            actT = T(A, [128, 22, TM], BF16, "actT")
            wd = T(A, [128, 22, D], BF16, "w_down")
            wd_r = [Res() for _ in range(22)]
            xring_n = Ring(A, 2, [128, D], F32, "xt4n")
            xring_r = Ring(A, 2, [128, D], F32, "xt4r")
            oring4 = Ring(A, 2, [128, D], F32, "xo4")
            sil = Ring(A, 2, [128, 512], F32, "sil")
            A.top = max(A.top, mO + 40 * 1024)
            early_base = A.top
            A2 = [T(A, [128, D], F32, "A2") for _ in range(2)]
            B2 = [T(A, [128, D], F32, "B2") for _ in range(2)]
            C2 = [T(A, [128, D], F32, "C2") for _ in range(2)]
            for si, j in ((0, 2), (1, b)):
                load_bcast(A2[si], modv[l, 3, j:j + 1, :])
                load_bcast(B2[si], modv[l, 4, j:j + 1, :])
                load_bcast(C2[si], modv[l, 5, j:j + 1, :])
            hT2 = T(A, [128, 8, TM], BF16, "hTf")
            hring4 = {"tmp": Ring(A, 2, [128, D], F32, "ntmp4"), "h": Ring(A, 3, [128, D], BF16, "h4")}
            ffn_top = A.top
            A.top = mO
            C1 = [T(A, [128, D], F32, "C1") for _ in range(2)]
            for si, j in ((0, 2), (1, b)):
                load_bcast(C1[si], modv[l, 2, j:j + 1, :])
            catT = Ring(A, 2, [128, 8, 128], BF16, "catT")
            xring = Ring(A, 2, [128, D], F32, "xt3")
            oring = Ring(A, 2, [128, D], F32, "xo3")
            for ti, tt in enumerate(tiles_all):
                if debug:
                    S.dma("sp", dbg_cat[b, tt * 128:(tt + 1) * 128, :], cat.t[:, tt, :], [catr[tt]], [])
                trb = ti % 2
                pbf = PS[trb][:, :].bitcast(BF16)
                for k in range(8):
                    S.tr(pbf[:, k * 128:(k + 1) * 128], cat.t[:, tt, k * 128:(k + 1) * 128], ident.t[:, :], [catr[tt], ident.r], [PR[trb]])
                ct = catT.next()
                S.cp("act", ct.t[:, :, :], pbf[:, :].rearrange("p (k t) -> p k t", k=8), [PR[trb]], [ct.r])
                yb = (2 + 2 * (ti % 2), 3 + 2 * (ti % 2))
                for hf in range(2):
                    for k in range(8):
                        S.mm(PS[yb[hf]][:, :], ct.t[:, k, :], wo.t[:, k, hf * 512:(hf + 1) * 512], k == 0, k == 7, [ct.r, wo_r[k]], [PR[yb[hf]]])
                xt = load_x(tt, xring)
                si = 0 if tt < 2 else 1
                o1 = post_residual(yb, xt, C1[si], xs1[b, tt * 128:(tt + 1) * 128, :], r_xs1[b][tt], oring)
                if ti < len(sgs[0]):
                    h2 = norm_elem(o1, A2[si], B2[si], hring4, add_eng="pool")
                    norm_tr(h2, hT2.t, hT2.r, ti * 128, 6 + (ti % 2))
            assert A.top <= early_base
            S.barrier()
            A.top = ffn_top

            wgb, wub = w_rings()
            oring = oring4
            hring = hring4
            wgu = Wd["ffn_w_gu"][l].rearrange("(k p) n -> p k n", p=128)
            wdn = Wd["ffn_w_down"][l].rearrange("(f p) n -> p f n", p=128)

            def load_x1(tt, ring):
                xt = ring.next()
                S.dma("sp", xt.t[:, :], xs1[b, tt * 128:(tt + 1) * 128, :], [r_xs1[b][tt]], [xt.r])
                return xt

            def ffn_elem(tt):
                xt = load_x1(tt, xring_n)
                si_ = 0 if tt < 2 else 1
                return norm_elem(xt, A2[si_], B2[si_], hring, add_eng="dve")

            for sgi, sg in enumerate(sgs):
                ntok = len(sg) * 128
                nxt = sgs[sgi + 1] if sgi + 1 < len(sgs) else []
                chunks = [(c0, min(512, ntok - c0)) for c0 in range(0, ntok, 512)]
                ui = 0
                assert len(nxt) <= len(sg)
                blocks = {}

                def issue_gu(cb_):
                    wg_, wu_ = wgb.next(), wub.next()
                    S.dma("pool", wg_.t[:, :, :], wgu[:, :, cb_ * 256:(cb_ + 1) * 256], [], [wg_.r])
                    S.dma("pool", wu_.t[:, :, :], wgu[:, :, DFF + cb_ * 256:DFF + (cb_ + 1) * 256], [], [wu_.r])
                    blocks[cb_] = (wg_, wu_)

                issue_gu(0)
                for cb in range(11):
                    if cb + 1 < 11:
                        issue_gu(cb + 1)
                    wg, wu = blocks[cb]
                    for f in (2 * cb, 2 * cb + 1):
                        S.dma("pool", wd.t[:, f, :], wdn[:, f, :], [], [wd_r[f]])
                    for fs in range(2):
                        fb = cb * 2 + fs
                        for (c0, cn) in chunks:
                            bg, bu = 2 + 2 * (ui % 2), 3 + 2 * (ui % 2)
                            ui += 1
                            for k in range(8):
                                S.mm(PS[bg][:, 0:cn], wg.t[:, k, fs * 128:(fs + 1) * 128], hT2.t[:, k, c0:c0 + cn], k == 0, k == 7,
                                     [wg.r, hT2.r], [PR[bg]])
                            for k in range(8):
                                S.mm(PS[bu][:, 0:cn], wu.t[:, k, fs * 128:(fs + 1) * 128], hT2.t[:, k, c0:c0 + cn], k == 0, k == 7,
                                     [wu.r, hT2.r], [PR[bu]])
                            sl = sil.next()
                            S.act(sl.t[:, 0:cn], PS[bg][:, 0:cn], AF.Exp, [PR[bg]], [sl.r], scale=-1.0)
                            S.act(sl.t[:, 0:cn], sl.t[:, 0:cn], AF.Ln, [sl.r], [sl.r], bias=1.0)
                            S.act(sl.t[:, 0:cn], sl.t[:, 0:cn], AF.Exp, [sl.r], [sl.r], scale=-1.0)
                            S.tt("dve", sl.t[:, 0:cn], sl.t[:, 0:cn], PS[bg][:, 0:cn], ALU.mult, [sl.r, PR[bg]], [sl.r])
                            S.tt("dve", actT.t[:, fb, c0:c0 + cn], sl.t[:, 0:cn], PS[bu][:, 0:cn], ALU.mult, [sl.r, PR[bu]], [actT.r])
                if sgi == len(sgs) - 1:
                    lnext = l if b + 1 < nb else (l + 1 if l + 1 < nlayers else None)
                    if lnext is not None:
                        pre["wa"] = w_load("wa", lnext)
                hq = [ffn_elem(nxt[0])] if len(nxt) > 0 else []
                for i, tt in enumerate(sg):
                    if i + 1 < len(nxt):
                        hq.append(ffn_elem(nxt[i + 1]))
                    hn = hq[i] if i < len(nxt) else None
                    yb = (2 + 2 * (i % 2), 3 + 2 * (i % 2))
                    for hf in range(2):
                        for f in range(22):
                            S.mm(PS[yb[hf]][:, :], actT.t[:, f, i * 128:(i + 1) * 128], wd.t[:, f, hf * 512:(hf + 1) * 512], f == 0, f == 21,
                                 [actT.r, wd_r[f]], [PR[yb[hf]]])
                    if hn is not None:
                        norm_tr(hn, hT2.t, hT2.r, i * 128, i % 2)
                    xt = load_x1(tt, xring_r)
                    si = 0 if tt < 2 else 1
                    if last and tt >= 2:
                        dst_ap, dst_res = out[b, (tt - 2) * 128:(tt - 1) * 128, :], Res()
                    else:
                        dst_ap, dst_res = xs2[b, tt * 128:(tt + 1) * 128, :], r_xs2[b][tt]
                    post_residual(yb, xt, C2[si], dst_ap, dst_res, oring)
            S.barrier()
            A.release(mB_)
        S.barrier()
        A.release(mL)
    print("ops:", {e: len(v) for e, v in S.ops.items()}, "peak sbuf", A.peak)
    S.emit()
    return nc


_CACHE = {}


def _core_inputs(inp, core, nb, W, C):
    b0 = core * nb
    x = np.asarray(inp["x"], np.float32)
    ctx = np.asarray(inp["ctx"], np.float32)
    c = np.asarray(inp["c"], np.float32)
    c_ctx = np.asarray(inp["c_ctx"], np.float32)
    m = {}
    m["xin"] = np.ascontiguousarray(np.concatenate([ctx[b0:b0 + nb], x[b0:b0 + nb]], axis=1))
    cc = np.stack([c[b0 + (j % nb)] for j in range(2)] + [c_ctx], 0)
    m["ccT"] = np.ascontiguousarray(cc.reshape(3, 8, 128).transpose(2, 1, 0))
    m.update(W)
    for k, v in C.items():
        m["c_" + k] = v
    return m


def kernel(**inp):
    nb = 2
    if "nc" not in _CACHE:
        _CACHE["nc"] = build(nb=nb, nlayers=2)
    nc = _CACHE["nc"]
    W = _prep_weights(inp)
    C = _consts()
    in_maps = [_core_inputs(inp, core, nb, W, C) for core in range(NCORES)]
    res = run_bass_kernel_spmd(nc, in_maps, core_ids=list(range(NCORES)))
    outs = [np.asarray(r["out"], np.float32) for r in res.results]
    return np.concatenate(outs, axis=0)
```

```python
import numpy as np
import ml_dtypes
import concourse.bass as bass
import concourse.mybir as mybir
from concourse.bass_utils import run_bass_kernel_spmd

F32 = mybir.dt.float32
BF16 = mybir.dt.bfloat16
ALU = mybir.AluOpType
AF = mybir.ActivationFunctionType
AX = mybir.AxisListType

D = 1024
NT = 18
DFF = 2816
NCORES = 8


class Res:
    __slots__ = ("name", "w", "r")

    def __init__(self, name=""):
        self.name = name
        self.w = None
        self.r = {}


class Op:
    __slots__ = ("eng", "fn", "deps", "is_dma", "sem", "val", "needs_inc", "ring_wait")

    def __init__(self, eng, fn, is_dma):
        self.eng = eng
        self.fn = fn
        self.deps = []
        self.is_dma = is_dma
        self.sem = None
        self.val = 0
        self.needs_inc = False
        self.ring_wait = None


class Sched:
    ENGS = ("pe", "dve", "act", "pool", "sp")
    RING = 12

    def __init__(self, nc):
        self.nc = nc
        self.ops = {e: [] for e in self.ENGS}
        self.dma_count = {e: 0 for e in self.ENGS}
        self.pending_barrier = {e: [] for e in self.ENGS}
        self.all_dma_since_barrier = []
        self.n_ops = 0

    def _dep(self, op, p, kind):
        if p is None or p is op:
            return
        if (not p.is_dma) and p.eng == op.eng and not op.is_dma and p.eng == "pe":
            return
        op.deps.append(p)
        if not p.is_dma:
            p.needs_inc = True

    def op(self, eng, fn, reads=(), writes=(), dma=False):
        o = Op(eng, fn, dma)
        for b in self.pending_barrier[eng]:
            self._dep(o, b, "raw")
        self.pending_barrier[eng] = []
        for r in reads:
            self._dep(o, r.w, "raw")
        for w in writes:
            self._dep(o, w.w, "waw")
            for rd in w.r.values():
                self._dep(o, rd, "war")
        for r in reads:
            if dma:
                r.r[("dma", id(o))] = o
            else:
                r.r[eng] = o
        for w in writes:
            w.w = o
            w.r = {}
        if dma:
            i = self.dma_count[eng]
            self.dma_count[eng] = i + 1
            o.sem = (eng, i % self.RING)
            o.val = 16 * (i // self.RING + 1)
            if i >= self.RING:
                o.ring_wait = (o.sem, o.val - 16)
            self.all_dma_since_barrier.append(o)
        self.ops[eng].append(o)
        self.n_ops += 1
        return o

    def barrier(self):
        lasts = []
        for e in self.ENGS:
            for o in reversed(self.ops[e]):
                if not o.is_dma:
                    lasts.append(o)
                    break
        lasts += self.all_dma_since_barrier
        self.all_dma_since_barrier = []
        for e in self.ENGS:
            self.pending_barrier[e] = list(lasts)

    def mm(self, out, lhsT, rhs, start, stop, reads, writes, **kw):
        return self.op("pe", lambda e: e.matmul(out, lhsT, rhs, start=start, stop=stop, **kw), reads, writes)

    def tr(self, out, in_, ident, reads, writes):
        return self.op("pe", lambda e: e.transpose(out, in_, ident), reads, writes)

    def dma(self, eng, out, in_, reads, writes):
        return self.op(eng, lambda e: e.dma_start(out=out, in_=in_), reads, writes, dma=True)

    def act(self, out, in_, func, reads, writes, **kw):
        return self.op("act", lambda e: e.activation(out, in_, func, **kw), reads, writes)

    def tt(self, eng, out, in0, in1, op, reads, writes):
        return self.op(eng, lambda e: e.tensor_tensor(out, in0, in1, op), reads, writes)

    def ts(self, eng, out, in0, s1, s2, op0, op1, reads, writes):
        return self.op(eng, lambda e: e.tensor_scalar(out, in0, s1, s2, op0, op1), reads, writes)

    def stt(self, eng, out, in0, scalar, in1, op0, op1, reads, writes):
        return self.op(eng, lambda e: e.scalar_tensor_tensor(out, in0, scalar, in1, op0, op1), reads, writes)

    def cp(self, eng, out, in_, reads, writes):
        if eng == "act":
            return self.op(eng, lambda e: e.copy(out, in_), reads, writes)
        return self.op(eng, lambda e: e.tensor_copy(out, in_), reads, writes)

    def memset(self, eng, ap, val, writes):
        return self.op(eng, lambda e: e.memset(ap, val), [], writes)

    def recip(self, out, in_, reads, writes):
        return self.op("dve", lambda e: e.reciprocal(out, in_), reads, writes)

    def reduce(self, out, in_, op, reads, writes):
        return self.op("dve", lambda e: e.tensor_reduce(out, in_, AX.X, op), reads, writes)

    def emit(self):
        nc = self.nc
        from contextlib import ExitStack

        with ExitStack() as st:
            esem = {e: st.enter_context(nc.semaphore("s_" + e)) for e in self.ENGS if e != "sp"}
            rsem = {}
            for e in self.ENGS:
                for k in range(min(self.RING, self.dma_count[e])):
                    rsem[(e, k)] = st.enter_context(nc.semaphore("d_%s_%d" % (e, k)))
            for e in self.ENGS:
                c = 0
                for o in self.ops[e]:
                    if o.is_dma:
                        continue
                    if o.needs_inc:
                        c += 1
                        o.val = c
            block = st.enter_context(nc.Block())

            def run(ename, eng):
                seen = {}

                def wait(semkey, v):
                    if seen.get(semkey, 0) >= v:
                        return
                    seen[semkey] = v
                    h = rsem[semkey] if isinstance(semkey, tuple) else esem[semkey]
                    eng.wait_ge(h, v)

                for o in self.ops[ename]:
                    for p in o.deps:
                        if p.is_dma:
                            wait(p.sem, p.val)
                        else:
                            wait(p.eng, p.val)
                    if o.ring_wait is not None:
                        wait(o.ring_wait[0], o.ring_wait[1])
                    ins = o.fn(eng)
                    if o.is_dma:
                        ins.then_inc(rsem[o.sem], 16)
                    elif o.needs_inc:
                        ins.then_inc(esem[ename], 1)
                n = self.dma_count[ename]
                for k in range(min(self.RING, n)):
                    uses = (n - 1 - k) // self.RING + 1
                    wait((ename, k), 16 * uses)

            @block.tensor
            def _(eng):
                run("pe", eng)

            @block.vector
            def _(eng):
                run("dve", eng)

            @block.scalar
            def _(eng):
                run("act", eng)

            @block.gpsimd
            def _(eng):
                run("pool", eng)

            @block.sync
            def _(eng):
                run("sp", eng)


class Arena:
    def __init__(self, nc, base=16384, limit=229376):
        self.nc = nc
        self.top = base
        self.limit = limit
        self.n = 0
        self.peak = base

    def mark(self):
        return self.top

    def release(self, m):
        self.top = m

    def alloc(self, shape, dtype, name="t"):
        esz = 4 if dtype == F32 else 2
        per_part = esz * int(np.prod(shape[1:]))
        per_part = (per_part + 63) // 64 * 64
        off = self.top
        self.top += per_part
        self.peak = max(self.peak, self.top)
        assert self.top <= self.limit, "SBUF arena overflow %d (%s)" % (self.top, name)
        self.n += 1
        return self.nc.alloc_sbuf_tensor_at("%s_%d" % (name, self.n), list(shape), dtype, offset=off)


class T:
    def __init__(self, A, shape, dtype, name="t"):
        self.t = A.alloc(shape, dtype, name)
        self.r = Res(name)


class Ring:
    def __init__(self, A, n, shape, dtype, name="r"):
        self.items = [T(A, shape, dtype, name) for _ in range(n)]
        self.i = 0

    def next(self):
        x = self.items[self.i % len(self.items)]
        self.i += 1
        return x


def _consts():
    c = {}
    c["ident"] = np.eye(128, dtype=np.float32)
    s = np.arange(128)
    same = (s[:, None] // 64) == (s[None, :] // 64)
    le = s[:, None] <= s[None, :]
    ge = s[:, None] >= s[None, :]
    mF = (same & le).astype(np.float32)
    mB = (same & ge).astype(np.float32)
    c["mF"] = mF
    c["mB"] = mB
    c["triF"] = mF / 16.0
    c["triB"] = mB / 16.0
    c["uF"] = (same & (s[:, None] > s[None, :])).astype(np.float32) / 16.0
    c["uB"] = (same & (s[:, None] < s[None, :])).astype(np.float32) / 16.0
    c["mP"] = ge.astype(np.float32)
    c["mN"] = le.astype(np.float32)
    p = np.arange(128)
    c["bm"] = (p[:, None] // 32 == np.arange(4)[None, :]).astype(np.float32)
    c["smask"] = np.repeat(c["bm"], 64, axis=1).astype(np.float32)
    rows = 2048 // 64
    row = np.repeat(np.arange(rows), 64).astype(np.float32)
    col = np.tile(np.arange(64), rows).astype(np.float32)
    inv_freq = np.power(np.float32(10000.0), -np.arange(0, 32, 2, dtype=np.float32) / np.float32(32)).astype(np.float32)
    ang_row = (row[:, None] * inv_freq[None, :]).astype(np.float32)
    ang_col = (col[:, None] * inv_freq[None, :]).astype(np.float32)
    cosT = np.zeros((64, 2048), np.float32)
    sinT = np.zeros((64, 2048), np.float32)
    for d in range(64):
        j = d % 16
        ang = ang_row if d < 32 else ang_col
        half = (d % 32) // 16
        cosT[d] = np.cos(ang[:, j])
        sn = np.sin(ang[:, j])
        sinT[d] = -sn if half == 0 else sn
    c["cosT"] = np.concatenate([cosT, cosT], 0)
    c["sinT"] = np.concatenate([sinT, sinT], 0)
    return c


def _partner(d):
    return d + 16 if (d % 32) < 16 else d - 16


def _prep_weights(inp):
    w = {}
    w_in = np.asarray(inp["w_in"], np.float32)
    w["w_in_a"] = np.ascontiguousarray(w_in[:, :, 0:1312])
    qoff, koff, voff = 1312, 1824, 1952
    qcols, qpcols = [], []
    for j in range(4):
        for hd in (j, 4 + j):
            for d in range(64):
                qcols.append(qoff + hd * 64 + d)
                qpcols.append(qoff + hd * 64 + _partner(d))
    kcols = [koff + g * 64 + d for g in range(2) for d in range(64)]
    kpcols = [koff + g * 64 + _partner(d) for g in range(2) for d in range(64)]
    vcols = list(range(voff, voff + 128))
    cols = qcols + kcols + vcols + qpcols + kpcols
    w["w_in_b"] = np.ascontiguousarray(w_in[:, :, cols])
    w["wsT"] = np.ascontiguousarray(np.transpose(np.asarray(inp["gmlp_ws"], np.float32), (0, 1, 3, 2)))
    w["bsT"] = np.ascontiguousarray(np.transpose(np.asarray(inp["gmlp_bs"], np.float32), (0, 2, 1)))
    w["gnorm4"] = np.ascontiguousarray(np.tile(np.asarray(inp["gla_norm"], np.float32), (1, 4)))
    for k in ("mod_w", "mod_b", "n1_pre", "n1_post", "n2_pre", "n2_post", "w_out", "gla_wa2", "gla_ba",
              "gmlp_ln_g", "gmlp_ln_b", "gmlp_out_g", "swa_sink", "swa_out_g", "ffn_w_gu", "ffn_w_down"):
        w[k] = np.ascontiguousarray(np.asarray(inp[k], np.float32))
    return w


def build(nb=2, nlayers=2, debug=False):
    nc = bass.Bass("TRN2", target_bir_lowering=False)
    S = Sched(nc)
    WBASE = 229376 - 24576 - 512
    A = Arena(nc, limit=WBASE)
    C = _consts()

    def din(name, shape, dt=F32):
        return nc.dram_tensor(name, list(shape), dt, kind="ExternalInput").ap()

    kind_dbg = "ExternalOutput" if debug else "Internal"
    xin = din("xin", [nb, NT * 128, D])
    ccT = din("ccT", [128, 8, 3])
    Wd = {}
    shapes = {
        "mod_w": [2, D, 6 * D], "mod_b": [2, 6 * D], "n1_pre": [2, D], "n1_post": [2, D], "n2_pre": [2, D],
        "n2_post": [2, D], "w_in_a": [2, D, 1312], "w_in_b": [2, D, 1408], "w_out": [2, D, D],
        "gla_wa2": [2, 2, 16, 128], "gla_ba": [2, 2, 128], "gnorm4": [2, 256], "gmlp_ln_g": [2, 256],
        "gmlp_ln_b": [2, 256], "wsT": [2, 4, 128, 128], "bsT": [2, 128, 4], "gmlp_out_g": [2, 256],
        "swa_sink": [2, 8], "swa_out_g": [2, 512], "ffn_w_gu": [2, D, 2 * DFF], "ffn_w_down": [2, DFF, D],
    }
    for k, shp in shapes.items():
        Wd[k] = din(k, shp)
    Cd = {k: din("c_" + k, list(v.shape)) for k, v in C.items()}
    out = nc.dram_tensor("out", [nb, 2048, D], F32, kind="ExternalOutput").ap()
    xs1 = nc.dram_tensor("xs1", [nb, NT * 128, D], F32, kind=kind_dbg).ap()
    xs2 = nc.dram_tensor("xs2", [nb, NT * 128, D], F32, kind=kind_dbg).ap()
    modv = nc.dram_tensor("modv", [2, 6, 3, D], F32, kind=kind_dbg).ap()
    dbg_cat = nc.dram_tensor("dbg_cat", [nb, NT * 128, D], BF16, kind="ExternalOutput").ap() if debug else None
    r_xs1 = [[Res() for _ in range(NT)] for _ in range(nb)]
    r_xs2 = [[Res() for _ in range(NT)] for _ in range(nb)]
    r_modv = Res()
    hts = nc.dram_tensor("hts", [nb, 5, 128, 8, 512], BF16, kind="Internal").ap()
    r_hts = [[Res() for _ in range(5)] for _ in range(nb)]

    PS = [nc.alloc_psum_tensor("ps%d" % i, [128, 512], F32) for i in range(8)]
    PR = [Res("ps%d" % i) for i in range(8)]

    ident = T(A, [128, 128], BF16, "ident")
    S.dma("pool", ident.t[:, :], Cd["ident"][:, :], [], [ident.r])
    cf = {}
    for k in ("mF", "mB", "triF", "triB", "uF", "uB"):
        cf[k] = T(A, [128, 128], F32, k)
        S.dma("sp", cf[k].t[:, :], Cd[k][:, :], [], [cf[k].r])
    for k in ("mP", "mN"):
        cf[k] = T(A, [128, 128], BF16, k)
        S.dma("pool", cf[k].t[:, :], Cd[k][:, :], [], [cf[k].r])
    bm = T(A, [128, 4], BF16, "bm")
    S.dma("pool", bm.t[:, :], Cd["bm"][:, :], [], [bm.r])
    bmf = T(A, [128, 4], F32, "bmf")
    S.dma("sp", bmf.t[:, :], Cd["bm"][:, :], [], [bmf.r])
    smask = T(A, [128, 256], F32, "smask")
    S.dma("sp", smask.t[:, :], Cd["smask"][:, :], [], [smask.r])
    junk = T(A, [128, 1024], F32, "junk")
    eps6 = 1e-6
    epsT = {}
    for ev in (1e-6, 1e-5):
        epsT[ev] = T(A, [128, 1], F32, "eps")
        S.memset("dve", epsT[ev].t[:, :], ev, [epsT[ev].r])

    stat = Ring(A, 24, [128, 8], F32, "stat")

    def rstd_from_ss(ss, n, eps, reads):
        k = ss.shape[1]
        a = stat.next()
        S.act(a.t[:, 0:k], ss, AF.Ln, list(reads) + [epsT[eps].r], [a.r], scale=1.0 / n, bias=epsT[eps].t[:, 0:1])
        c_ = stat.next()
        S.act(c_.t[:, 0:k], a.t[:, 0:k], AF.Exp, [a.r], [c_.r], scale=-0.5)
        return c_

    def sumsq(in_ap, reads, n_free):
        a = stat.next()
        S.act(junk.t[:, 0:n_free], in_ap, AF.Square, reads, [junk.r, a.r], accum_out=a.t[:, 0:1])
        return a

    def load_bcast(dst, src_row):
        S.dma("sp", dst.t[:, :], src_row.partition_broadcast(128), [r_modv], [dst.r])

    m0 = A.mark()
    cc32 = T(A, [128, 8, 3], F32, "cc32")
    S.dma("sp", cc32.t[:, :, :], ccT[:, :, :], [], [cc32.r])
    scT = T(A, [128, 8, 3], BF16, "scT")
    S.act(scT.t[:, :, :], cc32.t[:, :, :], AF.Silu, [cc32.r], [scT.r])
    modraw = T(A, [3, 6 * D], F32, "modraw")
    biasT = T(A, [3, 6 * D], F32, "biasT")
    nrm3 = {k: T(A, [3, D], F32, k) for k in ("n1_pre", "n1_post", "n2_pre", "n2_post")}
    mwb = Ring(A, 2, [128, 8, 512], BF16, "mwb")
    mtmp = Ring(A, 2, [3, D], F32, "mtmp")
    for l in range(nlayers):
        S.dma("sp", biasT.t[:, :], Wd["mod_b"][l:l + 1, :].partition_broadcast(3), [], [biasT.r])
        for k in nrm3:
            S.dma("sp", nrm3[k].t[:, :], Wd[k][l:l + 1, :].partition_broadcast(3), [], [nrm3[k].r])
        mw = Wd["mod_w"][l].rearrange("(k p) n -> p k n", p=128)
        for blk in range(12):
            wb = mwb.next()
            S.dma("pool", wb.t[:, :, :], mw[:, :, blk * 512:(blk + 1) * 512], [], [wb.r])
            pb = blk % 2
            for k in range(8):
                S.mm(PS[pb][0:3, :], scT.t[:, k, :], wb.t[:, k, :], k == 0, k == 7, [scT.r, wb.r], [PR[pb]])
            S.tt("dve", modraw.t[:, blk * 512:(blk + 1) * 512], PS[pb][0:3, :], biasT.t[:, blk * 512:(blk + 1) * 512],
                 ALU.add, [PR[pb], biasT.r], [modraw.r])
        combos = [(1, "n1_pre", "a"), (0, None, "b"), (2, "n1_post", "c"), (4, "n2_pre", "a"), (3, None, "b"), (5, "n2_post", "c")]
        for w_, (mi, nk, kind) in enumerate(combos):
            src = modraw.t[:, mi * D:(mi + 1) * D]
            if kind == "b":
                S.dma("sp", modv[l, w_, :, :], src, [modraw.r], [r_modv])
                continue
            tmp = mtmp.next()
            if kind == "a":
                S.stt("dve", tmp.t[:, :], src, 1.0, nrm3[nk].t[:, :], ALU.add, ALU.mult, [modraw.r, nrm3[nk].r], [tmp.r])
            else:
                S.tt("dve", tmp.t[:, :], src, nrm3[nk].t[:, :], ALU.mult, [modraw.r, nrm3[nk].r], [tmp.r])
            S.dma("sp", modv[l, w_, :, :], tmp.t[:, :], [tmp.r], [r_modv])
    S.barrier()
    A.release(m0)

    def norm_elem(xt, Am, Bm, hring, add_eng="pool"):
        ss = sumsq(xt.t[:, :], [xt.r], 1024)
        rs = rstd_from_ss(ss.t[:, 0:1], 1024.0, eps6, [ss.r])
        tmp = hring["tmp"].next()
        S.stt("dve", tmp.t[:, :], xt.t[:, :], rs.t[:, 0:1], Am.t[:, :], ALU.mult, ALU.mult, [xt.r, rs.r, Am.r], [tmp.r])
        h = hring["h"].next()
        S.tt(add_eng, h.t[:, :], tmp.t[:, :], Bm.t[:, :], ALU.add, [tmp.r, Bm.r], [h.r])
        return h

    def norm_tr(h, hT, hT_res, col0, trbank):
        pbf = PS[trbank][:, :].bitcast(BF16)
        for k in range(8):
            S.tr(pbf[:, k * 128:(k + 1) * 128], h.t[:, k * 128:(k + 1) * 128], ident.t[:, :], [h.r, ident.r], [PR[trbank]])
        S.cp("dve", hT[:, :, col0:col0 + 128], pbf[:, :].rearrange("p (k t) -> p k t", k=8), [PR[trbank]], [hT_res])

    def norm_mod_T(xt, Am, Bm, hT, col0, hring, trbank):
        h = norm_elem(xt, Am, Bm, hring)
        norm_tr(h, hT.t, hT.r, col0, trbank)

    def post_residual(y_banks, xt, Cm, dst_ap, dst_res, oring):
        s0 = sumsq(PS[y_banks[0]][:, :], [PR[y_banks[0]]], 512)
        s1 = sumsq(PS[y_banks[1]][:, :], [PR[y_banks[1]]], 512)
        st = stat.next()
        S.tt("dve", st.t[:, 0:1], s0.t[:, 0:1], s1.t[:, 0:1], ALU.add, [s0.r, s1.r], [st.r])
        rs = rstd_from_ss(st.t[:, 0:1], 1024.0, eps6, [st.r])
        o = oring.next()
        for hf in range(2):
            S.stt("dve", o.t[:, hf * 512:(hf + 1) * 512], PS[y_banks[hf]][:, :], rs.t[:, 0:1], Cm.t[:, hf * 512:(hf + 1) * 512],
                  ALU.mult, ALU.mult, [PR[y_banks[hf]], rs.r, Cm.r], [o.r])
        S.tt("dve", o.t[:, :], o.t[:, :], xt.t[:, :], ALU.add, [o.r, xt.r], [o.r])
        S.dma("sp", dst_ap, o.t[:, :], [o.r], [dst_res])
        return o

    class TX:
        def __init__(self, t, name="w"):
            self.t = t
            self.r = Res(name)

    wcur = {"res": []}
    wmark = T(A, [128, 1], F32, "wmark")
    wcount = [0]
    pre = {"wa": None}

    def w_marker():
        if wcur["res"]:
            S.memset("pool", wmark.t[:, :], 0.0, [wmark.r] + list(wcur["res"]))

    def w_load(kind, l_):
        w_marker()
        wcount[0] += 1
        ncols = {"wa": 1312, "wb": 1408, "wo": D}[kind]
        src = {"wa": Wd["w_in_a"], "wb": Wd["w_in_b"], "wo": Wd["w_out"]}[kind][l_].rearrange("(k p) n -> p k n", p=128)
        t = nc.alloc_sbuf_tensor_at("W%s_%d" % (kind, wcount[0]), [128, 8, ncols], BF16, offset=WBASE)
        rs = [Res() for _ in range(8)]
        for k in range(8):
            S.dma("pool", t[:, k, :], src[:, k, :], [], [rs[k]])
        wcur["res"] = rs
        return TX(t, kind), rs

    def w_rings():
        w_marker()
        wcount[0] += 1
        items = [TX(nc.alloc_sbuf_tensor_at("Wr%d_%d" % (i, wcount[0]), [128, 8, 256], BF16, offset=WBASE + i * 4096)) for i in range(6)]
        wcur["res"] = [x.r for x in items]
        rg, ru = Ring.__new__(Ring), Ring.__new__(Ring)
        rg.items, rg.i = items[0:3], 0
        ru.items, ru.i = items[3:6], 0
        return rg, ru

    for l in range(nlayers):
        need_ctx = l < 1
        last = l == nlayers - 1
        xsrc, r_xsrc = (xin, None) if l == 0 else (xs2, r_xs2)
        tiles_all = list(range(NT)) if need_ctx else list(range(2, NT))
        mL = A.mark()
        W2 = T(A, [33, 256], F32, "W2")
        S.memset("pool", W2.t[:, :], 0.0, [W2.r])
        S.dma("sp", W2.t[0:16, 0:128], Wd["gla_wa2"][l, 0, :, :], [], [W2.r])
        S.dma("sp", W2.t[16:32, 128:256], Wd["gla_wa2"][l, 1, :, :], [W2.r], [W2.r])
        S.dma("sp", W2.t[32:33, 0:128], Wd["gla_ba"][l, 0:1, :], [W2.r], [W2.r])
        S.dma("sp", W2.t[32:33, 128:256], Wd["gla_ba"][l, 1:2, :], [W2.r], [W2.r])
        wsT = T(A, [128, 4, 128], BF16, "wsT")
        S.dma("pool", wsT.t[:, :, :], Wd["wsT"][l].rearrange("g q p -> q g p"), [], [wsT.r])
        bsT = T(A, [128, 4], F32, "bsT")
        S.dma("sp", bsT.t[:, :], Wd["bsT"][l, :, :], [], [bsT.r])
        fv = {}
        for k, n in (("gnorm4", 256), ("gmlp_ln_g", 256), ("gmlp_ln_b", 256), ("gmlp_out_g", 256), ("swa_out_g", 512)):
            fv[k] = T(A, [128, n], F32, k)
            S.dma("sp", fv[k].t[:, :], Wd[k][l:l + 1, :].partition_broadcast(128), [], [fv[k].r])
        esink = T(A, [128, 8], F32, "esink")
        S.dma("sp", esink.t[:, :], Wd["swa_sink"][l:l + 1, :].partition_broadcast(128), [], [esink.r])
        S.act(esink.t[:, :], esink.t[:, :], AF.Exp, [esink.r], [esink.r])

        for b in range(nb):
            mB_ = A.mark()
            cat = T(A, [128, NT, D], BF16, "cat")
            catr = [Res() for _ in range(NT)]
            A1 = [T(A, [128, D], F32, "A1") for _ in range(2)]
            B1 = [T(A, [128, D], F32, "B1") for _ in range(2)]
            for si, j in ((0, 2), (1, b)):
                load_bcast(A1[si], modv[l, 0, j:j + 1, :])
                load_bcast(B1[si], modv[l, 1, j:j + 1, :])
            groups = [[0, 1], [2, 3, 4, 5], [6, 7, 8, 9], [10, 11, 12, 13], [14, 15, 16, 17]]

            def load_x(tt, xring):
                xt = xring.next()
                rd = [] if r_xsrc is None else [r_xsrc[b][tt]]
                S.dma("sp", xt.t[:, :], xsrc[b, tt * 128:(tt + 1) * 128, :], rd, [xt.r])
                return xt

            mG = A.mark()
            qst = T(A, [128, 2, NT * 128], BF16, "qst")
            kst = T(A, [128, 2, NT * 128], BF16, "kst")
            kpst = T(A, [128, NT, 2, 128], BF16, "kpst")
            vst = T(A, [128, NT, 256], BF16, "vst")
            sog = T(A, [128, NT, 256], BF16, "sog")
            dst_ = T(A, [128, 2, 2 * NT], F32, "dst")
            r_q = [Res() for _ in range(NT)]
            r_k = [Res() for _ in range(NT)]
            r_kp = [Res() for _ in range(NT)]
            r_v = [Res() for _ in range(NT)]
            r_sog = [Res() for _ in range(NT)]
            r_d = [Res() for _ in range(NT)]
            mP1 = A.mark()
            if pre["wa"] is not None:
                wa, wa_r = pre["wa"]
                pre["wa"] = None
            else:
                wa, wa_r = w_load("wa", l)
            hTs = [T(A, [128, 8, 512], BF16, "hT") for _ in range(2)]
            xring = Ring(A, 2, [128, D], F32, "xt")
            hring = {"tmp": Ring(A, 1, [128, D], F32, "ntmp"), "h": Ring(A, 5, [128, D], BF16, "h")}
            codes = T(A, [33, 512], F32, "codes")
            S.memset("pool", codes.t[32:33, :], 1.0, [codes.r])
            R2 = lambda shp, dt, nm: Ring(A, 2, shp, dt, nm)
            e_sbR, spR, e1R, e2R, erR = (R2([128, 256], F32, n_) for n_ in ("e_sb", "sp", "e1", "e2", "er"))
            zfR = R2([128, 512], F32, "zf")
            vnR = R2([128, 256], F32, "vn")
            vgR = R2([128, 256], BF16, "vg")
            goutR = R2([128, 256], F32, "gout")
            bnstR = R2([128, 6], F32, "bnst")
            bnagR = R2([128, 2], F32, "bnag")
            ktokR = R2([128, 128], F32, "ktok")
            sgR = R2([128, 256], F32, "sgate")
            cq = 32.0 ** -0.5

            def p1_elem(grp_):
                hs_ = []
                for tt_ in grp_:
                    xt_ = load_x(tt_, xring)
                    si_ = 0 if tt_ < 2 else 1
                    hs_.append(norm_elem(xt_, A1[si_], B1[si_], hring))
                return hs_

            def p1_tr(hs_, hT_, gi_):
                for i_, h_ in enumerate(hs_):
                    norm_tr(h_, hT_.t, hT_.r, i_ * 128, 0)
                n_ = len(hs_) * 128
                S.dma("sp", hts[b, gi_, :, :, 0:n_], hT_.t[:, :, 0:n_], [hT_.r], [r_hts[b][gi_]])

            def stage_a(hT, i, tt):
                cs = slice(i * 128, (i + 1) * 128)
                for (bank, o0, c0, m) in ((4, 0, 128, 384), (5, 0, 512, 256), (6, 0, 800, 512)):
                    for k in range(8):
                        S.mm(PS[bank][:, o0:o0 + m], hT.t[:, k, cs], wa.t[:, k, c0:c0 + m], k == 0, k == 7,
                             [hT.r, wa_r[k]], [PR[bank]])
                S.mm(PS[5][:, 256:512], codes.t[0:33, cs], W2.t[0:33, :], True, True, [codes.r, W2.r], [PR[5]])
                st_ = {}
                e_sb, sp_, zf, ktok = e_sbR.next(), spR.next(), zfR.next(), ktokR.next()
                S.act(e_sb.t[:, :], PS[5][:, 256:512], AF.Exp, [PR[5]], [e_sb.r], scale=-1.0)
                S.act(sp_.t[:, :], e_sb.t[:, :], AF.Ln, [e_sb.r], [sp_.r], bias=1.0)
                S.cp("dve", ktok.t[:, :], PS[4][:, 0:128], [PR[4]], [ktok.r])
                S.cp("dve", vst.t[:, tt, :], PS[4][:, 128:384], [PR[4]], [r_v[tt]])
                sg_ = sgR.next()
                S.act(sg_.t[:, :], PS[5][:, 0:256], AF.Exp, [PR[5]], [sg_.r], scale=-1.0)
                S.act(sg_.t[:, :], sg_.t[:, :], AF.Ln, [sg_.r], [sg_.r], bias=1.0)
                S.act(sg_.t[:, :], sg_.t[:, :], AF.Exp, [sg_.r], [sg_.r], scale=-1.0)
                S.tt("dve", sog.t[:, tt, :], PS[5][:, 0:256], sg_.t[:, :], ALU.mult, [PR[5], sg_.r], [r_sog[tt]])
                S.act(zf.t[:, :], PS[6][:, :], AF.Gelu, [PR[6]], [zf.r])
                bnst, bnag, vn, vg = bnstR.next(), bnagR.next(), vnR.next(), vgR.next()
                S.op("dve", (lambda a, b_: (lambda e: e.bn_stats(a, b_)))(bnst.t[:, :], zf.t[:, 256:512]), [zf.r], [bnst.r])
                S.op("dve", (lambda a, b_: (lambda e: e.bn_aggr(a, b_)))(bnag.t[:, :], bnst.t[:, :]), [bnst.r], [bnag.r])
                rs = rstd_from_ss(bnag.t[:, 1:2], 1.0, 1e-5, [bnag.r])
                S.ts("dve", vn.t[:, :], zf.t[:, 256:512], bnag.t[:, 0:1], rs.t[:, 0:1], ALU.subtract, ALU.mult,
                     [zf.r, bnag.r, rs.r], [vn.r])
                S.tt("pool", vn.t[:, :], vn.t[:, :], fv["gmlp_ln_g"].t[:, :], ALU.mult, [vn.r, fv["gmlp_ln_g"].r], [vn.r])
                S.tt("pool", vg.t[:, :], vn.t[:, :], fv["gmlp_ln_b"].t[:, :], ALU.add, [vn.r, fv["gmlp_ln_b"].r], [vg.r])
                return dict(sp=sp_, zf=zf, ktok=ktok, vg=vg, cs=cs, tt=tt)

            def stage_b(st_):
                sp_, zf, ktok, vg, cs, tt = st_["sp"], st_["zf"], st_["ktok"], st_["vg"], st_["cs"], st_["tt"]
                tk = slice(tt * 128, (tt + 1) * 128)
                S.mm(PS[7][:, 0:128], sp_.t[:, 0:128], cf["triF"].t[:, :], True, True, [sp_.r, cf["triF"].r], [PR[7]])
                S.mm(PS[7][:, 128:256], sp_.t[:, 128:256], cf["triB"].t[:, :], True, True, [sp_.r, cf["triB"].r], [PR[7]])
                S.mm(PS[7][:, 256:384], cf["uF"].t[:, :], sp_.t[:, 0:128], True, True, [sp_.r, cf["uF"].r], [PR[7]])
                S.mm(PS[7][:, 384:512], cf["uB"].t[:, :], sp_.t[:, 128:256], True, True, [sp_.r, cf["uB"].r], [PR[7]])
                for g in range(4):
                    S.mm(PS[3][:, g * 64:(g + 1) * 64], wsT.t[:, g, :], vg.t[:, g * 64:(g + 1) * 64], True, True,
                         [wsT.r, vg.r], [PR[3]])
                e1, e2, er = e1R.next(), e2R.next(), erR.next()
                S.act(e1.t[:, :], PS[7][:, 0:256], AF.Exp, [PR[7]], [e1.r], scale=-1.0)
                S.act(e2.t[:, :], PS[7][:, 0:256], AF.Exp, [PR[7]], [e2.r])
                S.act(er.t[:, :], PS[7][:, 256:512], AF.Exp, [PR[7]], [er.r], scale=-1.0)
                e1v = e1.t[:, :].rearrange("p (d t) -> p d t", d=2)
                e2v = e2.t[:, :].rearrange("p (d t) -> p d t", d=2)
                erv = er.t[:, :].rearrange("p (d t) -> p d t", d=2)
                S.stt("dve", qst.t[:, :, tk], e1v, cq, PS[1][:, cs].unsqueeze(1).to_broadcast([128, 2, 128]),
                      ALU.mult, ALU.mult, [e1.r, PR[1]], [r_q[tt]])
                S.tt("dve", kst.t[:, :, tk], e2v, PS[2][:, cs].unsqueeze(1).to_broadcast([128, 2, 128]), ALU.mult,
                     [e2.r, PR[2]], [r_k[tt]])
                S.tt("dve", kpst.t[:, tt, :, :], erv, ktok.t[:, :].unsqueeze(1).to_broadcast([128, 2, 128]), ALU.mult,
                     [er.r, ktok.r], [r_kp[tt]])
                S.cp("dve", dst_.t[:, 0, 2 * tt:2 * tt + 2], e1.t[:, 63:128:64], [e1.r], [r_d[tt]])
                S.cp("dve", dst_.t[:, 1, 2 * tt:2 * tt + 2], e1.t[:, 128:256:64], [e1.r], [r_d[tt]])
                gout = goutR.next()
                S.tt("dve", gout.t[:, :].rearrange("p (g c) -> p g c", g=4), PS[3][:, 0:256].rearrange("p (g c) -> p g c", g=4),
                     bsT.t[:, :].unsqueeze(2).to_broadcast([128, 4, 64]), ALU.add, [PR[3], bsT.r], [gout.r])
                S.tt("dve", gout.t[:, :], gout.t[:, :], zf.t[:, 0:256], ALU.mult, [gout.r, zf.r], [gout.r])
                ss = sumsq(gout.t[:, :], [gout.r], 256)
                rs2 = rstd_from_ss(ss.t[:, 0:1], 256.0, eps6, [ss.r])
                S.stt("dve", cat.t[:, tt, 256:512], gout.t[:, :], rs2.t[:, 0:1], fv["gmlp_out_g"].t[:, :], ALU.mult, ALU.mult,
                      [gout.r, rs2.r, fv["gmlp_out_g"].r], [catr[tt]])

            p1_tr(p1_elem(groups[0]), hTs[0], 0)
            for gi, grp in enumerate(groups):
                n = len(grp) * 128
                hT = hTs[gi % 2]
                hs_next = p1_elem(groups[gi + 1]) if gi + 1 < len(groups) else None
                for (bank, c0, m) in ((3, 768, 32), (1, 0, 128), (2, 128, 128)):
                    for k in range(8):
                        S.mm(PS[bank][0:m, 0:n], wa.t[:, k, c0:c0 + m], hT.t[:, k, 0:n], k == 0, k == 7,
                             [wa_r[k], hT.r], [PR[bank]])
                    if bank == 3:
                        S.cp("act", codes.t[0:32, 0:n], PS[3][0:32, 0:n], [PR[3]], [codes.r])
                pend = None
                for i, tt in enumerate(grp):
                    st_ = stage_a(hT, i, tt)
                    if i == 0 and hs_next is not None:
                        p1_tr(hs_next, hTs[(gi + 1) % 2], gi + 1)
                    if pend is not None:
                        stage_b(pend)
                    pend = st_
                stage_b(pend)
            S.barrier()
            A.release(mP1)

            wb_, wb_r = w_load("wb", l)
            ofs = T(A, [128, NT, 256], F32, "ofs")
            ofs_r = [Res() for _ in range(NT)]
            osqR = Ring(A, 2, [128, 256], F32, "osq")
            osnR = Ring(A, 2, [128, 256], F32, "osn")
            arrived = [False] * NT
            dirs = []
            for dr in range(2):
                dd = dict(dr=dr, Sst=Ring(A, 2, [128, 256], F32, "Sst"), Sbd=Ring(A, 3, [128, 256], BF16, "Sbd"),
                          Qbd=Ring(A, 2, [128, 4, 128], BF16, "Qbd"), att=Ring(A, 2, [128, 4, 128], BF16, "att"),
                          order=list(range(NT)) if dr == 0 else [1, 0] + list(range(NT - 1, 1, -1)),
                          mk=cf["mF"] if dr == 0 else cf["mB"], kvb=(0, 1) if dr == 0 else (4, 5), ab=2 + dr, ob=6 + dr)
                dd["Scur"] = dd["Sst"].next()
                S.memset("dve", dd["Scur"].t[:, :], 0.0, [dd["Scur"].r])
                dd["Sb0"] = dd["Sbd"].next()
                S.memset("dve", dd["Sb0"].t[:, :], 0.0, [dd["Sb0"].r])
                dirs.append(dd)

            def gla_step(dd, tt):
                dr = dd["dr"]
                tk0 = tt * 128
                chunks = (0, 1) if dr == 0 else (1, 0)
                need_out = need_ctx or tt >= 2
                obank, abank = dd["ob"], dd["ab"]
                Sbs = [dd["Sb0"]]
                if need_out:
                    qb = dd["Qbd"].next()
                    for h in range(4):
                        S.act(qb.t[:, h, :], qst.t[:, dr, tk0:tk0 + 128], AF.Copy, [r_q[tt], bmf.r], [qb.r], scale=bmf.t[:, h:h + 1])
                for ci, c in enumerate(chunks):
                    rows = slice(c * 64, (c + 1) * 64)
                    kvbank = dd["kvb"][ci]
                    S.mm(PS[kvbank][:, 0:256], kpst.t[rows, tt, dr, :], vst.t[rows, tt, :], True, True,
                         [r_kp[tt], r_v[tt]], [PR[kvbank]])
                if need_out:
                    S.mm(PS[abank][:, :], kst.t[:, dr, tk0:tk0 + 128], qb.t[:, :, :].rearrange("p h t -> p (h t)"), True, True,
                         [r_k[tt], qb.r], [PR[abank]])
                for ci, c in enumerate(chunks):
                    kvbank = dd["kvb"][ci]
                    Sn = dd["Sst"].next()
                    ch = 2 * tt + c
                    S.stt("dve", Sn.t[:, :], dd["Scur"].t[:, :], dst_.t[:, dr, ch:ch + 1], PS[kvbank][:, 0:256], ALU.mult, ALU.add,
                          [dd["Scur"].r, r_d[tt], PR[kvbank]], [Sn.r])
                    dd["Scur"] = Sn
                    Sbn = dd["Sbd"].next()
                    S.tt("pool", Sbn.t[:, :], Sn.t[:, :], smask.t[:, :], ALU.mult, [Sn.r, smask.r], [Sbn.r])
                    Sbs.append(Sbn)
                dd["Sb0"] = Sbs[2]
                if not need_out:
                    return
                at = dd["att"].next()
                S.tt("dve", at.t[:, :, :], PS[abank][:, :].rearrange("p (h t) -> p h t", h=4),
                     dd["mk"].t[:, :].unsqueeze(1).to_broadcast([128, 4, 128]), ALU.mult, [PR[abank], dd["mk"].r], [at.r])
                for cj, c2 in enumerate(chunks):
                    r2 = slice(c2 * 64, (c2 + 1) * 64)
                    kw = {"tile_position": (0, 64)} if c2 == 1 else {}
                    S.mm(PS[obank][r2, 0:256], qst.t[:, dr, tk0 + c2 * 64:tk0 + (c2 + 1) * 64], Sbs[cj].t[:, :],
                         True, False, [r_q[tt], Sbs[cj].r], [PR[obank]], skip_group_check=True, **kw)
                for h in range(4):
                    S.mm(PS[obank][:, h * 64:(h + 1) * 64], at.t[:, h, :], vst.t[:, tt, h * 64:(h + 1) * 64], False, h == 3,
                         [at.r, r_v[tt]], [PR[obank]], skip_group_check=True)
                if not arrived[tt]:
                    arrived[tt] = True
                    S.cp("act", ofs.t[:, tt, :], PS[obank][:, 0:256], [PR[obank]], [ofs_r[tt]])
                else:
                    S.tt("dve", ofs.t[:, tt, :], PS[obank][:, 0:256], ofs.t[:, tt, :], ALU.add, [PR[obank], ofs_r[tt]], [ofs_r[tt]])

            for stp in range(NT):
                for dd in dirs:
                    gla_step(dd, dd["order"][stp])
            for tt in tiles_all:
                osq, osn = osqR.next(), osnR.next()
                S.tt("pool", osq.t[:, :], ofs.t[:, tt, :], ofs.t[:, tt, :], ALU.mult, [ofs_r[tt]], [osq.r])
                s4 = stat.next()
                S.reduce(s4.t[:, 0:4], osq.t[:, :].rearrange("p (h d) -> p h d", h=4), ALU.add, [osq.r], [s4.r])
                r4 = rstd_from_ss(s4.t[:, 0:4], 64.0, eps6, [s4.r])
                S.tt("dve", osn.t[:, :].rearrange("p (h d) -> p h d", h=4), ofs.t[:, tt, :].rearrange("p (h d) -> p h d", h=4),
                     r4.t[:, 0:4].unsqueeze(2).to_broadcast([128, 4, 64]), ALU.mult, [ofs_r[tt], r4.r], [osn.r])
                S.tt("pool", osn.t[:, :], osn.t[:, :], fv["gnorm4"].t[:, :], ALU.mult, [osn.r, fv["gnorm4"].r], [osn.r])
                S.tt("dve", cat.t[:, tt, 0:256], osn.t[:, :], sog.t[:, tt, :], ALU.mult, [osn.r, r_sog[tt]], [catr[tt]])
            S.barrier()
            A.release(mG)

            mS = A.mark()
            cosT = T(A, [128, 2048], F32, "cosT")
            sinT = T(A, [128, 2048], F32, "sinT")
            S.dma("sp", cosT.t[:, :], Cd["cosT"][:, :], [], [cosT.r])
            S.dma("sp", sinT.t[:, :], Cd["sinT"][:, :], [], [sinT.r])
            qs = T(A, [128, 4, 2048], BF16, "qs")
            qc = T(A, [128, 4, 256], BF16, "qc")
            ks = T(A, [128, NT * 128], BF16, "ks")
            va = T(A, [128, NT, 2, 65], BF16, "va")
            S.memset("pool", va.t[:, :, :, :].rearrange("p a b c -> p (a b c)"), 1.0, [va.r])
            grp_r = [Res() for _ in range(5)]
            mP2 = A.mark()
            hTs = [T(A, [128, 8, 512], BF16, "hT2") for _ in range(2)]
            rt = Ring(A, 2, [128, 512], F32, "ropetmp")

            def p2_load(gi_):
                n_ = len(groups[gi_]) * 128
                S.dma("sp", hTs[gi_ % 2].t[:, :, 0:n_], hts[b, gi_, :, :, 0:n_], [r_hts[b][gi_]], [hTs[gi_ % 2].r])

            p2_load(0)
            for gi, grp in enumerate(groups):
                n = len(grp) * 128
                is_ctx = gi == 0
                hT = hTs[gi % 2]
                if gi + 1 < len(groups):
                    p2_load(gi + 1)
                p0 = (grp[0] - 2) * 128
                units = []
                if not (is_ctx and not need_ctx):
                    units += [("q", j) for j in range(4)]
                units.append(("k", 0))
                for ui, (kind, j) in enumerate(units):
                    c0 = j * 128 if kind == "q" else 512
                    cp0 = 768 + j * 128 if kind == "q" else 1280
                    b1, b2 = 1 + 2 * (ui % 2), 2 + 2 * (ui % 2)
                    for k in range(8):
                        S.mm(PS[b1][:, 0:n], wb_.t[:, k, c0:c0 + 128], hT.t[:, k, 0:n], k == 0, k == 7, [wb_r[k], hT.r], [PR[b1]])
                    if is_ctx:
                        dst = qc.t[:, j, :] if kind == "q" else ks.t[:, 0:256]
                        S.cp("act", dst, PS[b1][:, 0:n], [PR[b1]], [grp_r[gi]])
                        continue
                    for k in range(8):
                        S.mm(PS[b2][:, 0:n], wb_.t[:, k, cp0:cp0 + 128], hT.t[:, k, 0:n], k == 0, k == 7, [wb_r[k], hT.r], [PR[b2]])
                    t1, t2 = rt.next(), rt.next()
                    S.tt("dve", t1.t[:, :], PS[b1][:, :], cosT.t[:, p0:p0 + 512], ALU.mult, [PR[b1], cosT.r], [t1.r])
                    S.tt("dve", t2.t[:, :], PS[b2][:, :], sinT.t[:, p0:p0 + 512], ALU.mult, [PR[b2], sinT.r], [t2.r])
                    dst = qs.t[:, j, p0:p0 + 512] if kind == "q" else ks.t[:, 256 + p0:256 + p0 + 512]
                    S.tt("pool", dst, t1.t[:, :], t2.t[:, :], ALU.add, [t1.r, t2.r], [grp_r[gi]])
                for i, tt in enumerate(grp):
                    cs = slice(i * 128, (i + 1) * 128)
                    vb = 5 + (i % 2)
                    for k in range(8):
                        S.mm(PS[vb][:, 0:128], hT.t[:, k, cs], wb_.t[:, k, 640:768], k == 0, k == 7, [hT.r, wb_r[k]], [PR[vb]])
                    S.cp("act", va.t[:, tt, :, 0:64], PS[vb][:, 0:128].rearrange("p (g d) -> p g d", g=2), [PR[vb], va.r], [grp_r[gi]])
            S.barrier()
            A.release(mP2)

            wo, wo_r = w_load("wo", l)
            pT = Ring(A, 4, [128, 4, 128], BF16, "pT")
            csos = Ring(A, 2, [128, 512], F32, "cso")
            den = Ring(A, 2, [128, 4], F32, "den")
            all_r = grp_r + [va.r]
            qblocks = ([("c", 0), ("c", 1)] if need_ctx else []) + [("l", n_) for n_ in range(16)]
            units = []
            for (qk, n_) in qblocks:
                tt = n_ if qk == "c" else 2 + n_
                cso = csos.next()
                for g in range(2):
                    pr = slice(64 * g, 64 * g + 64)
                    if qk == "c":
                        qap = qc.t[pr, :, n_ * 128:(n_ + 1) * 128]
                        keys = [(0, None), (1, None)]
                    else:
                        qap = qs.t[pr, :, n_ * 128:(n_ + 1) * 128]
                        keys = [(0, None), (1, None)]
                        if n_ > 0:
                            keys.append((2 + n_ - 1, "mP"))
                        keys.append((2 + n_, None))
                        if n_ < 15:
                            keys.append((2 + n_ + 1, "mN"))
                    units.append((tt, g, pr, qap, keys, cso))
            steps = [(ui, ki) for ui, u in enumerate(units) for ki in range(len(u[4]))]
            LOOK = 2

            def swa_score(si):
                ui, ki = steps[si]
                tt, g, pr, qap, keys, cso = units[ui]
                kt = keys[ki][0]
                sb = 1 + si % 4
                S.mm(PS[sb][:, :].rearrange("p (r q) -> p r q", r=4), ks.t[pr, kt * 128:(kt + 1) * 128], qap, True, True, all_r, [PR[sb]])

            def swa_rest(si):
                ui, ki = steps[si]
                tt, g, pr, qap, keys, cso = units[ui]
                kt, mname = keys[ki]
                sb = 1 + si % 4
                ob = 6 + (ui % 2)
                p = pT.next()
                S.act(p.t[:, :, :], PS[sb][:, :].rearrange("p (r q) -> p r q", r=4), AF.Exp, [PR[sb]], [p.r], scale=0.125)
                if mname is not None:
                    S.tt("dve", p.t[:, :, :], p.t[:, :, :], cf[mname].t[:, :].unsqueeze(1).to_broadcast([128, 4, 128]), ALU.mult,
                         [p.r, cf[mname].r], [p.r])
                for r_ in range(4):
                    S.mm(PS[ob][:, r_ * 65:(r_ + 1) * 65], p.t[:, r_, :], va.t[:, kt, g, :], ki == 0 and r_ == 0,
                         ki == len(keys) - 1 and r_ == 3, [p.r] + all_r, [PR[ob]], skip_group_check=True)
                if ki < len(keys) - 1:
                    return
                ov = PS[ob][:, 0:260].rearrange("p (r d) -> p r d", r=4)
                dn = den.next()
                S.tt("dve", dn.t[:, :], ov[:, :, 64], esink.t[:, 4 * g:4 * g + 4], ALU.add, [PR[ob], esink.r], [dn.r])
                S.recip(dn.t[:, :], dn.t[:, :], [dn.r], [dn.r])
                S.tt("dve", cso.t[:, 256 * g:256 * g + 256].rearrange("p (r d) -> p r d", r=4), ov[:, :, 0:64],
                     dn.t[:, :].unsqueeze(2).to_broadcast([128, 4, 64]), ALU.mult, [PR[ob], dn.r], [cso.r])
                if g == 1:
                    ss = sumsq(cso.t[:, :], [cso.r], 512)
                    rs = rstd_from_ss(ss.t[:, 0:1], 512.0, eps6, [ss.r])
                    S.stt("dve", cat.t[:, tt, 512:1024], cso.t[:, :], rs.t[:, 0:1], fv["swa_out_g"].t[:, :], ALU.mult, ALU.mult,
                          [cso.r, rs.r, fv["swa_out_g"].r], [catr[tt]])

            for si in range(len(steps) + LOOK):
                if si < len(steps):
                    swa_score(si)
                if si >= LOOK:
                    swa_rest(si - LOOK)
            S.barrier()
            A.release(mS)

            mO = A.mark()
            A.top = mB_
            sgs = [tiles_all[i:i + 6] for i in range(0, len(tiles_all), 6)]
            TM = max(len(s_) for s_ in sgs) * 128
            actT = T(A, [128, 22, TM], BF16, "actT")
            wd = T(A, [128, 22, D], BF16, "w_down")
            wd_r = [Res() for _ in range(22)]
            xring_n = Ring(A, 2, [128, D], F32, "xt4n")
            xring_r = Ring(A, 2, [128, D], F32, "xt4r")
            oring4 = Ring(A, 2, [128, D], F32, "xo4")
            sil = Ring(A, 2, [128, 512], F32, "sil")
            A.top = max(A.top, mO + 40 * 1024)
            early_base = A.top
            A2 = [T(A, [128, D], F32, "A2") for _ in range(2)]
            B2 = [T(A, [128, D], F32, "B2") for _ in range(2)]
            C2 = [T(A, [128, D], F32, "C2") for _ in range(2)]
            for si, j in ((0, 2), (1, b)):
                load_bcast(A2[si], modv[l, 3, j:j + 1, :])
                load_bcast(B2[si], modv[l, 4, j:j + 1, :])
                load_bcast(C2[si], modv[l, 5, j:j + 1, :])
            hT2 = T(A, [128, 8, TM], BF16, "hTf")
            hring4 = {"tmp": Ring(A, 2, [128, D], F32, "ntmp4"), "h": Ring(A, 3, [128, D], BF16, "h4")}
            ffn_top = A.top
            A.top = mO
            C1 = [T(A, [128, D], F32, "C1") for _ in range(2)]
            for si, j in ((0, 2), (1, b)):
                load_bcast(C1[si], modv[l, 2, j:j + 1, :])
            catT = Ring(A, 2, [128, 8, 128], BF16, "catT")
            xring = Ring(A, 2, [128, D], F32, "xt3")
            oring = Ring(A, 2, [128, D], F32, "xo3")
            pend_tr = None
            for ti, tt in enumerate(tiles_all):
                if debug:
                    S.dma("sp", dbg_cat[b, tt * 128:(tt + 1) * 128, :], cat.t[:, tt, :], [catr[tt]], [])
                trb = ti % 2
                pbf = PS[trb][:, :].bitcast(BF16)
                for k in range(8):
                    S.tr(pbf[:, k * 128:(k + 1) * 128], cat.t[:, tt, k * 128:(k + 1) * 128], ident.t[:, :], [catr[tt], ident.r], [PR[trb]])
                ct = catT.next()
                S.cp("act", ct.t[:, :, :], pbf[:, :].rearrange("p (k t) -> p k t", k=8), [PR[trb]], [ct.r])
                yb = (2 + 2 * (ti % 2), 3 + 2 * (ti % 2))
                for hf in range(2):
                    for k in range(8):
                        S.mm(PS[yb[hf]][:, :], ct.t[:, k, :], wo.t[:, k, hf * 512:(hf + 1) * 512], k == 0, k == 7, [ct.r, wo_r[k]], [PR[yb[hf]]])
                xt = load_x(tt, xring)
                si = 0 if tt < 2 else 1
                o1 = post_residual(yb, xt, C1[si], xs1[b, tt * 128:(tt + 1) * 128, :], r_xs1[b][tt], oring)
                if pend_tr is not None:
                    norm_tr(pend_tr[0], hT2.t, hT2.r, pend_tr[1] * 128, 6 + (pend_tr[1] % 2))
                    pend_tr = None
                if ti < len(sgs[0]):
                    pend_tr = (norm_elem(o1, A2[si], B2[si], hring4, add_eng="pool"), ti)
            if pend_tr is not None:
                norm_tr(pend_tr[0], hT2.t, hT2.r, pend_tr[1] * 128, 6 + (pend_tr[1] % 2))
            assert A.top <= early_base
            S.barrier()
            A.top = ffn_top

            wgb, wub = w_rings()
            oring = oring4
            hring = hring4
            wgu = Wd["ffn_w_gu"][l].rearrange("(k p) n -> p k n", p=128)
            wdn = Wd["ffn_w_down"][l].rearrange("(f p) n -> p f n", p=128)

            def load_x1(tt, ring):
                xt = ring.next()
                S.dma("sp", xt.t[:, :], xs1[b, tt * 128:(tt + 1) * 128, :], [r_xs1[b][tt]], [xt.r])
                return xt

            def ffn_elem(tt):
                xt = load_x1(tt, xring_n)
                si_ = 0 if tt < 2 else 1
                return norm_elem(xt, A2[si_], B2[si_], hring, add_eng="dve")

            for sgi, sg in enumerate(sgs):
                ntok = len(sg) * 128
                nxt = sgs[sgi + 1] if sgi + 1 < len(sgs) else []
                chunks = [(c0, min(512, ntok - c0)) for c0 in range(0, ntok, 512)]
                ui = 0
                assert len(nxt) <= len(sg)
                blocks = {}

                def issue_gu(cb_):
                    wg_, wu_ = wgb.next(), wub.next()
                    S.dma("pool", wg_.t[:, :, :], wgu[:, :, cb_ * 256:(cb_ + 1) * 256], [], [wg_.r])
                    S.dma("pool", wu_.t[:, :, :], wgu[:, :, DFF + cb_ * 256:DFF + (cb_ + 1) * 256], [], [wu_.r])
                    blocks[cb_] = (wg_, wu_)

                issue_gu(0)
                for cb in range(11):
                    if cb + 1 < 11:
                        issue_gu(cb + 1)
                    wg, wu = blocks[cb]
                    for f in (2 * cb, 2 * cb + 1):
                        S.dma("pool", wd.t[:, f, :], wdn[:, f, :], [], [wd_r[f]])
                    for fs in range(2):
                        fb = cb * 2 + fs
                        for (c0, cn) in chunks:
                            bg, bu = 2 + 2 * (ui % 2), 3 + 2 * (ui % 2)
                            ui += 1
                            for k in range(8):
                                S.mm(PS[bg][:, 0:cn], wg.t[:, k, fs * 128:(fs + 1) * 128], hT2.t[:, k, c0:c0 + cn], k == 0, k == 7,
                                     [wg.r, hT2.r], [PR[bg]])
                            for k in range(8):
                                S.mm(PS[bu][:, 0:cn], wu.t[:, k, fs * 128:(fs + 1) * 128], hT2.t[:, k, c0:c0 + cn], k == 0, k == 7,
                                     [wu.r, hT2.r], [PR[bu]])
                            sl = sil.next()
                            S.act(sl.t[:, 0:cn], PS[bg][:, 0:cn], AF.Exp, [PR[bg]], [sl.r], scale=-1.0)
                            S.act(sl.t[:, 0:cn], sl.t[:, 0:cn], AF.Ln, [sl.r], [sl.r], bias=1.0)
                            S.act(sl.t[:, 0:cn], sl.t[:, 0:cn], AF.Exp, [sl.r], [sl.r], scale=-1.0)
                            S.tt("dve", sl.t[:, 0:cn], sl.t[:, 0:cn], PS[bg][:, 0:cn], ALU.mult, [sl.r, PR[bg]], [sl.r])
                            S.tt("dve", actT.t[:, fb, c0:c0 + cn], sl.t[:, 0:cn], PS[bu][:, 0:cn], ALU.mult, [sl.r, PR[bu]], [actT.r])
                if sgi == len(sgs) - 1:
                    lnext = l if b + 1 < nb else (l + 1 if l + 1 < nlayers else None)
                    if lnext is not None:
                        pre["wa"] = w_load("wa", lnext)
                hq = [ffn_elem(nxt[0])] if len(nxt) > 0 else []
                for i, tt in enumerate(sg):
                    if i + 1 < len(nxt):
                        hq.append(ffn_elem(nxt[i + 1]))
                    hn = hq[i] if i < len(nxt) else None
                    yb = (2 + 2 * (i % 2), 3 + 2 * (i % 2))
                    for hf in range(2):
                        for f in range(22):
                            S.mm(PS[yb[hf]][:, :], actT.t[:, f, i * 128:(i + 1) * 128], wd.t[:, f, hf * 512:(hf + 1) * 512], f == 0, f == 21,
                                 [actT.r, wd_r[f]], [PR[yb[hf]]])
                    if hn is not None:
                        norm_tr(hn, hT2.t, hT2.r, i * 128, i % 2)
                    xt = load_x1(tt, xring_r)
                    si = 0 if tt < 2 else 1
                    if last and tt >= 2:
                        dst_ap, dst_res = out[b, (tt - 2) * 128:(tt - 1) * 128, :], Res()
                    else:
                        dst_ap, dst_res = xs2[b, tt * 128:(tt + 1) * 128, :], r_xs2[b][tt]
                    post_residual(yb, xt, C2[si], dst_ap, dst_res, oring)
            S.barrier()
            A.release(mB_)
        S.barrier()
        A.release(mL)
    print("ops:", {e: len(v) for e, v in S.ops.items()}, "peak sbuf", A.peak)
    S.emit()
    return nc


_CACHE = {}


def _core_inputs(inp, core, nb, W, C):
    b0 = core * nb
    x = np.asarray(inp["x"], np.float32)
    ctx = np.asarray(inp["ctx"], np.float32)
    c = np.asarray(inp["c"], np.float32)
    c_ctx = np.asarray(inp["c_ctx"], np.float32)
    m = {}
    m["xin"] = np.ascontiguousarray(np.concatenate([ctx[b0:b0 + nb], x[b0:b0 + nb]], axis=1))
    cc = np.stack([c[b0 + (j % nb)] for j in range(2)] + [c_ctx], 0)
    m["ccT"] = np.ascontiguousarray(cc.reshape(3, 8, 128).transpose(2, 1, 0))
    m.update(W)
    for k, v in C.items():
        m["c_" + k] = v
    return m


def kernel(**inp):
    nb = 2
    if "nc" not in _CACHE:
        _CACHE["nc"] = build(nb=nb, nlayers=2)
    nc = _CACHE["nc"]
    W = _prep_weights(inp)
    C = _consts()
    in_maps = [_core_inputs(inp, core, nb, W, C) for core in range(NCORES)]
    res = run_bass_kernel_spmd(nc, in_maps, core_ids=list(range(NCORES)))
    outs = [np.asarray(r["out"], np.float32) for r in res.results]
    return np.concatenate(outs, axis=0)
```

```python
import numpy as np
import ml_dtypes
import concourse.bass as bass
import concourse.mybir as mybir
from concourse.bass_utils import run_bass_kernel_spmd

F32 = mybir.dt.float32
BF16 = mybir.dt.bfloat16
ALU = mybir.AluOpType
AF = mybir.ActivationFunctionType
AX = mybir.AxisListType

D = 1024
NT = 18
DFF = 2816
NCORES = 8


class Res:
    __slots__ = ("name", "w", "r")

    def __init__(self, name=""):
        self.name = name
        self.w = None
        self.r = {}


class Op:
    __slots__ = ("eng", "fn", "deps", "is_dma", "sem", "val", "needs_inc", "ring_wait")

    def __init__(self, eng, fn, is_dma):
        self.eng = eng
        self.fn = fn
        self.deps = []
        self.is_dma = is_dma
        self.sem = None
        self.val = 0
        self.needs_inc = False
        self.ring_wait = None


class Sched:
    ENGS = ("pe", "dve", "act", "pool", "sp")
    RING = 12

    def __init__(self, nc):
        self.nc = nc
        self.ops = {e: [] for e in self.ENGS}
        self.dma_count = {e: 0 for e in self.ENGS}
        self.pending_barrier = {e: [] for e in self.ENGS}
        self.all_dma_since_barrier = []
        self.n_ops = 0

    def _dep(self, op, p, kind):
        if p is None or p is op:
            return
        if (not p.is_dma) and p.eng == op.eng and not op.is_dma and p.eng == "pe":
            return
        op.deps.append(p)
        if not p.is_dma:
            p.needs_inc = True

    def op(self, eng, fn, reads=(), writes=(), dma=False):
        o = Op(eng, fn, dma)
        for b in self.pending_barrier[eng]:
            self._dep(o, b, "raw")
        self.pending_barrier[eng] = []
        for r in reads:
            self._dep(o, r.w, "raw")
        for w in writes:
            self._dep(o, w.w, "waw")
            for rd in w.r.values():
                self._dep(o, rd, "war")
        for r in reads:
            if dma:
                r.r[("dma", id(o))] = o
            else:
                r.r[eng] = o
        for w in writes:
            w.w = o
            w.r = {}
        if dma:
            i = self.dma_count[eng]
            self.dma_count[eng] = i + 1
            o.sem = (eng, i % self.RING)
            o.val = 16 * (i // self.RING + 1)
            if i >= self.RING:
                o.ring_wait = (o.sem, o.val - 16)
            self.all_dma_since_barrier.append(o)
        self.ops[eng].append(o)
        self.n_ops += 1
        return o

    def barrier(self):
        lasts = []
        for e in self.ENGS:
            for o in reversed(self.ops[e]):
                if not o.is_dma:
                    lasts.append(o)
                    break
        lasts += self.all_dma_since_barrier
        self.all_dma_since_barrier = []
        for e in self.ENGS:
            self.pending_barrier[e] = list(lasts)

    def mm(self, out, lhsT, rhs, start, stop, reads, writes, **kw):
        return self.op("pe", lambda e: e.matmul(out, lhsT, rhs, start=start, stop=stop, **kw), reads, writes)

    def tr(self, out, in_, ident, reads, writes):
        return self.op("pe", lambda e: e.transpose(out, in_, ident), reads, writes)

    def dma(self, eng, out, in_, reads, writes):
        return self.op(eng, lambda e: e.dma_start(out=out, in_=in_), reads, writes, dma=True)

    def act(self, out, in_, func, reads, writes, **kw):
        return self.op("act", lambda e: e.activation(out, in_, func, **kw), reads, writes)

    def tt(self, eng, out, in0, in1, op, reads, writes):
        return self.op(eng, lambda e: e.tensor_tensor(out, in0, in1, op), reads, writes)

    def ts(self, eng, out, in0, s1, s2, op0, op1, reads, writes):
        return self.op(eng, lambda e: e.tensor_scalar(out, in0, s1, s2, op0, op1), reads, writes)

    def stt(self, eng, out, in0, scalar, in1, op0, op1, reads, writes):
        return self.op(eng, lambda e: e.scalar_tensor_tensor(out, in0, scalar, in1, op0, op1), reads, writes)

    def cp(self, eng, out, in_, reads, writes):
        if eng == "act":
            return self.op(eng, lambda e: e.copy(out, in_), reads, writes)
        return self.op(eng, lambda e: e.tensor_copy(out, in_), reads, writes)

    def memset(self, eng, ap, val, writes):
        return self.op(eng, lambda e: e.memset(ap, val), [], writes)

    def recip(self, out, in_, reads, writes):
        return self.op("dve", lambda e: e.reciprocal(out, in_), reads, writes)

    def reduce(self, out, in_, op, reads, writes):
        return self.op("dve", lambda e: e.tensor_reduce(out, in_, AX.X, op), reads, writes)

    def emit(self):
        nc = self.nc
        from contextlib import ExitStack

        with ExitStack() as st:
            esem = {e: st.enter_context(nc.semaphore("s_" + e)) for e in self.ENGS if e != "sp"}
            rsem = {}
            for e in self.ENGS:
                for k in range(min(self.RING, self.dma_count[e])):
                    rsem[(e, k)] = st.enter_context(nc.semaphore("d_%s_%d" % (e, k)))
            for e in self.ENGS:
                c = 0
                for o in self.ops[e]:
                    if o.is_dma:
                        continue
                    if o.needs_inc:
                        c += 1
                        o.val = c
            block = st.enter_context(nc.Block())

            def run(ename, eng):
                seen = {}

                def wait(semkey, v):
                    if seen.get(semkey, 0) >= v:
                        return
                    seen[semkey] = v
                    h = rsem[semkey] if isinstance(semkey, tuple) else esem[semkey]
                    eng.wait_ge(h, v)

                for o in self.ops[ename]:
                    for p in o.deps:
                        if p.is_dma:
                            wait(p.sem, p.val)
                        else:
                            wait(p.eng, p.val)
                    if o.ring_wait is not None:
                        wait(o.ring_wait[0], o.ring_wait[1])
                    ins = o.fn(eng)
                    if o.is_dma:
                        ins.then_inc(rsem[o.sem], 16)
                    elif o.needs_inc:
                        ins.then_inc(esem[ename], 1)
                n = self.dma_count[ename]
                for k in range(min(self.RING, n)):
                    uses = (n - 1 - k) // self.RING + 1
                    wait((ename, k), 16 * uses)

            @block.tensor
            def _(eng):
                run("pe", eng)

            @block.vector
            def _(eng):
                run("dve", eng)

            @block.scalar
            def _(eng):
                run("act", eng)

            @block.gpsimd
            def _(eng):
                run("pool", eng)

            @block.sync
            def _(eng):
                run("sp", eng)


class Arena:
    def __init__(self, nc, base=16384, limit=229376):
        self.nc = nc
        self.top = base
        self.limit = limit
        self.n = 0
        self.peak = base

    def mark(self):
        return self.top

    def release(self, m):
        self.top = m

    def alloc(self, shape, dtype, name="t"):
        esz = 4 if dtype == F32 else 2
        per_part = esz * int(np.prod(shape[1:]))
        per_part = (per_part + 63) // 64 * 64
        off = self.top
        self.top += per_part
        self.peak = max(self.peak, self.top)
        assert self.top <= self.limit, "SBUF arena overflow %d (%s)" % (self.top, name)
        self.n += 1
        return self.nc.alloc_sbuf_tensor_at("%s_%d" % (name, self.n), list(shape), dtype, offset=off)


class T:
    def __init__(self, A, shape, dtype, name="t"):
        self.t = A.alloc(shape, dtype, name)
        self.r = Res(name)


class Ring:
    def __init__(self, A, n, shape, dtype, name="r"):
        self.items = [T(A, shape, dtype, name) for _ in range(n)]
        self.i = 0

    def next(self):
        x = self.items[self.i % len(self.items)]
        self.i += 1
        return x


def _consts():
    c = {}
    c["ident"] = np.eye(128, dtype=np.float32)
    s = np.arange(128)
    same = (s[:, None] // 64) == (s[None, :] // 64)
    le = s[:, None] <= s[None, :]
    ge = s[:, None] >= s[None, :]
    mF = (same & le).astype(np.float32)
    mB = (same & ge).astype(np.float32)
    c["mF"] = mF
    c["mB"] = mB
    c["triF"] = mF / 16.0
    c["triB"] = mB / 16.0
    c["uF"] = (same & (s[:, None] > s[None, :])).astype(np.float32) / 16.0
    c["uB"] = (same & (s[:, None] < s[None, :])).astype(np.float32) / 16.0
    c["mP"] = ge.astype(np.float32)
    c["mN"] = le.astype(np.float32)
    p = np.arange(128)
    c["bm"] = (p[:, None] // 32 == np.arange(4)[None, :]).astype(np.float32)
    c["smask"] = np.repeat(c["bm"], 64, axis=1).astype(np.float32)
    rows = 2048 // 64
    row = np.repeat(np.arange(rows), 64).astype(np.float32)
    col = np.tile(np.arange(64), rows).astype(np.float32)
    inv_freq = np.power(np.float32(10000.0), -np.arange(0, 32, 2, dtype=np.float32) / np.float32(32)).astype(np.float32)
    ang_row = (row[:, None] * inv_freq[None, :]).astype(np.float32)
    ang_col = (col[:, None] * inv_freq[None, :]).astype(np.float32)
    cosT = np.zeros((64, 2048), np.float32)
    sinT = np.zeros((64, 2048), np.float32)
    for d in range(64):
        j = d % 16
        ang = ang_row if d < 32 else ang_col
        half = (d % 32) // 16
        cosT[d] = np.cos(ang[:, j])
        sn = np.sin(ang[:, j])
        sinT[d] = -sn if half == 0 else sn
    c["cosT"] = np.concatenate([cosT, cosT], 0)
    c["sinT"] = np.concatenate([sinT, sinT], 0)
    return c


def _partner(d):
    return d + 16 if (d % 32) < 16 else d - 16


def _prep_weights(inp):
    w = {}
    w_in = np.asarray(inp["w_in"], np.float32)
    w["w_in_a"] = np.ascontiguousarray(w_in[:, :, 0:1312])
    qoff, koff, voff = 1312, 1824, 1952
    qcols, qpcols = [], []
    for j in range(4):
        for hd in (j, 4 + j):
            for d in range(64):
                qcols.append(qoff + hd * 64 + d)
                qpcols.append(qoff + hd * 64 + _partner(d))
    kcols = [koff + g * 64 + d for g in range(2) for d in range(64)]
    kpcols = [koff + g * 64 + _partner(d) for g in range(2) for d in range(64)]
    vcols = list(range(voff, voff + 128))
    cols = qcols + kcols + vcols + qpcols + kpcols
    w["w_in_b"] = np.ascontiguousarray(w_in[:, :, cols])
    w["wsT"] = np.ascontiguousarray(np.transpose(np.asarray(inp["gmlp_ws"], np.float32), (0, 1, 3, 2)))
    w["bsT"] = np.ascontiguousarray(np.transpose(np.asarray(inp["gmlp_bs"], np.float32), (0, 2, 1)))
    w["gnorm4"] = np.ascontiguousarray(np.tile(np.asarray(inp["gla_norm"], np.float32), (1, 4)))
    for k in ("mod_w", "mod_b", "n1_pre", "n1_post", "n2_pre", "n2_post", "w_out", "gla_wa2", "gla_ba",
              "gmlp_ln_g", "gmlp_ln_b", "gmlp_out_g", "swa_sink", "swa_out_g", "ffn_w_gu", "ffn_w_down"):
        w[k] = np.ascontiguousarray(np.asarray(inp[k], np.float32))
    return w


def build(nb=2, nlayers=2, debug=False):
    nc = bass.Bass("TRN2", target_bir_lowering=False)
    S = Sched(nc)
    WBASE = 229376 - 24576 - 512
    A = Arena(nc, limit=WBASE)
    C = _consts()

    def din(name, shape, dt=F32):
        return nc.dram_tensor(name, list(shape), dt, kind="ExternalInput").ap()

    kind_dbg = "ExternalOutput" if debug else "Internal"
    xin = din("xin", [nb, NT * 128, D])
    ccT = din("ccT", [128, 8, 3])
    Wd = {}
    shapes = {
        "mod_w": [2, D, 6 * D], "mod_b": [2, 6 * D], "n1_pre": [2, D], "n1_post": [2, D], "n2_pre": [2, D],
        "n2_post": [2, D], "w_in_a": [2, D, 1312], "w_in_b": [2, D, 1408], "w_out": [2, D, D],
        "gla_wa2": [2, 2, 16, 128], "gla_ba": [2, 2, 128], "gnorm4": [2, 256], "gmlp_ln_g": [2, 256],
        "gmlp_ln_b": [2, 256], "wsT": [2, 4, 128, 128], "bsT": [2, 128, 4], "gmlp_out_g": [2, 256],
        "swa_sink": [2, 8], "swa_out_g": [2, 512], "ffn_w_gu": [2, D, 2 * DFF], "ffn_w_down": [2, DFF, D],
    }
    for k, shp in shapes.items():
        Wd[k] = din(k, shp)
    Cd = {k: din("c_" + k, list(v.shape)) for k, v in C.items()}
    out = nc.dram_tensor("out", [nb, 2048, D], F32, kind="ExternalOutput").ap()
    xs1 = nc.dram_tensor("xs1", [nb, NT * 128, D], F32, kind=kind_dbg).ap()
    xs2 = nc.dram_tensor("xs2", [nb, NT * 128, D], F32, kind=kind_dbg).ap()
    modv = nc.dram_tensor("modv", [2, 6, 3, D], F32, kind=kind_dbg).ap()
    dbg_cat = nc.dram_tensor("dbg_cat", [nb, NT * 128, D], BF16, kind="ExternalOutput").ap() if debug else None
    r_xs1 = [[Res() for _ in range(NT)] for _ in range(nb)]
    r_xs2 = [[Res() for _ in range(NT)] for _ in range(nb)]
    r_modv = Res()
    hts = nc.dram_tensor("hts", [nb, 5, 128, 8, 512], BF16, kind="Internal").ap()
    r_hts = [[Res() for _ in range(5)] for _ in range(nb)]

    PS = [nc.alloc_psum_tensor("ps%d" % i, [128, 512], F32) for i in range(8)]
    PR = [Res("ps%d" % i) for i in range(8)]

    ident = T(A, [128, 128], BF16, "ident")
    S.dma("pool", ident.t[:, :], Cd["ident"][:, :], [], [ident.r])
    cf = {}
    for k in ("mF", "mB", "triF", "triB", "uF", "uB"):
        cf[k] = T(A, [128, 128], F32, k)
        S.dma("sp", cf[k].t[:, :], Cd[k][:, :], [], [cf[k].r])
    for k in ("mP", "mN"):
        cf[k] = T(A, [128, 128], BF16, k)
        S.dma("pool", cf[k].t[:, :], Cd[k][:, :], [], [cf[k].r])
    bm = T(A, [128, 4], BF16, "bm")
    S.dma("pool", bm.t[:, :], Cd["bm"][:, :], [], [bm.r])
    bmf = T(A, [128, 4], F32, "bmf")
    S.dma("sp", bmf.t[:, :], Cd["bm"][:, :], [], [bmf.r])
    smask = T(A, [128, 256], F32, "smask")
    S.dma("sp", smask.t[:, :], Cd["smask"][:, :], [], [smask.r])
    junk = T(A, [128, 1024], F32, "junk")
    eps6 = 1e-6
    epsT = {}
    for ev in (1e-6, 1e-5):
        epsT[ev] = T(A, [128, 1], F32, "eps")
        S.memset("dve", epsT[ev].t[:, :], ev, [epsT[ev].r])

    stat = Ring(A, 24, [128, 8], F32, "stat")

    def rstd_from_ss(ss, n, eps, reads):
        k = ss.shape[1]
        a = stat.next()
        S.act(a.t[:, 0:k], ss, AF.Ln, list(reads) + [epsT[eps].r], [a.r], scale=1.0 / n, bias=epsT[eps].t[:, 0:1])
        c_ = stat.next()
        S.act(c_.t[:, 0:k], a.t[:, 0:k], AF.Exp, [a.r], [c_.r], scale=-0.5)
        return c_

    def sumsq(in_ap, reads, n_free):
        a = stat.next()
        S.act(junk.t[:, 0:n_free], in_ap, AF.Square, reads, [junk.r, a.r], accum_out=a.t[:, 0:1])
        return a

    def load_bcast(dst, src_row):
        S.dma("sp", dst.t[:, :], src_row.partition_broadcast(128), [r_modv], [dst.r])

    m0 = A.mark()
    cc32 = T(A, [128, 8, 3], F32, "cc32")
    S.dma("sp", cc32.t[:, :, :], ccT[:, :, :], [], [cc32.r])
    scT = T(A, [128, 8, 3], BF16, "scT")
    S.act(scT.t[:, :, :], cc32.t[:, :, :], AF.Silu, [cc32.r], [scT.r])
    modraw = T(A, [3, 6 * D], F32, "modraw")
    biasT = T(A, [3, 6 * D], F32, "biasT")
    nrm3 = {k: T(A, [3, D], F32, k) for k in ("n1_pre", "n1_post", "n2_pre", "n2_post")}
    mwb = Ring(A, 2, [128, 8, 512], BF16, "mwb")
    mtmp = Ring(A, 2, [3, D], F32, "mtmp")
    for l in range(nlayers):
        S.dma("sp", biasT.t[:, :], Wd["mod_b"][l:l + 1, :].partition_broadcast(3), [], [biasT.r])
        for k in nrm3:
            S.dma("sp", nrm3[k].t[:, :], Wd[k][l:l + 1, :].partition_broadcast(3), [], [nrm3[k].r])
        mw = Wd["mod_w"][l].rearrange("(k p) n -> p k n", p=128)
        for blk in range(12):
            wb = mwb.next()
            S.dma("pool", wb.t[:, :, :], mw[:, :, blk * 512:(blk + 1) * 512], [], [wb.r])
            pb = blk % 2
            for k in range(8):
                S.mm(PS[pb][0:3, :], scT.t[:, k, :], wb.t[:, k, :], k == 0, k == 7, [scT.r, wb.r], [PR[pb]])
            S.tt("dve", modraw.t[:, blk * 512:(blk + 1) * 512], PS[pb][0:3, :], biasT.t[:, blk * 512:(blk + 1) * 512],
                 ALU.add, [PR[pb], biasT.r], [modraw.r])
        combos = [(1, "n1_pre", "a"), (0, None, "b"), (2, "n1_post", "c"), (4, "n2_pre", "a"), (3, None, "b"), (5, "n2_post", "c")]
        for w_, (mi, nk, kind) in enumerate(combos):
            src = modraw.t[:, mi * D:(mi + 1) * D]
            if kind == "b":
                S.dma("sp", modv[l, w_, :, :], src, [modraw.r], [r_modv])
                continue
            tmp = mtmp.next()
            if kind == "a":
                S.stt("dve", tmp.t[:, :], src, 1.0, nrm3[nk].t[:, :], ALU.add, ALU.mult, [modraw.r, nrm3[nk].r], [tmp.r])
            else:
                S.tt("dve", tmp.t[:, :], src, nrm3[nk].t[:, :], ALU.mult, [modraw.r, nrm3[nk].r], [tmp.r])
            S.dma("sp", modv[l, w_, :, :], tmp.t[:, :], [tmp.r], [r_modv])
    S.barrier()
    A.release(m0)

    def norm_elem(xt, Am, Bm, hring, add_eng="pool"):
        ss = sumsq(xt.t[:, :], [xt.r], 1024)
        rs = rstd_from_ss(ss.t[:, 0:1], 1024.0, eps6, [ss.r])
        tmp = hring["tmp"].next()
        S.stt("dve", tmp.t[:, :], xt.t[:, :], rs.t[:, 0:1], Am.t[:, :], ALU.mult, ALU.mult, [xt.r, rs.r, Am.r], [tmp.r])
        h = hring["h"].next()
        S.tt(add_eng, h.t[:, :], tmp.t[:, :], Bm.t[:, :], ALU.add, [tmp.r, Bm.r], [h.r])
        return h

    def norm_tr(h, hT, hT_res, col0, trbank):
        pbf = PS[trbank][:, :].bitcast(BF16)
        for k in range(8):
            S.tr(pbf[:, k * 128:(k + 1) * 128], h.t[:, k * 128:(k + 1) * 128], ident.t[:, :], [h.r, ident.r], [PR[trbank]])
        S.cp("dve", hT[:, :, col0:col0 + 128], pbf[:, :].rearrange("p (k t) -> p k t", k=8), [PR[trbank]], [hT_res])

    def norm_mod_T(xt, Am, Bm, hT, col0, hring, trbank):
        h = norm_elem(xt, Am, Bm, hring)
        norm_tr(h, hT.t, hT.r, col0, trbank)

    def post_residual(y_banks, xt, Cm, dst_ap, dst_res, oring):
        s0 = sumsq(PS[y_banks[0]][:, :], [PR[y_banks[0]]], 512)
        s1 = sumsq(PS[y_banks[1]][:, :], [PR[y_banks[1]]], 512)
        st = stat.next()
        S.tt("dve", st.t[:, 0:1], s0.t[:, 0:1], s1.t[:, 0:1], ALU.add, [s0.r, s1.r], [st.r])
        rs = rstd_from_ss(st.t[:, 0:1], 1024.0, eps6, [st.r])
        o = oring.next()
        for hf in range(2):
            S.stt("dve", o.t[:, hf * 512:(hf + 1) * 512], PS[y_banks[hf]][:, :], rs.t[:, 0:1], Cm.t[:, hf * 512:(hf + 1) * 512],
                  ALU.mult, ALU.mult, [PR[y_banks[hf]], rs.r, Cm.r], [o.r])
        S.tt("dve", o.t[:, :], o.t[:, :], xt.t[:, :], ALU.add, [o.r, xt.r], [o.r])
        S.dma("sp", dst_ap, o.t[:, :], [o.r], [dst_res])
        return o

    class TX:
        def __init__(self, t, name="w"):
            self.t = t
            self.r = Res(name)

    wcur = {"res": []}
    wmark = T(A, [128, 1], F32, "wmark")
    wcount = [0]
    pre = {"wa": None}

    def w_marker():
        if wcur["res"]:
            S.memset("pool", wmark.t[:, :], 0.0, [wmark.r] + list(wcur["res"]))

    def w_load(kind, l_):
        w_marker()
        wcount[0] += 1
        ncols = {"wa": 1312, "wb": 1408, "wo": D}[kind]
        src = {"wa": Wd["w_in_a"], "wb": Wd["w_in_b"], "wo": Wd["w_out"]}[kind][l_].rearrange("(k p) n -> p k n", p=128)
        t = nc.alloc_sbuf_tensor_at("W%s_%d" % (kind, wcount[0]), [128, 8, ncols], BF16, offset=WBASE)
        rs = [Res() for _ in range(8)]
        for k in range(8):
            S.dma("pool", t[:, k, :], src[:, k, :], [], [rs[k]])
        wcur["res"] = rs
        return TX(t, kind), rs

    def w_rings():
        w_marker()
        wcount[0] += 1
        items = [TX(nc.alloc_sbuf_tensor_at("Wr%d_%d" % (i, wcount[0]), [128, 8, 256], BF16, offset=WBASE + i * 4096)) for i in range(6)]
        wcur["res"] = [x.r for x in items]
        rg, ru = Ring.__new__(Ring), Ring.__new__(Ring)
        rg.items, rg.i = items[0:3], 0
        ru.items, ru.i = items[3:6], 0
        return rg, ru

    for l in range(nlayers):
        need_ctx = l < 1
        last = l == nlayers - 1
        xsrc, r_xsrc = (xin, None) if l == 0 else (xs2, r_xs2)
        tiles_all = list(range(NT)) if need_ctx else list(range(2, NT))
        mL = A.mark()
        W2 = T(A, [33, 256], F32, "W2")
        S.memset("pool", W2.t[:, :], 0.0, [W2.r])
        S.dma("sp", W2.t[0:16, 0:128], Wd["gla_wa2"][l, 0, :, :], [], [W2.r])
        S.dma("sp", W2.t[16:32, 128:256], Wd["gla_wa2"][l, 1, :, :], [W2.r], [W2.r])
        S.dma("sp", W2.t[32:33, 0:128], Wd["gla_ba"][l, 0:1, :], [W2.r], [W2.r])
        S.dma("sp", W2.t[32:33, 128:256], Wd["gla_ba"][l, 1:2, :], [W2.r], [W2.r])
        wsT = T(A, [128, 4, 128], BF16, "wsT")
        S.dma("pool", wsT.t[:, :, :], Wd["wsT"][l].rearrange("g q p -> q g p"), [], [wsT.r])
        bsT = T(A, [128, 4], F32, "bsT")
        S.dma("sp", bsT.t[:, :], Wd["bsT"][l, :, :], [], [bsT.r])
        fv = {}
        for k, n in (("gnorm4", 256), ("gmlp_ln_g", 256), ("gmlp_ln_b", 256), ("gmlp_out_g", 256), ("swa_out_g", 512)):
            fv[k] = T(A, [128, n], F32, k)
            S.dma("sp", fv[k].t[:, :], Wd[k][l:l + 1, :].partition_broadcast(128), [], [fv[k].r])
        esink = T(A, [128, 8], F32, "esink")
        S.dma("sp", esink.t[:, :], Wd["swa_sink"][l:l + 1, :].partition_broadcast(128), [], [esink.r])
        S.act(esink.t[:, :], esink.t[:, :], AF.Exp, [esink.r], [esink.r])

        for b in range(nb):
            mB_ = A.mark()
            cat = T(A, [128, NT, D], BF16, "cat")
            catr = [Res() for _ in range(NT)]
            A1 = [T(A, [128, D], F32, "A1") for _ in range(2)]
            B1 = [T(A, [128, D], F32, "B1") for _ in range(2)]
            for si, j in ((0, 2), (1, b)):
                load_bcast(A1[si], modv[l, 0, j:j + 1, :])
                load_bcast(B1[si], modv[l, 1, j:j + 1, :])
            groups = [[0, 1], [2, 3, 4, 5], [6, 7, 8, 9], [10, 11, 12, 13], [14, 15, 16, 17]]

            def load_x(tt, xring):
                xt = xring.next()
                rd = [] if r_xsrc is None else [r_xsrc[b][tt]]
                S.dma("sp", xt.t[:, :], xsrc[b, tt * 128:(tt + 1) * 128, :], rd, [xt.r])
                return xt

            mG = A.mark()
            qst = T(A, [128, 2, NT * 128], BF16, "qst")
            kst = T(A, [128, 2, NT * 128], BF16, "kst")
            kpst = T(A, [128, NT, 2, 128], BF16, "kpst")
            vst = T(A, [128, NT, 256], BF16, "vst")
            sog = T(A, [128, NT, 256], BF16, "sog")
            dst_ = T(A, [128, 2, 2 * NT], F32, "dst")
            r_q = [Res() for _ in range(NT)]
            r_k = [Res() for _ in range(NT)]
            r_kp = [Res() for _ in range(NT)]
            r_v = [Res() for _ in range(NT)]
            r_sog = [Res() for _ in range(NT)]
            r_d = [Res() for _ in range(NT)]
            mP1 = A.mark()
            if pre["wa"] is not None:
                wa, wa_r = pre["wa"]
                pre["wa"] = None
            else:
                wa, wa_r = w_load("wa", l)
            hTs = [T(A, [128, 8, 512], BF16, "hT") for _ in range(2)]
            xring = Ring(A, 2, [128, D], F32, "xt")
            hring = {"tmp": Ring(A, 1, [128, D], F32, "ntmp"), "h": Ring(A, 5, [128, D], BF16, "h")}
            codes = T(A, [33, 512], F32, "codes")
            S.memset("pool", codes.t[32:33, :], 1.0, [codes.r])
            R2 = lambda shp, dt, nm: Ring(A, 2, shp, dt, nm)
            e_sbR, spR, e1R, e2R, erR = (R2([128, 256], F32, n_) for n_ in ("e_sb", "sp", "e1", "e2", "er"))
            zfR = R2([128, 512], F32, "zf")
            vnR = R2([128, 256], F32, "vn")
            vgR = R2([128, 256], BF16, "vg")
            goutR = R2([128, 256], F32, "gout")
            bnstR = R2([128, 6], F32, "bnst")
            bnagR = R2([128, 2], F32, "bnag")
            ktokR = R2([128, 128], F32, "ktok")
            sgR = R2([128, 256], F32, "sgate")
            cq = 32.0 ** -0.5

            def p1_elem(grp_):
                hs_ = []
                for tt_ in grp_:
                    xt_ = load_x(tt_, xring)
                    si_ = 0 if tt_ < 2 else 1
                    hs_.append(norm_elem(xt_, A1[si_], B1[si_], hring))
                return hs_

            def p1_tr(hs_, hT_, gi_):
                for i_, h_ in enumerate(hs_):
                    norm_tr(h_, hT_.t, hT_.r, i_ * 128, 0)
                n_ = len(hs_) * 128
                S.dma("sp", hts[b, gi_, :, :, 0:n_], hT_.t[:, :, 0:n_], [hT_.r], [r_hts[b][gi_]])

            def stage_a(hT, i, tt):
                cs = slice(i * 128, (i + 1) * 128)
                for (bank, o0, c0, m) in ((4, 0, 128, 384), (5, 0, 512, 256), (6, 0, 800, 512)):
                    for k in range(8):
                        S.mm(PS[bank][:, o0:o0 + m], hT.t[:, k, cs], wa.t[:, k, c0:c0 + m], k == 0, k == 7,
                             [hT.r, wa_r[k]], [PR[bank]])
                S.mm(PS[5][:, 256:512], codes.t[0:33, cs], W2.t[0:33, :], True, True, [codes.r, W2.r], [PR[5]])
                st_ = {}
                e_sb, sp_, zf, ktok = e_sbR.next(), spR.next(), zfR.next(), ktokR.next()
                S.act(e_sb.t[:, :], PS[5][:, 256:512], AF.Exp, [PR[5]], [e_sb.r], scale=-1.0)
                S.act(sp_.t[:, :], e_sb.t[:, :], AF.Ln, [e_sb.r], [sp_.r], bias=1.0)
                S.cp("dve", ktok.t[:, :], PS[4][:, 0:128], [PR[4]], [ktok.r])
                S.cp("dve", vst.t[:, tt, :], PS[4][:, 128:384], [PR[4]], [r_v[tt]])
                sg_ = sgR.next()
                S.act(sg_.t[:, :], PS[5][:, 0:256], AF.Exp, [PR[5]], [sg_.r], scale=-1.0)
                S.act(sg_.t[:, :], sg_.t[:, :], AF.Ln, [sg_.r], [sg_.r], bias=1.0)
                S.act(sg_.t[:, :], sg_.t[:, :], AF.Exp, [sg_.r], [sg_.r], scale=-1.0)
                S.tt("dve", sog.t[:, tt, :], PS[5][:, 0:256], sg_.t[:, :], ALU.mult, [PR[5], sg_.r], [r_sog[tt]])
                S.act(zf.t[:, :], PS[6][:, :], AF.Gelu, [PR[6]], [zf.r])
                bnst, bnag, vn, vg = bnstR.next(), bnagR.next(), vnR.next(), vgR.next()
                S.op("dve", (lambda a, b_: (lambda e: e.bn_stats(a, b_)))(bnst.t[:, :], zf.t[:, 256:512]), [zf.r], [bnst.r])
                S.op("dve", (lambda a, b_: (lambda e: e.bn_aggr(a, b_)))(bnag.t[:, :], bnst.t[:, :]), [bnst.r], [bnag.r])
                rs = rstd_from_ss(bnag.t[:, 1:2], 1.0, 1e-5, [bnag.r])
                S.ts("dve", vn.t[:, :], zf.t[:, 256:512], bnag.t[:, 0:1], rs.t[:, 0:1], ALU.subtract, ALU.mult,
                     [zf.r, bnag.r, rs.r], [vn.r])
                S.tt("pool", vn.t[:, :], vn.t[:, :], fv["gmlp_ln_g"].t[:, :], ALU.mult, [vn.r, fv["gmlp_ln_g"].r], [vn.r])
                S.tt("pool", vg.t[:, :], vn.t[:, :], fv["gmlp_ln_b"].t[:, :], ALU.add, [vn.r, fv["gmlp_ln_b"].r], [vg.r])
                return dict(sp=sp_, zf=zf, ktok=ktok, vg=vg, cs=cs, tt=tt)

            def stage_b(st_):
                sp_, zf, ktok, vg, cs, tt = st_["sp"], st_["zf"], st_["ktok"], st_["vg"], st_["cs"], st_["tt"]
                tk = slice(tt * 128, (tt + 1) * 128)
                S.mm(PS[7][:, 0:128], sp_.t[:, 0:128], cf["triF"].t[:, :], True, True, [sp_.r, cf["triF"].r], [PR[7]])
                S.mm(PS[7][:, 128:256], sp_.t[:, 128:256], cf["triB"].t[:, :], True, True, [sp_.r, cf["triB"].r], [PR[7]])
                S.mm(PS[7][:, 256:384], cf["uF"].t[:, :], sp_.t[:, 0:128], True, True, [sp_.r, cf["uF"].r], [PR[7]])
                S.mm(PS[7][:, 384:512], cf["uB"].t[:, :], sp_.t[:, 128:256], True, True, [sp_.r, cf["uB"].r], [PR[7]])
                for g in range(4):
                    S.mm(PS[3][:, g * 64:(g + 1) * 64], wsT.t[:, g, :], vg.t[:, g * 64:(g + 1) * 64], True, True,
                         [wsT.r, vg.r], [PR[3]])
                e1, e2, er = e1R.next(), e2R.next(), erR.next()
                S.act(e1.t[:, :], PS[7][:, 0:256], AF.Exp, [PR[7]], [e1.r], scale=-1.0)
                S.act(e2.t[:, :], PS[7][:, 0:256], AF.Exp, [PR[7]], [e2.r])
                S.act(er.t[:, :], PS[7][:, 256:512], AF.Exp, [PR[7]], [er.r], scale=-1.0)
                e1v = e1.t[:, :].rearrange("p (d t) -> p d t", d=2)
                e2v = e2.t[:, :].rearrange("p (d t) -> p d t", d=2)
                erv = er.t[:, :].rearrange("p (d t) -> p d t", d=2)
                S.stt("dve", qst.t[:, :, tk], e1v, cq, PS[1][:, cs].unsqueeze(1).to_broadcast([128, 2, 128]),
                      ALU.mult, ALU.mult, [e1.r, PR[1]], [r_q[tt]])
                S.tt("dve", kst.t[:, :, tk], e2v, PS[2][:, cs].unsqueeze(1).to_broadcast([128, 2, 128]), ALU.mult,
                     [e2.r, PR[2]], [r_k[tt]])
                S.tt("dve", kpst.t[:, tt, :, :], erv, ktok.t[:, :].unsqueeze(1).to_broadcast([128, 2, 128]), ALU.mult,
                     [er.r, ktok.r], [r_kp[tt]])
                S.cp("dve", dst_.t[:, 0, 2 * tt:2 * tt + 2], e1.t[:, 63:128:64], [e1.r], [r_d[tt]])
                S.cp("dve", dst_.t[:, 1, 2 * tt:2 * tt + 2], e1.t[:, 128:256:64], [e1.r], [r_d[tt]])
                gout = goutR.next()
                S.tt("dve", gout.t[:, :].rearrange("p (g c) -> p g c", g=4), PS[3][:, 0:256].rearrange("p (g c) -> p g c", g=4),
                     bsT.t[:, :].unsqueeze(2).to_broadcast([128, 4, 64]), ALU.add, [PR[3], bsT.r], [gout.r])
                S.tt("dve", gout.t[:, :], gout.t[:, :], zf.t[:, 0:256], ALU.mult, [gout.r, zf.r], [gout.r])
                ss = sumsq(gout.t[:, :], [gout.r], 256)
                rs2 = rstd_from_ss(ss.t[:, 0:1], 256.0, eps6, [ss.r])
                S.stt("dve", cat.t[:, tt, 256:512], gout.t[:, :], rs2.t[:, 0:1], fv["gmlp_out_g"].t[:, :], ALU.mult, ALU.mult,
                      [gout.r, rs2.r, fv["gmlp_out_g"].r], [catr[tt]])

            p1_tr(p1_elem(groups[0]), hTs[0], 0)
            for gi, grp in enumerate(groups):
                n = len(grp) * 128
                hT = hTs[gi % 2]
                hs_next = p1_elem(groups[gi + 1]) if gi + 1 < len(groups) else None
                for (bank, c0, m) in ((3, 768, 32), (1, 0, 128), (2, 128, 128)):
                    for k in range(8):
                        S.mm(PS[bank][0:m, 0:n], wa.t[:, k, c0:c0 + m], hT.t[:, k, 0:n], k == 0, k == 7,
                             [wa_r[k], hT.r], [PR[bank]])
                    if bank == 3:
                        S.cp("act", codes.t[0:32, 0:n], PS[3][0:32, 0:n], [PR[3]], [codes.r])
                pend = None
                for i, tt in enumerate(grp):
                    st_ = stage_a(hT, i, tt)
                    if i == 0 and hs_next is not None:
                        p1_tr(hs_next, hTs[(gi + 1) % 2], gi + 1)
                    if pend is not None:
                        stage_b(pend)
                    pend = st_
                stage_b(pend)
            S.barrier()
            A.release(mP1)

            wb_, wb_r = w_load("wb", l)
            ofs = T(A, [128, NT, 256], F32, "ofs")
            ofs_r = [Res() for _ in range(NT)]
            osqR = Ring(A, 2, [128, 256], F32, "osq")
            osnR = Ring(A, 2, [128, 256], F32, "osn")
            arrived = [False] * NT
            dirs = []
            for dr in range(2):
                dd = dict(dr=dr, Sst=Ring(A, 2, [128, 256], F32, "Sst"), Sbd=Ring(A, 3, [128, 256], BF16, "Sbd"),
                          Qbd=Ring(A, 2, [128, 4, 128], BF16, "Qbd"), att=Ring(A, 2, [128, 4, 128], BF16, "att"),
                          order=list(range(NT)) if dr == 0 else [1, 0] + list(range(NT - 1, 1, -1)),
                          mk=cf["mF"] if dr == 0 else cf["mB"], kvb=(0, 1) if dr == 0 else (4, 5), ab=2 + dr, ob=6 + dr)
                dd["Scur"] = dd["Sst"].next()
                S.memset("dve", dd["Scur"].t[:, :], 0.0, [dd["Scur"].r])
                dd["Sb0"] = dd["Sbd"].next()
                S.memset("dve", dd["Sb0"].t[:, :], 0.0, [dd["Sb0"].r])
                dirs.append(dd)

            def gla_front(dd, tt):
                dr = dd["dr"]
                tk0 = tt * 128
                chunks = (0, 1) if dr == 0 else (1, 0)
                need_out = need_ctx or tt >= 2
                abank = dd["ab"]
                Sbs = [dd["Sb0"]]
                if need_out:
                    qb = dd["Qbd"].next()
                    for h in range(4):
                        S.act(qb.t[:, h, :], qst.t[:, dr, tk0:tk0 + 128], AF.Copy, [r_q[tt], bmf.r], [qb.r], scale=bmf.t[:, h:h + 1])
                for ci, c in enumerate(chunks):
                    rows = slice(c * 64, (c + 1) * 64)
                    kvbank = dd["kvb"][ci]
                    S.mm(PS[kvbank][:, 0:256], kpst.t[rows, tt, dr, :], vst.t[rows, tt, :], True, True,
                         [r_kp[tt], r_v[tt]], [PR[kvbank]])
                if need_out:
                    S.mm(PS[abank][:, :], kst.t[:, dr, tk0:tk0 + 128], qb.t[:, :, :].rearrange("p h t -> p (h t)"), True, True,
                         [r_k[tt], qb.r], [PR[abank]])
                for ci, c in enumerate(chunks):
                    kvbank = dd["kvb"][ci]
                    Sn = dd["Sst"].next()
                    ch = 2 * tt + c
                    S.stt("dve", Sn.t[:, :], dd["Scur"].t[:, :], dst_.t[:, dr, ch:ch + 1], PS[kvbank][:, 0:256], ALU.mult, ALU.add,
                          [dd["Scur"].r, r_d[tt], PR[kvbank]], [Sn.r])
                    dd["Scur"] = Sn
                    Sbn = dd["Sbd"].next()
                    S.tt("pool", Sbn.t[:, :], Sn.t[:, :], smask.t[:, :], ALU.mult, [Sn.r, smask.r], [Sbn.r])
                    Sbs.append(Sbn)
                dd["Sb0"] = Sbs[2]
                if not need_out:
                    return None
                at = dd["att"].next()
                S.tt("dve", at.t[:, :, :], PS[abank][:, :].rearrange("p (h t) -> p h t", h=4),
                     dd["mk"].t[:, :].unsqueeze(1).to_broadcast([128, 4, 128]), ALU.mult, [PR[abank], dd["mk"].r], [at.r])
                return (tt, Sbs, at)

            def gla_back(dd, fr):
                if fr is None:
                    return
                tt, Sbs, at = fr
                dr = dd["dr"]
                tk0 = tt * 128
                chunks = (0, 1) if dr == 0 else (1, 0)
                obank = dd["ob"]
                for cj, c2 in enumerate(chunks):
                    r2 = slice(c2 * 64, (c2 + 1) * 64)
                    kw = {"tile_position": (0, 64)} if c2 == 1 else {}
                    S.mm(PS[obank][r2, 0:256], qst.t[:, dr, tk0 + c2 * 64:tk0 + (c2 + 1) * 64], Sbs[cj].t[:, :],
                         True, False, [r_q[tt], Sbs[cj].r], [PR[obank]], skip_group_check=True, **kw)
                for h in range(4):
                    S.mm(PS[obank][:, h * 64:(h + 1) * 64], at.t[:, h, :], vst.t[:, tt, h * 64:(h + 1) * 64], False, h == 3,
                         [at.r, r_v[tt]], [PR[obank]], skip_group_check=True)
                if not arrived[tt]:
                    arrived[tt] = True
                    S.cp("act", ofs.t[:, tt, :], PS[obank][:, 0:256], [PR[obank]], [ofs_r[tt]])
                else:
                    S.tt("dve", ofs.t[:, tt, :], PS[obank][:, 0:256], ofs.t[:, tt, :], ALU.add, [PR[obank], ofs_r[tt]], [ofs_r[tt]])

            pend_g = [None, None]
            for stp in range(NT):
                for di, dd in enumerate(dirs):
                    gla_back(dd, pend_g[di])
                    pend_g[di] = gla_front(dd, dd["order"][stp])
            for di, dd in enumerate(dirs):
                gla_back(dd, pend_g[di])
            for tt in tiles_all:
                osq, osn = osqR.next(), osnR.next()
                S.tt("pool", osq.t[:, :], ofs.t[:, tt, :], ofs.t[:, tt, :], ALU.mult, [ofs_r[tt]], [osq.r])
                s4 = stat.next()
                S.reduce(s4.t[:, 0:4], osq.t[:, :].rearrange("p (h d) -> p h d", h=4), ALU.add, [osq.r], [s4.r])
                r4 = rstd_from_ss(s4.t[:, 0:4], 64.0, eps6, [s4.r])
                S.tt("dve", osn.t[:, :].rearrange("p (h d) -> p h d", h=4), ofs.t[:, tt, :].rearrange("p (h d) -> p h d", h=4),
                     r4.t[:, 0:4].unsqueeze(2).to_broadcast([128, 4, 64]), ALU.mult, [ofs_r[tt], r4.r], [osn.r])
                S.tt("pool", osn.t[:, :], osn.t[:, :], fv["gnorm4"].t[:, :], ALU.mult, [osn.r, fv["gnorm4"].r], [osn.r])
                S.tt("dve", cat.t[:, tt, 0:256], osn.t[:, :], sog.t[:, tt, :], ALU.mult, [osn.r, r_sog[tt]], [catr[tt]])
            S.barrier()
            A.release(mG)

            mS = A.mark()
            cosT = T(A, [128, 2048], F32, "cosT")
            sinT = T(A, [128, 2048], F32, "sinT")
            S.dma("sp", cosT.t[:, :], Cd["cosT"][:, :], [], [cosT.r])
            S.dma("sp", sinT.t[:, :], Cd["sinT"][:, :], [], [sinT.r])
            qs = T(A, [128, 4, 2048], BF16, "qs")
            qc = T(A, [128, 4, 256], BF16, "qc")
            ks = T(A, [128, NT * 128], BF16, "ks")
            va = T(A, [128, NT, 2, 65], BF16, "va")
            S.memset("pool", va.t[:, :, :, :].rearrange("p a b c -> p (a b c)"), 1.0, [va.r])
            grp_r = [Res() for _ in range(5)]
            mP2 = A.mark()
            hTs = [T(A, [128, 8, 512], BF16, "hT2") for _ in range(2)]
            rt = Ring(A, 2, [128, 512], F32, "ropetmp")

            def p2_load(gi_):
                n_ = len(groups[gi_]) * 128
                S.dma("sp", hTs[gi_ % 2].t[:, :, 0:n_], hts[b, gi_, :, :, 0:n_], [r_hts[b][gi_]], [hTs[gi_ % 2].r])

            p2_load(0)
            for gi, grp in enumerate(groups):
                n = len(grp) * 128
                is_ctx = gi == 0
                hT = hTs[gi % 2]
                if gi + 1 < len(groups):
                    p2_load(gi + 1)
                p0 = (grp[0] - 2) * 128
                units = []
                if not (is_ctx and not need_ctx):
                    units += [("q", j) for j in range(4)]
                units.append(("k", 0))
                for ui, (kind, j) in enumerate(units):
                    c0 = j * 128 if kind == "q" else 512
                    cp0 = 768 + j * 128 if kind == "q" else 1280
                    b1, b2 = 1 + 2 * (ui % 2), 2 + 2 * (ui % 2)
                    for k in range(8):
                        S.mm(PS[b1][:, 0:n], wb_.t[:, k, c0:c0 + 128], hT.t[:, k, 0:n], k == 0, k == 7, [wb_r[k], hT.r], [PR[b1]])
                    if is_ctx:
                        dst = qc.t[:, j, :] if kind == "q" else ks.t[:, 0:256]
                        S.cp("act", dst, PS[b1][:, 0:n], [PR[b1]], [grp_r[gi]])
                        continue
                    for k in range(8):
                        S.mm(PS[b2][:, 0:n], wb_.t[:, k, cp0:cp0 + 128], hT.t[:, k, 0:n], k == 0, k == 7, [wb_r[k], hT.r], [PR[b2]])
                    t1, t2 = rt.next(), rt.next()
                    S.tt("dve", t1.t[:, :], PS[b1][:, :], cosT.t[:, p0:p0 + 512], ALU.mult, [PR[b1], cosT.r], [t1.r])
                    S.tt("dve", t2.t[:, :], PS[b2][:, :], sinT.t[:, p0:p0 + 512], ALU.mult, [PR[b2], sinT.r], [t2.r])
                    dst = qs.t[:, j, p0:p0 + 512] if kind == "q" else ks.t[:, 256 + p0:256 + p0 + 512]
                    S.tt("pool", dst, t1.t[:, :], t2.t[:, :], ALU.add, [t1.r, t2.r], [grp_r[gi]])
                for i, tt in enumerate(grp):
                    cs = slice(i * 128, (i + 1) * 128)
                    vb = 5 + (i % 2)
                    for k in range(8):
                        S.mm(PS[vb][:, 0:128], hT.t[:, k, cs], wb_.t[:, k, 640:768], k == 0, k == 7, [hT.r, wb_r[k]], [PR[vb]])
                    S.cp("act", va.t[:, tt, :, 0:64], PS[vb][:, 0:128].rearrange("p (g d) -> p g d", g=2), [PR[vb], va.r], [grp_r[gi]])
            S.barrier()
            A.release(mP2)

            wo, wo_r = w_load("wo", l)
            pT = Ring(A, 4, [128, 4, 128], BF16, "pT")
            csos = Ring(A, 2, [128, 512], F32, "cso")
            den = Ring(A, 2, [128, 4], F32, "den")
            all_r = grp_r + [va.r]
            qblocks = ([("c", 0), ("c", 1)] if need_ctx else []) + [("l", n_) for n_ in range(16)]
            units = []
            for (qk, n_) in qblocks:
                tt = n_ if qk == "c" else 2 + n_
                cso = csos.next()
                for g in range(2):
                    pr = slice(64 * g, 64 * g + 64)
                    if qk == "c":
                        qap = qc.t[pr, :, n_ * 128:(n_ + 1) * 128]
                        keys = [(0, None), (1, None)]
                    else:
                        qap = qs.t[pr, :, n_ * 128:(n_ + 1) * 128]
                        keys = [(0, None), (1, None)]
                        if n_ > 0:
                            keys.append((2 + n_ - 1, "mP"))
                        keys.append((2 + n_, None))
                        if n_ < 15:
                            keys.append((2 + n_ + 1, "mN"))
                    units.append((tt, g, pr, qap, keys, cso))
            steps = [(ui, ki) for ui, u in enumerate(units) for ki in range(len(u[4]))]
            LOOK = 2

            def swa_score(si):
                ui, ki = steps[si]
                tt, g, pr, qap, keys, cso = units[ui]
                kt = keys[ki][0]
                sb = 1 + si % 4
                S.mm(PS[sb][:, :].rearrange("p (r q) -> p r q", r=4), ks.t[pr, kt * 128:(kt + 1) * 128], qap, True, True, all_r, [PR[sb]])

            def swa_rest(si):
                ui, ki = steps[si]
                tt, g, pr, qap, keys, cso = units[ui]
                kt, mname = keys[ki]
                sb = 1 + si % 4
                ob = 6 + (ui % 2)
                p = pT.next()
                S.act(p.t[:, :, :], PS[sb][:, :].rearrange("p (r q) -> p r q", r=4), AF.Exp, [PR[sb]], [p.r], scale=0.125)
                if mname is not None:
                    S.tt("dve", p.t[:, :, :], p.t[:, :, :], cf[mname].t[:, :].unsqueeze(1).to_broadcast([128, 4, 128]), ALU.mult,
                         [p.r, cf[mname].r], [p.r])
                for r_ in range(4):
                    S.mm(PS[ob][:, r_ * 65:(r_ + 1) * 65], p.t[:, r_, :], va.t[:, kt, g, :], ki == 0 and r_ == 0,
                         ki == len(keys) - 1 and r_ == 3, [p.r] + all_r, [PR[ob]], skip_group_check=True)
                if ki < len(keys) - 1:
                    return
                ov = PS[ob][:, 0:260].rearrange("p (r d) -> p r d", r=4)
                dn = den.next()
                S.tt("dve", dn.t[:, :], ov[:, :, 64], esink.t[:, 4 * g:4 * g + 4], ALU.add, [PR[ob], esink.r], [dn.r])
                S.recip(dn.t[:, :], dn.t[:, :], [dn.r], [dn.r])
                S.tt("dve", cso.t[:, 256 * g:256 * g + 256].rearrange("p (r d) -> p r d", r=4), ov[:, :, 0:64],
                     dn.t[:, :].unsqueeze(2).to_broadcast([128, 4, 64]), ALU.mult, [PR[ob], dn.r], [cso.r])
                if g == 1:
                    ss = sumsq(cso.t[:, :], [cso.r], 512)
                    rs = rstd_from_ss(ss.t[:, 0:1], 512.0, eps6, [ss.r])
                    S.stt("dve", cat.t[:, tt, 512:1024], cso.t[:, :], rs.t[:, 0:1], fv["swa_out_g"].t[:, :], ALU.mult, ALU.mult,
                          [cso.r, rs.r, fv["swa_out_g"].r], [catr[tt]])

            for si in range(len(steps) + LOOK):
                if si < len(steps):
                    swa_score(si)
                if si >= LOOK:
                    swa_rest(si - LOOK)
            S.barrier()
            A.release(mS)

            mO = A.mark()
            A.top = mB_
            sgs = [tiles_all[i:i + 6] for i in range(0, len(tiles_all), 6)]
            TM = max(len(s_) for s_ in sgs) * 128
            actT = T(A, [128, 22, TM], BF16, "actT")
            wd = T(A, [128, 22, D], BF16, "w_down")
            wd_r = [Res() for _ in range(22)]
            xring_n = Ring(A, 2, [128, D], F32, "xt4n")
            xring_r = Ring(A, 2, [128, D], F32, "xt4r")
            oring4 = Ring(A, 2, [128, D], F32, "xo4")
            sil = Ring(A, 2, [128, 512], F32, "sil")
            A.top = max(A.top, mO + 40 * 1024)
            early_base = A.top
            A2 = [T(A, [128, D], F32, "A2") for _ in range(2)]
            B2 = [T(A, [128, D], F32, "B2") for _ in range(2)]
            C2 = [T(A, [128, D], F32, "C2") for _ in range(2)]
            for si, j in ((0, 2), (1, b)):
                load_bcast(A2[si], modv[l, 3, j:j + 1, :])
                load_bcast(B2[si], modv[l, 4, j:j + 1, :])
                load_bcast(C2[si], modv[l, 5, j:j + 1, :])
            hT2 = T(A, [128, 8, TM], BF16, "hTf")
            hring4 = {"tmp": Ring(A, 2, [128, D], F32, "ntmp4"), "h": Ring(A, 3, [128, D], BF16, "h4")}
            ffn_top = A.top
            A.top = mO
            C1 = [T(A, [128, D], F32, "C1") for _ in range(2)]
            for si, j in ((0, 2), (1, b)):
                load_bcast(C1[si], modv[l, 2, j:j + 1, :])
            catT = Ring(A, 2, [128, 8, 128], BF16, "catT")
            xring = Ring(A, 2, [128, D], F32, "xt3")
            oring = Ring(A, 2, [128, D], F32, "xo3")
            pend_tr = None
            for ti, tt in enumerate(tiles_all):
                if debug:
                    S.dma("sp", dbg_cat[b, tt * 128:(tt + 1) * 128, :], cat.t[:, tt, :], [catr[tt]], [])
                trb = ti % 2
                pbf = PS[trb][:, :].bitcast(BF16)
                for k in range(8):
                    S.tr(pbf[:, k * 128:(k + 1) * 128], cat.t[:, tt, k * 128:(k + 1) * 128], ident.t[:, :], [catr[tt], ident.r], [PR[trb]])
                ct = catT.next()
                S.cp("act", ct.t[:, :, :], pbf[:, :].rearrange("p (k t) -> p k t", k=8), [PR[trb]], [ct.r])
                yb = (2 + 2 * (ti % 2), 3 + 2 * (ti % 2))
                for hf in range(2):
                    for k in range(8):
                        S.mm(PS[yb[hf]][:, :], ct.t[:, k, :], wo.t[:, k, hf * 512:(hf + 1) * 512], k == 0, k == 7, [ct.r, wo_r[k]], [PR[yb[hf]]])
                xt = load_x(tt, xring)
                si = 0 if tt < 2 else 1
                o1 = post_residual(yb, xt, C1[si], xs1[b, tt * 128:(tt + 1) * 128, :], r_xs1[b][tt], oring)
                if pend_tr is not None:
                    norm_tr(pend_tr[0], hT2.t, hT2.r, pend_tr[1] * 128, 6 + (pend_tr[1] % 2))
                    pend_tr = None
                if ti < len(sgs[0]):
                    pend_tr = (norm_elem(o1, A2[si], B2[si], hring4, add_eng="pool"), ti)
            if pend_tr is not None:
                norm_tr(pend_tr[0], hT2.t, hT2.r, pend_tr[1] * 128, 6 + (pend_tr[1] % 2))
            assert A.top <= early_base
            S.barrier()
            A.top = ffn_top

            wgb, wub = w_rings()
            oring = oring4
            hring = hring4
            wgu = Wd["ffn_w_gu"][l].rearrange("(k p) n -> p k n", p=128)
            wdn = Wd["ffn_w_down"][l].rearrange("(f p) n -> p f n", p=128)

            def load_x1(tt, ring):
                xt = ring.next()
                S.dma("sp", xt.t[:, :], xs1[b, tt * 128:(tt + 1) * 128, :], [r_xs1[b][tt]], [xt.r])
                return xt

            def ffn_elem(tt):
                xt = load_x1(tt, xring_n)
                si_ = 0 if tt < 2 else 1
                return norm_elem(xt, A2[si_], B2[si_], hring, add_eng="dve")

            for sgi, sg in enumerate(sgs):
                ntok = len(sg) * 128
                nxt = sgs[sgi + 1] if sgi + 1 < len(sgs) else []
                chunks = [(c0, min(512, ntok - c0)) for c0 in range(0, ntok, 512)]
                ui = 0
                assert len(nxt) <= len(sg)
                blocks = {}

                def issue_gu(cb_):
                    wg_, wu_ = wgb.next(), wub.next()
                    S.dma("pool", wg_.t[:, :, :], wgu[:, :, cb_ * 256:(cb_ + 1) * 256], [], [wg_.r])
                    S.dma("pool", wu_.t[:, :, :], wgu[:, :, DFF + cb_ * 256:DFF + (cb_ + 1) * 256], [], [wu_.r])
                    blocks[cb_] = (wg_, wu_)

                issue_gu(0)
                for cb in range(11):
                    if cb + 1 < 11:
                        issue_gu(cb + 1)
                    wg, wu = blocks[cb]
                    for f in (2 * cb, 2 * cb + 1):
                        S.dma("pool", wd.t[:, f, :], wdn[:, f, :], [], [wd_r[f]])
                    for fs in range(2):
                        fb = cb * 2 + fs
                        for (c0, cn) in chunks:
                            bg, bu = 2 + 2 * (ui % 2), 3 + 2 * (ui % 2)
                            ui += 1
                            for k in range(8):
                                S.mm(PS[bg][:, 0:cn], wg.t[:, k, fs * 128:(fs + 1) * 128], hT2.t[:, k, c0:c0 + cn], k == 0, k == 7,
                                     [wg.r, hT2.r], [PR[bg]])
                            for k in range(8):
                                S.mm(PS[bu][:, 0:cn], wu.t[:, k, fs * 128:(fs + 1) * 128], hT2.t[:, k, c0:c0 + cn], k == 0, k == 7,
                                     [wu.r, hT2.r], [PR[bu]])
                            sl = sil.next()
                            S.act(sl.t[:, 0:cn], PS[bg][:, 0:cn], AF.Exp, [PR[bg]], [sl.r], scale=-1.0)
                            S.act(sl.t[:, 0:cn], sl.t[:, 0:cn], AF.Ln, [sl.r], [sl.r], bias=1.0)
                            S.act(sl.t[:, 0:cn], sl.t[:, 0:cn], AF.Exp, [sl.r], [sl.r], scale=-1.0)
                            S.tt("dve", sl.t[:, 0:cn], sl.t[:, 0:cn], PS[bg][:, 0:cn], ALU.mult, [sl.r, PR[bg]], [sl.r])
                            S.tt("dve", actT.t[:, fb, c0:c0 + cn], sl.t[:, 0:cn], PS[bu][:, 0:cn], ALU.mult, [sl.r, PR[bu]], [actT.r])
                if sgi == len(sgs) - 1:
                    lnext = l if b + 1 < nb else (l + 1 if l + 1 < nlayers else None)
                    if lnext is not None:
                        pre["wa"] = w_load("wa", lnext)
                hq = [ffn_elem(nxt[0])] if len(nxt) > 0 else []
                for i, tt in enumerate(sg):
                    if i + 1 < len(nxt):
                        hq.append(ffn_elem(nxt[i + 1]))
                    hn = hq[i] if i < len(nxt) else None
                    yb = (2 + 2 * (i % 2), 3 + 2 * (i % 2))
                    for hf in range(2):
                        for f in range(22):
                            S.mm(PS[yb[hf]][:, :], actT.t[:, f, i * 128:(i + 1) * 128], wd.t[:, f, hf * 512:(hf + 1) * 512], f == 0, f == 21,
                                 [actT.r, wd_r[f]], [PR[yb[hf]]])
                    if hn is not None:
                        norm_tr(hn, hT2.t, hT2.r, i * 128, i % 2)
                    xt = load_x1(tt, xring_r)
                    si = 0 if tt < 2 else 1
                    if last and tt >= 2:
                        dst_ap, dst_res = out[b, (tt - 2) * 128:(tt - 1) * 128, :], Res()
                    else:
                        dst_ap, dst_res = xs2[b, tt * 128:(tt + 1) * 128, :], r_xs2[b][tt]
                    post_residual(yb, xt, C2[si], dst_ap, dst_res, oring)
            S.barrier()
            A.release(mB_)
        S.barrier()
        A.release(mL)
    print("ops:", {e: len(v) for e, v in S.ops.items()}, "peak sbuf", A.peak)
    S.emit()
    return nc


_CACHE = {}


def _core_inputs(inp, core, nb, W, C):
    b0 = core * nb
    x = np.asarray(inp["x"], np.float32)
    ctx = np.asarray(inp["ctx"], np.float32)
    c = np.asarray(inp["c"], np.float32)
    c_ctx = np.asarray(inp["c_ctx"], np.float32)
    m = {}
    m["xin"] = np.ascontiguousarray(np.concatenate([ctx[b0:b0 + nb], x[b0:b0 + nb]], axis=1))
    cc = np.stack([c[b0 + (j % nb)] for j in range(2)] + [c_ctx], 0)
    m["ccT"] = np.ascontiguousarray(cc.reshape(3, 8, 128).transpose(2, 1, 0))
    m.update(W)
    for k, v in C.items():
        m["c_" + k] = v
    return m


def kernel(**inp):
    nb = 2
    if "nc" not in _CACHE:
        _CACHE["nc"] = build(nb=nb, nlayers=2)
    nc = _CACHE["nc"]
    W = _prep_weights(inp)
    C = _consts()
    in_maps = [_core_inputs(inp, core, nb, W, C) for core in range(NCORES)]
    res = run_bass_kernel_spmd(nc, in_maps, core_ids=list(range(NCORES)))
    outs = [np.asarray(r["out"], np.float32) for r in res.results]
    return np.concatenate(outs, axis=0)
```

```python
import numpy as np
import ml_dtypes
import concourse.bass as bass
import concourse.mybir as mybir
from concourse.bass_utils import run_bass_kernel_spmd

F32 = mybir.dt.float32
BF16 = mybir.dt.bfloat16
ALU = mybir.AluOpType
AF = mybir.ActivationFunctionType
AX = mybir.AxisListType

D = 1024
NT = 18
DFF = 2816
NCORES = 8


class Res:
    __slots__ = ("name", "w", "r")

    def __init__(self, name=""):
        self.name = name
        self.w = None
        self.r = {}


class Op:
    __slots__ = ("eng", "fn", "deps", "is_dma", "sem", "val", "needs_inc", "ring_wait")

    def __init__(self, eng, fn, is_dma):
        self.eng = eng
        self.fn = fn
        self.deps = []
        self.is_dma = is_dma
        self.sem = None
        self.val = 0
        self.needs_inc = False
        self.ring_wait = None


class Sched:
    ENGS = ("pe", "dve", "act", "pool", "sp")
    RING = 12

    def __init__(self, nc):
        self.nc = nc
        self.ops = {e: [] for e in self.ENGS}
        self.dma_count = {e: 0 for e in self.ENGS}
        self.pending_barrier = {e: [] for e in self.ENGS}
        self.all_dma_since_barrier = []
        self.n_ops = 0

    def _dep(self, op, p, kind):
        if p is None or p is op:
            return
        if (not p.is_dma) and p.eng == op.eng and not op.is_dma and p.eng == "pe":
            return
        op.deps.append(p)
        if not p.is_dma:
            p.needs_inc = True

    def op(self, eng, fn, reads=(), writes=(), dma=False):
        o = Op(eng, fn, dma)
        for b in self.pending_barrier[eng]:
            self._dep(o, b, "raw")
        self.pending_barrier[eng] = []
        for r in reads:
            self._dep(o, r.w, "raw")
        for w in writes:
            self._dep(o, w.w, "waw")
            for rd in w.r.values():
                self._dep(o, rd, "war")
        for r in reads:
            if dma:
                r.r[("dma", id(o))] = o
            else:
                r.r[eng] = o
        for w in writes:
            w.w = o
            w.r = {}
        if dma:
            i = self.dma_count[eng]
            self.dma_count[eng] = i + 1
            o.sem = (eng, i % self.RING)
            o.val = 16 * (i // self.RING + 1)
            if i >= self.RING:
                o.ring_wait = (o.sem, o.val - 16)
            self.all_dma_since_barrier.append(o)
        self.ops[eng].append(o)
        self.n_ops += 1
        return o

    def barrier(self):
        lasts = []
        for e in self.ENGS:
            for o in reversed(self.ops[e]):
                if not o.is_dma:
                    lasts.append(o)
                    break
        lasts += self.all_dma_since_barrier
        self.all_dma_since_barrier = []
        for e in self.ENGS:
            self.pending_barrier[e] = list(lasts)

    def mm(self, out, lhsT, rhs, start, stop, reads, writes, **kw):
        return self.op("pe", lambda e: e.matmul(out, lhsT, rhs, start=start, stop=stop, **kw), reads, writes)

    def tr(self, out, in_, ident, reads, writes):
        return self.op("pe", lambda e: e.transpose(out, in_, ident), reads, writes)

    def dma(self, eng, out, in_, reads, writes):
        return self.op(eng, lambda e: e.dma_start(out=out, in_=in_), reads, writes, dma=True)

    def act(self, out, in_, func, reads, writes, **kw):
        return self.op("act", lambda e: e.activation(out, in_, func, **kw), reads, writes)

    def tt(self, eng, out, in0, in1, op, reads, writes):
        return self.op(eng, lambda e: e.tensor_tensor(out, in0, in1, op), reads, writes)

    def ts(self, eng, out, in0, s1, s2, op0, op1, reads, writes):
        return self.op(eng, lambda e: e.tensor_scalar(out, in0, s1, s2, op0, op1), reads, writes)

    def stt(self, eng, out, in0, scalar, in1, op0, op1, reads, writes):
        return self.op(eng, lambda e: e.scalar_tensor_tensor(out, in0, scalar, in1, op0, op1), reads, writes)

    def cp(self, eng, out, in_, reads, writes):
        if eng == "act":
            return self.op(eng, lambda e: e.copy(out, in_), reads, writes)
        return self.op(eng, lambda e: e.tensor_copy(out, in_), reads, writes)

    def memset(self, eng, ap, val, writes):
        return self.op(eng, lambda e: e.memset(ap, val), [], writes)

    def recip(self, out, in_, reads, writes):
        return self.op("dve", lambda e: e.reciprocal(out, in_), reads, writes)

    def reduce(self, out, in_, op, reads, writes):
        return self.op("dve", lambda e: e.tensor_reduce(out, in_, AX.X, op), reads, writes)

    def emit(self):
        nc = self.nc
        from contextlib import ExitStack

        with ExitStack() as st:
            esem = {e: st.enter_context(nc.semaphore("s_" + e)) for e in self.ENGS if e != "sp"}
            rsem = {}
            for e in self.ENGS:
                for k in range(min(self.RING, self.dma_count[e])):
                    rsem[(e, k)] = st.enter_context(nc.semaphore("d_%s_%d" % (e, k)))
            for e in self.ENGS:
                c = 0
                for o in self.ops[e]:
                    if o.is_dma:
                        continue
                    if o.needs_inc:
                        c += 1
                        o.val = c
            block = st.enter_context(nc.Block())

            def run(ename, eng):
                seen = {}

                def wait(semkey, v):
                    if seen.get(semkey, 0) >= v:
                        return
                    seen[semkey] = v
                    h = rsem[semkey] if isinstance(semkey, tuple) else esem[semkey]
                    eng.wait_ge(h, v)

                for o in self.ops[ename]:
                    for p in o.deps:
                        if p.is_dma:
                            wait(p.sem, p.val)
                        else:
                            wait(p.eng, p.val)
                    if o.ring_wait is not None:
                        wait(o.ring_wait[0], o.ring_wait[1])
                    ins = o.fn(eng)
                    if o.is_dma:
                        ins.then_inc(rsem[o.sem], 16)
                    elif o.needs_inc:
                        ins.then_inc(esem[ename], 1)
                n = self.dma_count[ename]
                for k in range(min(self.RING, n)):
                    uses = (n - 1 - k) // self.RING + 1
                    wait((ename, k), 16 * uses)

            @block.tensor
            def _(eng):
                run("pe", eng)

            @block.vector
            def _(eng):
                run("dve", eng)

            @block.scalar
            def _(eng):
                run("act", eng)

            @block.gpsimd
            def _(eng):
                run("pool", eng)

            @block.sync
            def _(eng):
                run("sp", eng)


class Arena:
    def __init__(self, nc, base=16384, limit=229376):
        self.nc = nc
        self.top = base
        self.limit = limit
        self.n = 0
        self.peak = base

    def mark(self):
        return self.top

    def release(self, m):
        self.top = m

    def alloc(self, shape, dtype, name="t"):
        esz = 4 if dtype == F32 else 2
        per_part = esz * int(np.prod(shape[1:]))
        per_part = (per_part + 63) // 64 * 64
        off = self.top
        self.top += per_part
        self.peak = max(self.peak, self.top)
        assert self.top <= self.limit, "SBUF arena overflow %d (%s)" % (self.top, name)
        self.n += 1
        return self.nc.alloc_sbuf_tensor_at("%s_%d" % (name, self.n), list(shape), dtype, offset=off)


class T:
    def __init__(self, A, shape, dtype, name="t"):
        self.t = A.alloc(shape, dtype, name)
        self.r = Res(name)


class Ring:
    def __init__(self, A, n, shape, dtype, name="r"):
        self.items = [T(A, shape, dtype, name) for _ in range(n)]
        self.i = 0

    def next(self):
        x = self.items[self.i % len(self.items)]
        self.i += 1
        return x


def _consts():
    c = {}
    c["ident"] = np.eye(128, dtype=np.float32)
    s = np.arange(128)
    same = (s[:, None] // 64) == (s[None, :] // 64)
    le = s[:, None] <= s[None, :]
    ge = s[:, None] >= s[None, :]
    mF = (same & le).astype(np.float32)
    mB = (same & ge).astype(np.float32)
    c["mF"] = mF
    c["mB"] = mB
    c["triF"] = mF / 16.0
    c["triB"] = mB / 16.0
    c["uF"] = (same & (s[:, None] > s[None, :])).astype(np.float32) / 16.0
    c["uB"] = (same & (s[:, None] < s[None, :])).astype(np.float32) / 16.0
    c["mP"] = ge.astype(np.float32)
    c["mN"] = le.astype(np.float32)
    p = np.arange(128)
    c["bm"] = (p[:, None] // 32 == np.arange(4)[None, :]).astype(np.float32)
    c["smask"] = np.repeat(c["bm"], 64, axis=1).astype(np.float32)
    rows = 2048 // 64
    row = np.repeat(np.arange(rows), 64).astype(np.float32)
    col = np.tile(np.arange(64), rows).astype(np.float32)
    inv_freq = np.power(np.float32(10000.0), -np.arange(0, 32, 2, dtype=np.float32) / np.float32(32)).astype(np.float32)
    ang_row = (row[:, None] * inv_freq[None, :]).astype(np.float32)
    ang_col = (col[:, None] * inv_freq[None, :]).astype(np.float32)
    cosT = np.zeros((64, 2048), np.float32)
    sinT = np.zeros((64, 2048), np.float32)
    for d in range(64):
        j = d % 16
        ang = ang_row if d < 32 else ang_col
        half = (d % 32) // 16
        cosT[d] = np.cos(ang[:, j])
        sn = np.sin(ang[:, j])
        sinT[d] = -sn if half == 0 else sn
    c["cosT"] = np.concatenate([cosT, cosT], 0)
    c["sinT"] = np.concatenate([sinT, sinT], 0)
    return c


def _partner(d):
    return d + 16 if (d % 32) < 16 else d - 16


def _prep_weights(inp):
    w = {}
    w_in = np.asarray(inp["w_in"], np.float32)
    w["w_in_a"] = np.ascontiguousarray(w_in[:, :, 0:1312])
    qoff, koff, voff = 1312, 1824, 1952
    qcols, qpcols = [], []
    for j in range(4):
        for hd in (j, 4 + j):
            for d in range(64):
                qcols.append(qoff + hd * 64 + d)
                qpcols.append(qoff + hd * 64 + _partner(d))
    kcols = [koff + g * 64 + d for g in range(2) for d in range(64)]
    kpcols = [koff + g * 64 + _partner(d) for g in range(2) for d in range(64)]
    vcols = list(range(voff, voff + 128))
    cols = qcols + kcols + vcols + qpcols + kpcols
    w["w_in_b"] = np.ascontiguousarray(w_in[:, :, cols])
    w["wsT"] = np.ascontiguousarray(np.transpose(np.asarray(inp["gmlp_ws"], np.float32), (0, 1, 3, 2)))
    w["bsT"] = np.ascontiguousarray(np.transpose(np.asarray(inp["gmlp_bs"], np.float32), (0, 2, 1)))
    w["gnorm4"] = np.ascontiguousarray(np.tile(np.asarray(inp["gla_norm"], np.float32), (1, 4)))
    for k in ("mod_w", "mod_b", "n1_pre", "n1_post", "n2_pre", "n2_post", "w_out", "gla_wa2", "gla_ba",
              "gmlp_ln_g", "gmlp_ln_b", "gmlp_out_g", "swa_sink", "swa_out_g", "ffn_w_gu", "ffn_w_down"):
        w[k] = np.ascontiguousarray(np.asarray(inp[k], np.float32))
    return w


def build(nb=2, nlayers=2, debug=False):
    nc = bass.Bass("TRN2", target_bir_lowering=False)
    S = Sched(nc)
    WBASE = 229376 - 24576 - 512
    A = Arena(nc, limit=WBASE)
    C = _consts()

    def din(name, shape, dt=F32):
        return nc.dram_tensor(name, list(shape), dt, kind="ExternalInput").ap()

    kind_dbg = "ExternalOutput" if debug else "Internal"
    xin = din("xin", [nb, NT * 128, D])
    ccT = din("ccT", [128, 8, 3])
    Wd = {}
    shapes = {
        "mod_w": [2, D, 6 * D], "mod_b": [2, 6 * D], "n1_pre": [2, D], "n1_post": [2, D], "n2_pre": [2, D],
        "n2_post": [2, D], "w_in_a": [2, D, 1312], "w_in_b": [2, D, 1408], "w_out": [2, D, D],
        "gla_wa2": [2, 2, 16, 128], "gla_ba": [2, 2, 128], "gnorm4": [2, 256], "gmlp_ln_g": [2, 256],
        "gmlp_ln_b": [2, 256], "wsT": [2, 4, 128, 128], "bsT": [2, 128, 4], "gmlp_out_g": [2, 256],
        "swa_sink": [2, 8], "swa_out_g": [2, 512], "ffn_w_gu": [2, D, 2 * DFF], "ffn_w_down": [2, DFF, D],
    }
    for k, shp in shapes.items():
        Wd[k] = din(k, shp)
    Cd = {k: din("c_" + k, list(v.shape)) for k, v in C.items()}
    out = nc.dram_tensor("out", [nb, 2048, D], F32, kind="ExternalOutput").ap()
    xs1 = nc.dram_tensor("xs1", [nb, NT * 128, D], F32, kind=kind_dbg).ap()
    xs2 = nc.dram_tensor("xs2", [nb, NT * 128, D], F32, kind=kind_dbg).ap()
    modv = nc.dram_tensor("modv", [2, 6, 3, D], F32, kind=kind_dbg).ap()
    dbg_cat = nc.dram_tensor("dbg_cat", [nb, NT * 128, D], BF16, kind="ExternalOutput").ap() if debug else None
    r_xs1 = [[Res() for _ in range(NT)] for _ in range(nb)]
    r_xs2 = [[Res() for _ in range(NT)] for _ in range(nb)]
    r_modv = Res()
    hts = nc.dram_tensor("hts", [nb, 5, 128, 8, 512], BF16, kind="Internal").ap()
    r_hts = [[Res() for _ in range(5)] for _ in range(nb)]

    PS = [nc.alloc_psum_tensor("ps%d" % i, [128, 512], F32) for i in range(8)]
    PR = [Res("ps%d" % i) for i in range(8)]

    ident = T(A, [128, 128], BF16, "ident")
    S.dma("pool", ident.t[:, :], Cd["ident"][:, :], [], [ident.r])
    cf = {}
    for k in ("mF", "mB", "triF", "triB", "uF", "uB"):
        cf[k] = T(A, [128, 128], F32, k)
        S.dma("sp", cf[k].t[:, :], Cd[k][:, :], [], [cf[k].r])
    for k in ("mP", "mN"):
        cf[k] = T(A, [128, 128], BF16, k)
        S.dma("pool", cf[k].t[:, :], Cd[k][:, :], [], [cf[k].r])
    bm = T(A, [128, 4], BF16, "bm")
    S.dma("pool", bm.t[:, :], Cd["bm"][:, :], [], [bm.r])
    bmf = T(A, [128, 4], F32, "bmf")
    S.dma("sp", bmf.t[:, :], Cd["bm"][:, :], [], [bmf.r])
    smask = T(A, [128, 256], F32, "smask")
    S.dma("sp", smask.t[:, :], Cd["smask"][:, :], [], [smask.r])
    junk = T(A, [128, 1024], F32, "junk")
    eps6 = 1e-6
    epsT = {}
    for ev in (1e-6, 1e-5):
        epsT[ev] = T(A, [128, 1], F32, "eps")
        S.memset("dve", epsT[ev].t[:, :], ev, [epsT[ev].r])

    stat = Ring(A, 24, [128, 8], F32, "stat")

    def rstd_from_ss(ss, n, eps, reads):
        k = ss.shape[1]
        a = stat.next()
        S.act(a.t[:, 0:k], ss, AF.Ln, list(reads) + [epsT[eps].r], [a.r], scale=1.0 / n, bias=epsT[eps].t[:, 0:1])
        c_ = stat.next()
        S.act(c_.t[:, 0:k], a.t[:, 0:k], AF.Exp, [a.r], [c_.r], scale=-0.5)
        return c_

    def sumsq(in_ap, reads, n_free):
        a = stat.next()
        S.act(junk.t[:, 0:n_free], in_ap, AF.Square, reads, [junk.r, a.r], accum_out=a.t[:, 0:1])
        return a

    def load_bcast(dst, src_row):
        S.dma("sp", dst.t[:, :], src_row.partition_broadcast(128), [r_modv], [dst.r])

    m0 = A.mark()
    cc32 = T(A, [128, 8, 3], F32, "cc32")
    S.dma("sp", cc32.t[:, :, :], ccT[:, :, :], [], [cc32.r])
    scT = T(A, [128, 8, 3], BF16, "scT")
    S.act(scT.t[:, :, :], cc32.t[:, :, :], AF.Silu, [cc32.r], [scT.r])
    modraw = T(A, [3, 6 * D], F32, "modraw")
    biasT = T(A, [3, 6 * D], F32, "biasT")
    nrm3 = {k: T(A, [3, D], F32, k) for k in ("n1_pre", "n1_post", "n2_pre", "n2_post")}
    mwb = Ring(A, 2, [128, 8, 512], BF16, "mwb")
    mtmp = Ring(A, 2, [3, D], F32, "mtmp")
    for l in range(nlayers):
        S.dma("sp", biasT.t[:, :], Wd["mod_b"][l:l + 1, :].partition_broadcast(3), [], [biasT.r])
        for k in nrm3:
            S.dma("sp", nrm3[k].t[:, :], Wd[k][l:l + 1, :].partition_broadcast(3), [], [nrm3[k].r])
        mw = Wd["mod_w"][l].rearrange("(k p) n -> p k n", p=128)
        for blk in range(12):
            wb = mwb.next()
            S.dma("pool", wb.t[:, :, :], mw[:, :, blk * 512:(blk + 1) * 512], [], [wb.r])
            pb = blk % 2
            for k in range(8):
                S.mm(PS[pb][0:3, :], scT.t[:, k, :], wb.t[:, k, :], k == 0, k == 7, [scT.r, wb.r], [PR[pb]])
            S.tt("dve", modraw.t[:, blk * 512:(blk + 1) * 512], PS[pb][0:3, :], biasT.t[:, blk * 512:(blk + 1) * 512],
                 ALU.add, [PR[pb], biasT.r], [modraw.r])
        combos = [(1, "n1_pre", "a"), (0, None, "b"), (2, "n1_post", "c"), (4, "n2_pre", "a"), (3, None, "b"), (5, "n2_post", "c")]
        for w_, (mi, nk, kind) in enumerate(combos):
            src = modraw.t[:, mi * D:(mi + 1) * D]
            if kind == "b":
                S.dma("sp", modv[l, w_, :, :], src, [modraw.r], [r_modv])
                continue
            tmp = mtmp.next()
            if kind == "a":
                S.stt("dve", tmp.t[:, :], src, 1.0, nrm3[nk].t[:, :], ALU.add, ALU.mult, [modraw.r, nrm3[nk].r], [tmp.r])
            else:
                S.tt("dve", tmp.t[:, :], src, nrm3[nk].t[:, :], ALU.mult, [modraw.r, nrm3[nk].r], [tmp.r])
            S.dma("sp", modv[l, w_, :, :], tmp.t[:, :], [tmp.r], [r_modv])
    S.barrier()
    A.release(m0)

    def norm_elem(xt, Am, Bm, hring, add_eng="pool"):
        ss = sumsq(xt.t[:, :], [xt.r], 1024)
        rs = rstd_from_ss(ss.t[:, 0:1], 1024.0, eps6, [ss.r])
        tmp = hring["tmp"].next()
        S.stt("dve", tmp.t[:, :], xt.t[:, :], rs.t[:, 0:1], Am.t[:, :], ALU.mult, ALU.mult, [xt.r, rs.r, Am.r], [tmp.r])
        h = hring["h"].next()
        S.tt(add_eng, h.t[:, :], tmp.t[:, :], Bm.t[:, :], ALU.add, [tmp.r, Bm.r], [h.r])
        return h

    def norm_tr(h, hT, hT_res, col0, trbank):
        pbf = PS[trbank][:, :].bitcast(BF16)
        for k in range(8):
            S.tr(pbf[:, k * 128:(k + 1) * 128], h.t[:, k * 128:(k + 1) * 128], ident.t[:, :], [h.r, ident.r], [PR[trbank]])
        S.cp("dve", hT[:, :, col0:col0 + 128], pbf[:, :].rearrange("p (k t) -> p k t", k=8), [PR[trbank]], [hT_res])

    def norm_mod_T(xt, Am, Bm, hT, col0, hring, trbank):
        h = norm_elem(xt, Am, Bm, hring)
        norm_tr(h, hT.t, hT.r, col0, trbank)

    def post_residual(y_banks, xt, Cm, dst_ap, dst_res, oring):
        s0 = sumsq(PS[y_banks[0]][:, :], [PR[y_banks[0]]], 512)
        s1 = sumsq(PS[y_banks[1]][:, :], [PR[y_banks[1]]], 512)
        st = stat.next()
        S.tt("dve", st.t[:, 0:1], s0.t[:, 0:1], s1.t[:, 0:1], ALU.add, [s0.r, s1.r], [st.r])
        rs = rstd_from_ss(st.t[:, 0:1], 1024.0, eps6, [st.r])
        o = oring.next()
        for hf in range(2):
            S.stt("dve", o.t[:, hf * 512:(hf + 1) * 512], PS[y_banks[hf]][:, :], rs.t[:, 0:1], Cm.t[:, hf * 512:(hf + 1) * 512],
                  ALU.mult, ALU.mult, [PR[y_banks[hf]], rs.r, Cm.r], [o.r])
        S.tt("dve", o.t[:, :], o.t[:, :], xt.t[:, :], ALU.add, [o.r, xt.r], [o.r])
        S.dma("sp", dst_ap, o.t[:, :], [o.r], [dst_res])
        return o

    class TX:
        def __init__(self, t, name="w"):
            self.t = t
            self.r = Res(name)

    wcur = {"res": []}
    wmark = T(A, [128, 1], F32, "wmark")
    wcount = [0]
    pre = {"wa": None}

    def w_marker():
        if wcur["res"]:
            S.memset("pool", wmark.t[:, :], 0.0, [wmark.r] + list(wcur["res"]))

    def w_load(kind, l_):
        w_marker()
        wcount[0] += 1
        ncols = {"wa": 1312, "wb": 1408, "wo": D}[kind]
        src = {"wa": Wd["w_in_a"], "wb": Wd["w_in_b"], "wo": Wd["w_out"]}[kind][l_].rearrange("(k p) n -> p k n", p=128)
        t = nc.alloc_sbuf_tensor_at("W%s_%d" % (kind, wcount[0]), [128, 8, ncols], BF16, offset=WBASE)
        rs = [Res() for _ in range(8)]
        for k in range(8):
            S.dma("pool", t[:, k, :], src[:, k, :], [], [rs[k]])
        wcur["res"] = rs
        return TX(t, kind), rs

    def w_rings():
        w_marker()
        wcount[0] += 1
        items = [TX(nc.alloc_sbuf_tensor_at("Wr%d_%d" % (i, wcount[0]), [128, 8, 256], BF16, offset=WBASE + i * 4096)) for i in range(6)]
        wcur["res"] = [x.r for x in items]
        rg, ru = Ring.__new__(Ring), Ring.__new__(Ring)
        rg.items, rg.i = items[0:3], 0
        ru.items, ru.i = items[3:6], 0
        return rg, ru

    for l in range(nlayers):
        need_ctx = l < 1
        last = l == nlayers - 1
        xsrc, r_xsrc = (xin, None) if l == 0 else (xs2, r_xs2)
        tiles_all = list(range(NT)) if need_ctx else list(range(2, NT))
        mL = A.mark()
        W2 = T(A, [33, 256], F32, "W2")
        S.memset("pool", W2.t[:, :], 0.0, [W2.r])
        S.dma("sp", W2.t[0:16, 0:128], Wd["gla_wa2"][l, 0, :, :], [], [W2.r])
        S.dma("sp", W2.t[16:32, 128:256], Wd["gla_wa2"][l, 1, :, :], [W2.r], [W2.r])
        S.dma("sp", W2.t[32:33, 0:128], Wd["gla_ba"][l, 0:1, :], [W2.r], [W2.r])
        S.dma("sp", W2.t[32:33, 128:256], Wd["gla_ba"][l, 1:2, :], [W2.r], [W2.r])
        wsT = T(A, [128, 4, 128], BF16, "wsT")
        S.dma("pool", wsT.t[:, :, :], Wd["wsT"][l].rearrange("g q p -> q g p"), [], [wsT.r])
        bsT = T(A, [128, 4], F32, "bsT")
        S.dma("sp", bsT.t[:, :], Wd["bsT"][l, :, :], [], [bsT.r])
        fv = {}
        for k, n in (("gnorm4", 256), ("gmlp_ln_g", 256), ("gmlp_ln_b", 256), ("gmlp_out_g", 256), ("swa_out_g", 512)):
            fv[k] = T(A, [128, n], F32, k)
            S.dma("sp", fv[k].t[:, :], Wd[k][l:l + 1, :].partition_broadcast(128), [], [fv[k].r])
        esink = T(A, [128, 8], F32, "esink")
        S.dma("sp", esink.t[:, :], Wd["swa_sink"][l:l + 1, :].partition_broadcast(128), [], [esink.r])
        S.act(esink.t[:, :], esink.t[:, :], AF.Exp, [esink.r], [esink.r])

        for b in range(nb):
            mB_ = A.mark()
            cat = T(A, [128, NT, D], BF16, "cat")
            catr = [Res() for _ in range(NT)]
            A1 = [T(A, [128, D], F32, "A1") for _ in range(2)]
            B1 = [T(A, [128, D], F32, "B1") for _ in range(2)]
            for si, j in ((0, 2), (1, b)):
                load_bcast(A1[si], modv[l, 0, j:j + 1, :])
                load_bcast(B1[si], modv[l, 1, j:j + 1, :])
            groups = [[0, 1], [2, 3, 4, 5], [6, 7, 8, 9], [10, 11, 12, 13], [14, 15, 16, 17]]

            def load_x(tt, xring):
                xt = xring.next()
                rd = [] if r_xsrc is None else [r_xsrc[b][tt]]
                S.dma("sp", xt.t[:, :], xsrc[b, tt * 128:(tt + 1) * 128, :], rd, [xt.r])
                return xt

            mG = A.mark()
            qst = T(A, [128, 2, NT * 128], BF16, "qst")
            kst = T(A, [128, 2, NT * 128], BF16, "kst")
            kpst = T(A, [128, NT, 2, 128], BF16, "kpst")
            vst = T(A, [128, NT, 256], BF16, "vst")
            sog = T(A, [128, NT, 256], BF16, "sog")
            dst_ = T(A, [128, 2, 2 * NT], F32, "dst")
            r_q = [Res() for _ in range(NT)]
            r_k = [Res() for _ in range(NT)]
            r_kp = [Res() for _ in range(NT)]
            r_v = [Res() for _ in range(NT)]
            r_sog = [Res() for _ in range(NT)]
            r_d = [Res() for _ in range(NT)]
            mP1 = A.mark()
            if pre["wa"] is not None:
                wa, wa_r = pre["wa"]
                pre["wa"] = None
            else:
                wa, wa_r = w_load("wa", l)
            hTs = [T(A, [128, 8, 512], BF16, "hT") for _ in range(2)]
            xring = Ring(A, 2, [128, D], F32, "xt")
            hring = {"tmp": Ring(A, 1, [128, D], F32, "ntmp"), "h": Ring(A, 5, [128, D], BF16, "h")}
            codes = T(A, [33, 512], F32, "codes")
            S.memset("pool", codes.t[32:33, :], 1.0, [codes.r])
            R2 = lambda shp, dt, nm: Ring(A, 2, shp, dt, nm)
            e_sbR, spR, e1R, e2R, erR = (R2([128, 256], F32, n_) for n_ in ("e_sb", "sp", "e1", "e2", "er"))
            zfR = R2([128, 512], F32, "zf")
            vnR = R2([128, 256], F32, "vn")
            vgR = R2([128, 256], BF16, "vg")
            goutR = R2([128, 256], F32, "gout")
            bnstR = R2([128, 6], F32, "bnst")
            bnagR = R2([128, 2], F32, "bnag")
            ktokR = R2([128, 128], F32, "ktok")
            sgR = R2([128, 256], F32, "sgate")
            cq = 32.0 ** -0.5

            def p1_elem(grp_):
                hs_ = []
                for tt_ in grp_:
                    xt_ = load_x(tt_, xring)
                    si_ = 0 if tt_ < 2 else 1
                    hs_.append(norm_elem(xt_, A1[si_], B1[si_], hring))
                return hs_

            def p1_tr(hs_, hT_, gi_):
                for i_, h_ in enumerate(hs_):
                    norm_tr(h_, hT_.t, hT_.r, i_ * 128, 0)
                n_ = len(hs_) * 128
                S.dma("sp", hts[b, gi_, :, :, 0:n_], hT_.t[:, :, 0:n_], [hT_.r], [r_hts[b][gi_]])

            def stage_a(hT, i, tt):
                cs = slice(i * 128, (i + 1) * 128)
                for (bank, o0, c0, m) in ((4, 0, 128, 384), (5, 0, 512, 256), (6, 0, 800, 512)):
                    for k in range(8):
                        S.mm(PS[bank][:, o0:o0 + m], hT.t[:, k, cs], wa.t[:, k, c0:c0 + m], k == 0, k == 7,
                             [hT.r, wa_r[k]], [PR[bank]])
                S.mm(PS[5][:, 256:512], codes.t[0:33, cs], W2.t[0:33, :], True, True, [codes.r, W2.r], [PR[5]])
                st_ = {}
                e_sb, sp_, zf, ktok = e_sbR.next(), spR.next(), zfR.next(), ktokR.next()
                S.act(e_sb.t[:, :], PS[5][:, 256:512], AF.Exp, [PR[5]], [e_sb.r], scale=-1.0)
                S.act(sp_.t[:, :], e_sb.t[:, :], AF.Ln, [e_sb.r], [sp_.r], bias=1.0)
                S.cp("dve", ktok.t[:, :], PS[4][:, 0:128], [PR[4]], [ktok.r])
                S.cp("dve", vst.t[:, tt, :], PS[4][:, 128:384], [PR[4]], [r_v[tt]])
                sg_ = sgR.next()
                S.act(sg_.t[:, :], PS[5][:, 0:256], AF.Exp, [PR[5]], [sg_.r], scale=-1.0)
                S.act(sg_.t[:, :], sg_.t[:, :], AF.Ln, [sg_.r], [sg_.r], bias=1.0)
                S.act(sg_.t[:, :], sg_.t[:, :], AF.Exp, [sg_.r], [sg_.r], scale=-1.0)
                S.tt("dve", sog.t[:, tt, :], PS[5][:, 0:256], sg_.t[:, :], ALU.mult, [PR[5], sg_.r], [r_sog[tt]])
                S.act(zf.t[:, :], PS[6][:, :], AF.Gelu, [PR[6]], [zf.r])
                bnst, bnag, vn, vg = bnstR.next(), bnagR.next(), vnR.next(), vgR.next()
                S.op("dve", (lambda a, b_: (lambda e: e.bn_stats(a, b_)))(bnst.t[:, :], zf.t[:, 256:512]), [zf.r], [bnst.r])
                S.op("dve", (lambda a, b_: (lambda e: e.bn_aggr(a, b_)))(bnag.t[:, :], bnst.t[:, :]), [bnst.r], [bnag.r])
                rs = rstd_from_ss(bnag.t[:, 1:2], 1.0, 1e-5, [bnag.r])
                S.ts("dve", vn.t[:, :], zf.t[:, 256:512], bnag.t[:, 0:1], rs.t[:, 0:1], ALU.subtract, ALU.mult,
                     [zf.r, bnag.r, rs.r], [vn.r])
                S.tt("pool", vn.t[:, :], vn.t[:, :], fv["gmlp_ln_g"].t[:, :], ALU.mult, [vn.r, fv["gmlp_ln_g"].r], [vn.r])
                S.tt("pool", vg.t[:, :], vn.t[:, :], fv["gmlp_ln_b"].t[:, :], ALU.add, [vn.r, fv["gmlp_ln_b"].r], [vg.r])
                return dict(sp=sp_, zf=zf, ktok=ktok, vg=vg, cs=cs, tt=tt)

            def stage_b(st_):
                sp_, zf, ktok, vg, cs, tt = st_["sp"], st_["zf"], st_["ktok"], st_["vg"], st_["cs"], st_["tt"]
                tk = slice(tt * 128, (tt + 1) * 128)
                S.mm(PS[7][:, 0:128], sp_.t[:, 0:128], cf["triF"].t[:, :], True, True, [sp_.r, cf["triF"].r], [PR[7]])
                S.mm(PS[7][:, 128:256], sp_.t[:, 128:256], cf["triB"].t[:, :], True, True, [sp_.r, cf["triB"].r], [PR[7]])
                S.mm(PS[7][:, 256:384], cf["uF"].t[:, :], sp_.t[:, 0:128], True, True, [sp_.r, cf["uF"].r], [PR[7]])
                S.mm(PS[7][:, 384:512], cf["uB"].t[:, :], sp_.t[:, 128:256], True, True, [sp_.r, cf["uB"].r], [PR[7]])
                for g in range(4):
                    S.mm(PS[3][:, g * 64:(g + 1) * 64], wsT.t[:, g, :], vg.t[:, g * 64:(g + 1) * 64], True, True,
                         [wsT.r, vg.r], [PR[3]])
                e1, e2, er = e1R.next(), e2R.next(), erR.next()
                S.act(e1.t[:, :], PS[7][:, 0:256], AF.Exp, [PR[7]], [e1.r], scale=-1.0)
                S.act(e2.t[:, :], PS[7][:, 0:256], AF.Exp, [PR[7]], [e2.r])
                S.act(er.t[:, :], PS[7][:, 256:512], AF.Exp, [PR[7]], [er.r], scale=-1.0)
                e1v = e1.t[:, :].rearrange("p (d t) -> p d t", d=2)
                e2v = e2.t[:, :].rearrange("p (d t) -> p d t", d=2)
                erv = er.t[:, :].rearrange("p (d t) -> p d t", d=2)
                S.stt("dve", qst.t[:, :, tk], e1v, cq, PS[1][:, cs].unsqueeze(1).to_broadcast([128, 2, 128]),
                      ALU.mult, ALU.mult, [e1.r, PR[1]], [r_q[tt]])
                S.tt("dve", kst.t[:, :, tk], e2v, PS[2][:, cs].unsqueeze(1).to_broadcast([128, 2, 128]), ALU.mult,
                     [e2.r, PR[2]], [r_k[tt]])
                S.tt("dve", kpst.t[:, tt, :, :], erv, ktok.t[:, :].unsqueeze(1).to_broadcast([128, 2, 128]), ALU.mult,
                     [er.r, ktok.r], [r_kp[tt]])
                S.cp("dve", dst_.t[:, 0, 2 * tt:2 * tt + 2], e1.t[:, 63:128:64], [e1.r], [r_d[tt]])
                S.cp("dve", dst_.t[:, 1, 2 * tt:2 * tt + 2], e1.t[:, 128:256:64], [e1.r], [r_d[tt]])
                gout = goutR.next()
                S.tt("dve", gout.t[:, :].rearrange("p (g c) -> p g c", g=4), PS[3][:, 0:256].rearrange("p (g c) -> p g c", g=4),
                     bsT.t[:, :].unsqueeze(2).to_broadcast([128, 4, 64]), ALU.add, [PR[3], bsT.r], [gout.r])
                S.tt("dve", gout.t[:, :], gout.t[:, :], zf.t[:, 0:256], ALU.mult, [gout.r, zf.r], [gout.r])
                ss = sumsq(gout.t[:, :], [gout.r], 256)
                rs2 = rstd_from_ss(ss.t[:, 0:1], 256.0, eps6, [ss.r])
                S.stt("dve", cat.t[:, tt, 256:512], gout.t[:, :], rs2.t[:, 0:1], fv["gmlp_out_g"].t[:, :], ALU.mult, ALU.mult,
                      [gout.r, rs2.r, fv["gmlp_out_g"].r], [catr[tt]])

            p1_tr(p1_elem(groups[0]), hTs[0], 0)
            for gi, grp in enumerate(groups):
                n = len(grp) * 128
                hT = hTs[gi % 2]
                hs_next = p1_elem(groups[gi + 1]) if gi + 1 < len(groups) else None
                for (bank, c0, m) in ((3, 768, 32), (1, 0, 128), (2, 128, 128)):
                    for k in range(8):
                        S.mm(PS[bank][0:m, 0:n], wa.t[:, k, c0:c0 + m], hT.t[:, k, 0:n], k == 0, k == 7,
                             [wa_r[k], hT.r], [PR[bank]])
                    if bank == 3:
                        S.cp("act", codes.t[0:32, 0:n], PS[3][0:32, 0:n], [PR[3]], [codes.r])
                pend = None
                for i, tt in enumerate(grp):
                    st_ = stage_a(hT, i, tt)
                    if i == 0 and hs_next is not None:
                        p1_tr(hs_next, hTs[(gi + 1) % 2], gi + 1)
                    if pend is not None:
                        stage_b(pend)
                    pend = st_
                stage_b(pend)
            S.barrier()
            A.release(mP1)

            wb_, wb_r = w_load("wb", l)
            ofs = T(A, [128, NT, 256], F32, "ofs")
            ofs_r = [Res() for _ in range(NT)]
            osqR = Ring(A, 4, [128, 256], F32, "osq")
            osnR = Ring(A, 4, [128, 256], F32, "osn")
            visits = [0] * NT
            done_now = []
            post_q = []
            arrived = [False] * NT
            dirs = []
            for dr in range(2):
                dd = dict(dr=dr, Sst=Ring(A, 2, [128, 256], F32, "Sst"), Sbd=Ring(A, 3, [128, 256], BF16, "Sbd"),
                          Qbd=Ring(A, 2, [128, 4, 128], BF16, "Qbd"), att=Ring(A, 2, [128, 4, 128], BF16, "att"),
                          order=list(range(NT)) if dr == 0 else [1, 0] + list(range(NT - 1, 1, -1)),
                          mk=cf["mF"] if dr == 0 else cf["mB"], kvb=(0, 1) if dr == 0 else (4, 5), ab=2 + dr, ob=6 + dr)
                dd["Scur"] = dd["Sst"].next()
                S.memset("dve", dd["Scur"].t[:, :], 0.0, [dd["Scur"].r])
                dd["Sb0"] = dd["Sbd"].next()
                S.memset("dve", dd["Sb0"].t[:, :], 0.0, [dd["Sb0"].r])
                dirs.append(dd)

            def gla_front(dd, tt):
                dr = dd["dr"]
                tk0 = tt * 128
                chunks = (0, 1) if dr == 0 else (1, 0)
                need_out = need_ctx or tt >= 2
                abank = dd["ab"]
                Sbs = [dd["Sb0"]]
                if need_out:
                    qb = dd["Qbd"].next()
                    for h in range(4):
                        S.act(qb.t[:, h, :], qst.t[:, dr, tk0:tk0 + 128], AF.Copy, [r_q[tt], bmf.r], [qb.r], scale=bmf.t[:, h:h + 1])
                for ci, c in enumerate(chunks):
                    rows = slice(c * 64, (c + 1) * 64)
                    kvbank = dd["kvb"][ci]
                    S.mm(PS[kvbank][:, 0:256], kpst.t[rows, tt, dr, :], vst.t[rows, tt, :], True, True,
                         [r_kp[tt], r_v[tt]], [PR[kvbank]])
                if need_out:
                    S.mm(PS[abank][:, :], kst.t[:, dr, tk0:tk0 + 128], qb.t[:, :, :].rearrange("p h t -> p (h t)"), True, True,
                         [r_k[tt], qb.r], [PR[abank]])
                for ci, c in enumerate(chunks):
                    kvbank = dd["kvb"][ci]
                    Sn = dd["Sst"].next()
                    ch = 2 * tt + c
                    S.stt("dve", Sn.t[:, :], dd["Scur"].t[:, :], dst_.t[:, dr, ch:ch + 1], PS[kvbank][:, 0:256], ALU.mult, ALU.add,
                          [dd["Scur"].r, r_d[tt], PR[kvbank]], [Sn.r])
                    dd["Scur"] = Sn
                    Sbn = dd["Sbd"].next()
                    S.tt("pool", Sbn.t[:, :], Sn.t[:, :], smask.t[:, :], ALU.mult, [Sn.r, smask.r], [Sbn.r])
                    Sbs.append(Sbn)
                dd["Sb0"] = Sbs[2]
                if not need_out:
                    return None
                at = dd["att"].next()
                S.tt("dve", at.t[:, :, :], PS[abank][:, :].rearrange("p (h t) -> p h t", h=4),
                     dd["mk"].t[:, :].unsqueeze(1).to_broadcast([128, 4, 128]), ALU.mult, [PR[abank], dd["mk"].r], [at.r])
                return (tt, Sbs, at)

            def gla_back(dd, fr):
                if fr is None:
                    return
                tt, Sbs, at = fr
                dr = dd["dr"]
                tk0 = tt * 128
                chunks = (0, 1) if dr == 0 else (1, 0)
                obank = dd["ob"]
                for cj, c2 in enumerate(chunks):
                    r2 = slice(c2 * 64, (c2 + 1) * 64)
                    kw = {"tile_position": (0, 64)} if c2 == 1 else {}
                    S.mm(PS[obank][r2, 0:256], qst.t[:, dr, tk0 + c2 * 64:tk0 + (c2 + 1) * 64], Sbs[cj].t[:, :],
                         True, False, [r_q[tt], Sbs[cj].r], [PR[obank]], skip_group_check=True, **kw)
                for h in range(4):
                    S.mm(PS[obank][:, h * 64:(h + 1) * 64], at.t[:, h, :], vst.t[:, tt, h * 64:(h + 1) * 64], False, h == 3,
                         [at.r, r_v[tt]], [PR[obank]], skip_group_check=True)
                if not arrived[tt]:
                    arrived[tt] = True
                    S.cp("act", ofs.t[:, tt, :], PS[obank][:, 0:256], [PR[obank]], [ofs_r[tt]])
                else:
                    S.tt("dve", ofs.t[:, tt, :], PS[obank][:, 0:256], ofs.t[:, tt, :], ALU.add, [PR[obank], ofs_r[tt]], [ofs_r[tt]])
                visits[tt] += 1
                if visits[tt] == 2:
                    done_now.append(tt)

            def post_stage(it):
                tt, stg, cx = it
                if stg == 0:
                    cx["osq"] = osqR.next()
                    S.tt("pool", cx["osq"].t[:, :], ofs.t[:, tt, :], ofs.t[:, tt, :], ALU.mult, [ofs_r[tt]], [cx["osq"].r])
                elif stg == 1:
                    s4 = stat.next()
                    S.reduce(s4.t[:, 0:4], cx["osq"].t[:, :].rearrange("p (h d) -> p h d", h=4), ALU.add, [cx["osq"].r], [s4.r])
                    cx["r4"] = rstd_from_ss(s4.t[:, 0:4], 64.0, eps6, [s4.r])
                elif stg == 2:
                    osn, r4 = osnR.next(), cx["r4"]
                    cx["osn"] = osn
                    S.tt("dve", osn.t[:, :].rearrange("p (h d) -> p h d", h=4), ofs.t[:, tt, :].rearrange("p (h d) -> p h d", h=4),
                         r4.t[:, 0:4].unsqueeze(2).to_broadcast([128, 4, 64]), ALU.mult, [ofs_r[tt], r4.r], [osn.r])
                    S.tt("pool", osn.t[:, :], osn.t[:, :], fv["gnorm4"].t[:, :], ALU.mult, [osn.r, fv["gnorm4"].r], [osn.r])
                else:
                    S.tt("dve", cat.t[:, tt, 0:256], cx["osn"].t[:, :], sog.t[:, tt, :], ALU.mult, [cx["osn"].r, r_sog[tt]], [catr[tt]])
                it[1] += 1

            def post_advance():
                for it in post_q:
                    post_stage(it)
                post_q[:] = [it for it in post_q if it[1] < 4]
                post_q.extend([tt_, 0, {}] for tt_ in done_now)
                done_now[:] = []

            pend_g = [None, None]
            for stp in range(NT):
                for di, dd in enumerate(dirs):
                    gla_back(dd, pend_g[di])
                    pend_g[di] = gla_front(dd, dd["order"][stp])
                post_advance()
            for di, dd in enumerate(dirs):
                gla_back(dd, pend_g[di])
            n_post = 0
            while post_q or done_now:
                post_advance()
            assert all(visits[tt_] == 2 for tt_ in tiles_all)
            S.barrier()
            A.release(mG)

            mS = A.mark()
            cosT = T(A, [128, 2048], F32, "cosT")
            sinT = T(A, [128, 2048], F32, "sinT")
            S.dma("sp", cosT.t[:, :], Cd["cosT"][:, :], [], [cosT.r])
            S.dma("sp", sinT.t[:, :], Cd["sinT"][:, :], [], [sinT.r])
            qs = T(A, [128, 4, 2048], BF16, "qs")
            qc = T(A, [128, 4, 256], BF16, "qc")
            ks = T(A, [128, NT * 128], BF16, "ks")
            va = T(A, [128, NT, 2, 65], BF16, "va")
            S.memset("pool", va.t[:, :, :, :].rearrange("p a b c -> p (a b c)"), 1.0, [va.r])
            grp_r = [Res() for _ in range(5)]
            mP2 = A.mark()
            hTs = [T(A, [128, 8, 512], BF16, "hT2") for _ in range(2)]
            rt = Ring(A, 2, [128, 512], F32, "ropetmp")

            def p2_load(gi_):
                n_ = len(groups[gi_]) * 128
                S.dma("sp", hTs[gi_ % 2].t[:, :, 0:n_], hts[b, gi_, :, :, 0:n_], [r_hts[b][gi_]], [hTs[gi_ % 2].r])

            p2_load(0)
            for gi, grp in enumerate(groups):
                n = len(grp) * 128
                is_ctx = gi == 0
                hT = hTs[gi % 2]
                if gi + 1 < len(groups):
                    p2_load(gi + 1)
                p0 = (grp[0] - 2) * 128
                units = []
                if not (is_ctx and not need_ctx):
                    units += [("q", j) for j in range(4)]
                units.append(("k", 0))
                for ui, (kind, j) in enumerate(units):
                    c0 = j * 128 if kind == "q" else 512
                    cp0 = 768 + j * 128 if kind == "q" else 1280
                    b1, b2 = 1 + 2 * (ui % 2), 2 + 2 * (ui % 2)
                    for k in range(8):
                        S.mm(PS[b1][:, 0:n], wb_.t[:, k, c0:c0 + 128], hT.t[:, k, 0:n], k == 0, k == 7, [wb_r[k], hT.r], [PR[b1]])
                    if is_ctx:
                        dst = qc.t[:, j, :] if kind == "q" else ks.t[:, 0:256]
                        S.cp("act", dst, PS[b1][:, 0:n], [PR[b1]], [grp_r[gi]])
                        continue
                    for k in range(8):
                        S.mm(PS[b2][:, 0:n], wb_.t[:, k, cp0:cp0 + 128], hT.t[:, k, 0:n], k == 0, k == 7, [wb_r[k], hT.r], [PR[b2]])
                    t1, t2 = rt.next(), rt.next()
                    S.tt("dve", t1.t[:, :], PS[b1][:, :], cosT.t[:, p0:p0 + 512], ALU.mult, [PR[b1], cosT.r], [t1.r])
                    S.tt("dve", t2.t[:, :], PS[b2][:, :], sinT.t[:, p0:p0 + 512], ALU.mult, [PR[b2], sinT.r], [t2.r])
                    dst = qs.t[:, j, p0:p0 + 512] if kind == "q" else ks.t[:, 256 + p0:256 + p0 + 512]
                    S.tt("pool", dst, t1.t[:, :], t2.t[:, :], ALU.add, [t1.r, t2.r], [grp_r[gi]])
                for i, tt in enumerate(grp):
                    cs = slice(i * 128, (i + 1) * 128)
                    vb = 5 + (i % 2)
                    for k in range(8):
                        S.mm(PS[vb][:, 0:128], hT.t[:, k, cs], wb_.t[:, k, 640:768], k == 0, k == 7, [hT.r, wb_r[k]], [PR[vb]])
                    S.cp("act", va.t[:, tt, :, 0:64], PS[vb][:, 0:128].rearrange("p (g d) -> p g d", g=2), [PR[vb], va.r], [grp_r[gi]])
            S.barrier()
            A.release(mP2)

            wo, wo_r = w_load("wo", l)
            pT = Ring(A, 4, [128, 4, 128], BF16, "pT")
            csos = Ring(A, 2, [128, 512], F32, "cso")
            den = Ring(A, 2, [128, 4], F32, "den")
            all_r = grp_r + [va.r]
            qblocks = ([("c", 0), ("c", 1)] if need_ctx else []) + [("l", n_) for n_ in range(16)]
            units = []
            for (qk, n_) in qblocks:
                tt = n_ if qk == "c" else 2 + n_
                cso = csos.next()
                for g in range(2):
                    pr = slice(64 * g, 64 * g + 64)
                    if qk == "c":
                        qap = qc.t[pr, :, n_ * 128:(n_ + 1) * 128]
                        keys = [(0, None), (1, None)]
                    else:
                        qap = qs.t[pr, :, n_ * 128:(n_ + 1) * 128]
                        keys = [(0, None), (1, None)]
                        if n_ > 0:
                            keys.append((2 + n_ - 1, "mP"))
                        keys.append((2 + n_, None))
                        if n_ < 15:
                            keys.append((2 + n_ + 1, "mN"))
                    units.append((tt, g, pr, qap, keys, cso))
            steps = [(ui, ki) for ui, u in enumerate(units) for ki in range(len(u[4]))]
            LOOK = 2

            def swa_score(si):
                ui, ki = steps[si]
                tt, g, pr, qap, keys, cso = units[ui]
                kt = keys[ki][0]
                sb = 1 + si % 4
                S.mm(PS[sb][:, :].rearrange("p (r q) -> p r q", r=4), ks.t[pr, kt * 128:(kt + 1) * 128], qap, True, True, all_r, [PR[sb]])

            def swa_rest(si):
                ui, ki = steps[si]
                tt, g, pr, qap, keys, cso = units[ui]
                kt, mname = keys[ki]
                sb = 1 + si % 4
                ob = 6 + (ui % 2)
                p = pT.next()
                S.act(p.t[:, :, :], PS[sb][:, :].rearrange("p (r q) -> p r q", r=4), AF.Exp, [PR[sb]], [p.r], scale=0.125)
                if mname is not None:
                    S.tt("dve", p.t[:, :, :], p.t[:, :, :], cf[mname].t[:, :].unsqueeze(1).to_broadcast([128, 4, 128]), ALU.mult,
                         [p.r, cf[mname].r], [p.r])
                for r_ in range(4):
                    S.mm(PS[ob][:, r_ * 65:(r_ + 1) * 65], p.t[:, r_, :], va.t[:, kt, g, :], ki == 0 and r_ == 0,
                         ki == len(keys) - 1 and r_ == 3, [p.r] + all_r, [PR[ob]], skip_group_check=True)
                if ki < len(keys) - 1:
                    return
                ov = PS[ob][:, 0:260].rearrange("p (r d) -> p r d", r=4)
                dn = den.next()
                S.tt("dve", dn.t[:, :], ov[:, :, 64], esink.t[:, 4 * g:4 * g + 4], ALU.add, [PR[ob], esink.r], [dn.r])
                S.recip(dn.t[:, :], dn.t[:, :], [dn.r], [dn.r])
                S.tt("dve", cso.t[:, 256 * g:256 * g + 256].rearrange("p (r d) -> p r d", r=4), ov[:, :, 0:64],
                     dn.t[:, :].unsqueeze(2).to_broadcast([128, 4, 64]), ALU.mult, [PR[ob], dn.r], [cso.r])
                if g == 1:
                    ss = sumsq(cso.t[:, :], [cso.r], 512)
                    rs = rstd_from_ss(ss.t[:, 0:1], 512.0, eps6, [ss.r])
                    S.stt("dve", cat.t[:, tt, 512:1024], cso.t[:, :], rs.t[:, 0:1], fv["swa_out_g"].t[:, :], ALU.mult, ALU.mult,
                          [cso.r, rs.r, fv["swa_out_g"].r], [catr[tt]])

            for si in range(len(steps) + LOOK):
                if si < len(steps):
                    swa_score(si)
                if si >= LOOK:
                    swa_rest(si - LOOK)
            S.barrier()
            A.release(mS)

            mO = A.mark()
            A.top = mB_
            sgs = [tiles_all[i:i + 6] for i in range(0, len(tiles_all), 6)]
            TM = max(len(s_) for s_ in sgs) * 128
            actT = T(A, [128, 22, TM], BF16, "actT")
            wd = T(A, [128, 22, D], BF16, "w_down")
            wd_r = [Res() for _ in range(22)]
            xring_n = Ring(A, 2, [128, D], F32, "xt4n")
            xring_r = Ring(A, 2, [128, D], F32, "xt4r")
            oring4 = Ring(A, 2, [128, D], F32, "xo4")
            sil = Ring(A, 2, [128, 512], F32, "sil")
            A.top = max(A.top, mO + 40 * 1024)
            early_base = A.top
            A2 = [T(A, [128, D], F32, "A2") for _ in range(2)]
            B2 = [T(A, [128, D], F32, "B2") for _ in range(2)]
            C2 = [T(A, [128, D], F32, "C2") for _ in range(2)]
            for si, j in ((0, 2), (1, b)):
                load_bcast(A2[si], modv[l, 3, j:j + 1, :])
                load_bcast(B2[si], modv[l, 4, j:j + 1, :])
                load_bcast(C2[si], modv[l, 5, j:j + 1, :])
            hT2 = T(A, [128, 8, TM], BF16, "hTf")
            hring4 = {"tmp": Ring(A, 2, [128, D], F32, "ntmp4"), "h": Ring(A, 3, [128, D], BF16, "h4")}
            ffn_top = A.top
            A.top = mO
            C1 = [T(A, [128, D], F32, "C1") for _ in range(2)]
            for si, j in ((0, 2), (1, b)):
                load_bcast(C1[si], modv[l, 2, j:j + 1, :])
            catT = Ring(A, 2, [128, 8, 128], BF16, "catT")
            xring = Ring(A, 2, [128, D], F32, "xt3")
            oring = Ring(A, 2, [128, D], F32, "xo3")
            pend_tr = None

            def o_tr(ti_):
                tt_ = tiles_all[ti_]
                if debug:
                    S.dma("sp", dbg_cat[b, tt_ * 128:(tt_ + 1) * 128, :], cat.t[:, tt_, :], [catr[tt_]], [])
                trb = ti_ % 2
                pbf = PS[trb][:, :].bitcast(BF16)
                for k in range(8):
                    S.tr(pbf[:, k * 128:(k + 1) * 128], cat.t[:, tt_, k * 128:(k + 1) * 128], ident.t[:, :], [catr[tt_], ident.r], [PR[trb]])
                ct_ = catT.next()
                S.cp("act", ct_.t[:, :, :], pbf[:, :].rearrange("p (k t) -> p k t", k=8), [PR[trb]], [ct_.r])
                return ct_

            ct_next = o_tr(0)
            for ti, tt in enumerate(tiles_all):
                ct = ct_next
                ct_next = o_tr(ti + 1) if ti + 1 < len(tiles_all) else None
                yb = (2 + 2 * (ti % 2), 3 + 2 * (ti % 2))
                for hf in range(2):
                    for k in range(8):
                        S.mm(PS[yb[hf]][:, :], ct.t[:, k, :], wo.t[:, k, hf * 512:(hf + 1) * 512], k == 0, k == 7, [ct.r, wo_r[k]], [PR[yb[hf]]])
                xt = load_x(tt, xring)
                si = 0 if tt < 2 else 1
                o1 = post_residual(yb, xt, C1[si], xs1[b, tt * 128:(tt + 1) * 128, :], r_xs1[b][tt], oring)
                if pend_tr is not None:
                    norm_tr(pend_tr[0], hT2.t, hT2.r, pend_tr[1] * 128, 6 + (pend_tr[1] % 2))
                    pend_tr = None
                if ti < len(sgs[0]):
                    pend_tr = (norm_elem(o1, A2[si], B2[si], hring4, add_eng="pool"), ti)
            if pend_tr is not None:
                norm_tr(pend_tr[0], hT2.t, hT2.r, pend_tr[1] * 128, 6 + (pend_tr[1] % 2))
            assert A.top <= early_base
            S.barrier()
            A.top = ffn_top

            wgb, wub = w_rings()
            oring = oring4
            hring = hring4
            wgu = Wd["ffn_w_gu"][l].rearrange("(k p) n -> p k n", p=128)
            wdn = Wd["ffn_w_down"][l].rearrange("(f p) n -> p f n", p=128)

            def load_x1(tt, ring):
                xt = ring.next()
                S.dma("sp", xt.t[:, :], xs1[b, tt * 128:(tt + 1) * 128, :], [r_xs1[b][tt]], [xt.r])
                return xt

            def ffn_elem(tt):
                xt = load_x1(tt, xring_n)
                si_ = 0 if tt < 2 else 1
                return norm_elem(xt, A2[si_], B2[si_], hring, add_eng="dve")

            for sgi, sg in enumerate(sgs):
                ntok = len(sg) * 128
                nxt = sgs[sgi + 1] if sgi + 1 < len(sgs) else []
                chunks = [(c0, min(512, ntok - c0)) for c0 in range(0, ntok, 512)]
                ui = 0
                assert len(nxt) <= len(sg)
                blocks = {}

                def issue_gu(cb_):
                    wg_, wu_ = wgb.next(), wub.next()
                    S.dma("pool", wg_.t[:, :, :], wgu[:, :, cb_ * 256:(cb_ + 1) * 256], [], [wg_.r])
                    S.dma("pool", wu_.t[:, :, :], wgu[:, :, DFF + cb_ * 256:DFF + (cb_ + 1) * 256], [], [wu_.r])
                    blocks[cb_] = (wg_, wu_)

                issue_gu(0)
                for cb in range(11):
                    if cb + 1 < 11:
                        issue_gu(cb + 1)
                    wg, wu = blocks[cb]
                    for f in (2 * cb, 2 * cb + 1):
                        S.dma("pool", wd.t[:, f, :], wdn[:, f, :], [], [wd_r[f]])
                    for fs in range(2):
                        fb = cb * 2 + fs
                        for (c0, cn) in chunks:
                            bg, bu = 2 + 2 * (ui % 2), 3 + 2 * (ui % 2)
                            ui += 1
                            for k in range(8):
                                S.mm(PS[bg][:, 0:cn], wg.t[:, k, fs * 128:(fs + 1) * 128], hT2.t[:, k, c0:c0 + cn], k == 0, k == 7,
                                     [wg.r, hT2.r], [PR[bg]])
                            for k in range(8):
                                S.mm(PS[bu][:, 0:cn], wu.t[:, k, fs * 128:(fs + 1) * 128], hT2.t[:, k, c0:c0 + cn], k == 0, k == 7,
                                     [wu.r, hT2.r], [PR[bu]])
                            sl = sil.next()
                            S.act(sl.t[:, 0:cn], PS[bg][:, 0:cn], AF.Exp, [PR[bg]], [sl.r], scale=-1.0)
                            S.act(sl.t[:, 0:cn], sl.t[:, 0:cn], AF.Ln, [sl.r], [sl.r], bias=1.0)
                            S.act(sl.t[:, 0:cn], sl.t[:, 0:cn], AF.Exp, [sl.r], [sl.r], scale=-1.0)
                            S.tt("dve", sl.t[:, 0:cn], sl.t[:, 0:cn], PS[bg][:, 0:cn], ALU.mult, [sl.r, PR[bg]], [sl.r])
                            S.tt("dve", actT.t[:, fb, c0:c0 + cn], sl.t[:, 0:cn], PS[bu][:, 0:cn], ALU.mult, [sl.r, PR[bu]], [actT.r])
                if sgi == len(sgs) - 1:
                    lnext = l if b + 1 < nb else (l + 1 if l + 1 < nlayers else None)
                    if lnext is not None:
                        pre["wa"] = w_load("wa", lnext)
                hq = [ffn_elem(nxt[0])] if len(nxt) > 0 else []
                for i, tt in enumerate(sg):
                    if i + 1 < len(nxt):
                        hq.append(ffn_elem(nxt[i + 1]))
                    hn = hq[i] if i < len(nxt) else None
                    yb = (2 + 2 * (i % 2), 3 + 2 * (i % 2))
                    for hf in range(2):
                        for f in range(22):
                            S.mm(PS[yb[hf]][:, :], actT.t[:, f, i * 128:(i + 1) * 128], wd.t[:, f, hf * 512:(hf + 1) * 512], f == 0, f == 21,
                                 [actT.r, wd_r[f]], [PR[yb[hf]]])
                    if hn is not None:
                        norm_tr(hn, hT2.t, hT2.r, i * 128, i % 2)
                    xt = load_x1(tt, xring_r)
                    si = 0 if tt < 2 else 1
                    if last and tt >= 2:
                        dst_ap, dst_res = out[b, (tt - 2) * 128:(tt - 1) * 128, :], Res()
                    else:
                        dst_ap, dst_res = xs2[b, tt * 128:(tt + 1) * 128, :], r_xs2[b][tt]
                    post_residual(yb, xt, C2[si], dst_ap, dst_res, oring)
            S.barrier()
            A.release(mB_)
        S.barrier()
        A.release(mL)
    print("ops:", {e: len(v) for e, v in S.ops.items()}, "peak sbuf", A.peak)
    S.emit()
    return nc


_CACHE = {}


def _core_inputs(inp, core, nb, W, C):
    b0 = core * nb
    x = np.asarray(inp["x"], np.float32)
    ctx = np.asarray(inp["ctx"], np.float32)
    c = np.asarray(inp["c"], np.float32)
    c_ctx = np.asarray(inp["c_ctx"], np.float32)
    m = {}
    m["xin"] = np.ascontiguousarray(np.concatenate([ctx[b0:b0 + nb], x[b0:b0 + nb]], axis=1))
    cc = np.stack([c[b0 + (j % nb)] for j in range(2)] + [c_ctx], 0)
    m["ccT"] = np.ascontiguousarray(cc.reshape(3, 8, 128).transpose(2, 1, 0))
    m.update(W)
    for k, v in C.items():
        m["c_" + k] = v
    return m


def kernel(**inp):
    nb = 2
    if "nc" not in _CACHE:
        _CACHE["nc"] = build(nb=nb, nlayers=2)
    nc = _CACHE["nc"]
    W = _prep_weights(inp)
    C = _consts()
    in_maps = [_core_inputs(inp, core, nb, W, C) for core in range(NCORES)]
    res = run_bass_kernel_spmd(nc, in_maps, core_ids=list(range(NCORES)))
    outs = [np.asarray(r["out"], np.float32) for r in res.results]
    return np.concatenate(outs, axis=0)
```
